# Optimizing a Trainium2 kernel written in Bass

```python
import jax
import jax.numpy as jnp
from jax import lax
import numpy as np

D_MODEL = 2048
BATCH = 4
SEQ = 2048
DEPTH = 1
DEC_BATCH = 128
DEC_SEQ = 8
PAST_LEN = 16384
PAGE_SIZE = 128

MEM_LEN = 256
CHUNK = 128
A_HEADS = 4
A_HEAD_DIM = 128
A_WIDTH = A_HEADS * A_HEAD_DIM
B_HEADS = 16
B_HEAD_DIM = 64
B_WIDTH = B_HEADS * B_HEAD_DIM
DECAY_LORA = 64
ICLR_LORA = 64
GATE_LORA = 64
B_PROJ = 3 * B_WIDTH + DECAY_LORA + ICLR_LORA + GATE_LORA
C_HEADS = 4
C_HEAD_DIM = 128
C_WIDTH = C_HEADS * C_HEAD_DIM
MIX_WIDTH = A_WIDTH + B_WIDTH + C_WIDTH
IN_COLS = 2 * A_WIDTH + B_PROJ + C_WIDTH
D_FF = 5632
CONV_WIDTH = 3
RMS_EPS = 1e-6
LN_EPS = 1e-5
GN_EPS = 64e-5
DECAY_OFFSET = 0.5

kernel_name = 'hymba_gmlp_rwkv7_memxattn_convffn_step'


def rms_norm(x, g):
    xf = x.astype(jnp.float32)
    y = xf * lax.rsqrt(jnp.mean(xf * xf, axis=-1, keepdims=True) + RMS_EPS)
    return (y * g.astype(jnp.float32)).astype(x.dtype)


def layer_norm(x, g, b):
    xf = x.astype(jnp.float32)
    mean = jnp.mean(xf, axis=-1, keepdims=True)
    var = jnp.mean(jnp.square(xf - mean), axis=-1, keepdims=True)
    y = (xf - mean) * lax.rsqrt(var + LN_EPS)
    return (y * g.astype(jnp.float32) + b.astype(jnp.float32)).astype(x.dtype)


def chunk_spatial_gate(u, v, w_s, b_s):
    Bn, L, H, Dh = v.shape
    n_chunks = -(-L // CHUNK)
    pad = n_chunks * CHUNK - L
    vp = jnp.pad(v, ((0, 0), (0, pad), (0, 0), (0, 0))).reshape(Bn, n_chunks, CHUNK, H, Dh)
    causal = jnp.tril(jnp.ones((CHUNK, CHUNK), dtype=bool))
    w = jnp.where(causal[None], w_s, 0.0).astype(v.dtype)
    mixed = jnp.einsum('hts,bnshd->bnthd', w, vp) + b_s.T.astype(v.dtype)[None, None, :, :, None]
    mixed = mixed.reshape(Bn, n_chunks * CHUNK, H, Dh)[:, :L]
    return u * mixed


def wkv7_scan(r, w, k, v, a_vec, b_vec, state):
    def step(S, inp):
        r_t, w_t, k_t, v_t, a_t, b_t = inp
        sa = jnp.einsum('bhij,bhj->bhi', S, a_t)
        S = S * w_t[:, :, None, :] + sa[..., None] * b_t[:, :, None, :] + v_t[..., None] * k_t[:, :, None, :]
        y = jnp.einsum('bhij,bhj->bhi', S, r_t)
        return S, y
    xs = tuple(jnp.moveaxis(t, 1, 0) for t in (r, w, k, v, a_vec, b_vec))
    S, ys = lax.scan(step, state, xs)
    return jnp.moveaxis(ys, 0, 1), S


def rwkv7_time_mix(p, shift_prev, wkv_prev, mu, w0, w2, a0, a2, g2, k_k, k_a, r_k, lnx_g, lnx_b):
    Bn, L, _ = p.shape
    f32 = lambda t: t.astype(jnp.float32)
    pf = f32(p)
    prev = jnp.concatenate([f32(shift_prev)[:, None], pf[:, :-1]], axis=1)
    ps = pf + (prev - pf) * f32(mu)
    o1, o2, o3 = B_WIDTH, 2 * B_WIDTH, 3 * B_WIDTH
    o4, o5 = o3 + DECAY_LORA, o3 + DECAY_LORA + ICLR_LORA
    r, k, v = ps[..., :o1], ps[..., o1:o2], ps[..., o2:o3]
    dw, da, dg = ps[..., o3:o4], ps[..., o4:o5], ps[..., o5:]
    w_log = -jax.nn.softplus(-(f32(w0) + jnp.tanh(dw) @ f32(w2))) - DECAY_OFFSET
    decay = jnp.exp(-jnp.exp(w_log))
    a = jax.nn.sigmoid(f32(a0) + da @ f32(a2))
    g = jax.nn.sigmoid(dg) @ f32(g2)
    heads = lambda t: t.reshape(Bn, L, B_HEADS, B_HEAD_DIM)
    kk = heads(k * f32(k_k))
    kk = kk / jnp.maximum(jnp.sqrt(jnp.sum(kk * kk, axis=-1, keepdims=True)), 1e-12)
    k = k * (1.0 + (a - 1.0) * f32(k_a))
    rh, kh, vh, wh, ah = heads(r), heads(k), heads(v), heads(decay), heads(a)
    y, wkv_new = wkv7_scan(rh, wh, kh, vh, -kk, kk * ah, f32(wkv_prev))
    mean = jnp.mean(y, axis=-1, keepdims=True)
    var = jnp.mean(jnp.square(y - mean), axis=-1, keepdims=True)
    y = ((y - mean) * lax.rsqrt(var + GN_EPS)).reshape(Bn, L, B_WIDTH) * f32(lnx_g) + f32(lnx_b)
    bonus = jnp.sum(rh * kh * f32(r_k), axis=-1, keepdims=True) * vh
    y = (y + bonus.reshape(Bn, L, B_WIDTH)) * g
    return y.astype(p.dtype), p[:, -1], wkv_new


def memory_kv(mem, g_mem, w_mk, w_mv):
    Bn, M, _ = mem.shape
    m = rms_norm(mem, g_mem)
    k = (m @ w_mk).reshape(Bn, M, C_HEADS, C_HEAD_DIM)
    v = (m @ w_mv).reshape(Bn, M, C_HEADS, C_HEAD_DIM)
    return k, v


def memory_attend(q, mem_k, mem_v):
    s = jnp.einsum('blhd,bmhd->bhlm', q, mem_k.astype(q.dtype)).astype(jnp.float32) * (C_HEAD_DIM ** -0.5)
    p = jax.nn.softmax(s, axis=-1).astype(q.dtype)
    return jnp.einsum('bhlm,bmhd->blhd', p, mem_v.astype(q.dtype))


def mixing_sublayer(x, mem_k, mem_v, shift_prev, wkv_prev, lw):
    Bn, L, _ = x.shape
    h = rms_norm(x, lw['norm_mix_pre'])
    proj = h @ lw['w_in']
    a_u = proj[..., :A_WIDTH]
    a_v = proj[..., A_WIDTH:2 * A_WIDTH]
    b_p = proj[..., 2 * A_WIDTH:2 * A_WIDTH + B_PROJ]
    c_q = proj[..., 2 * A_WIDTH + B_PROJ:]
    a_u = jax.nn.gelu(a_u, approximate=True).reshape(Bn, L, A_HEADS, A_HEAD_DIM)
    a_v = layer_norm(jax.nn.gelu(a_v, approximate=True), lw['gm_ln_g'], lw['gm_ln_b'])
    a_v = a_v.reshape(Bn, L, A_HEADS, A_HEAD_DIM)
    a_out = chunk_spatial_gate(a_u, a_v, lw['gm_ws'], lw['gm_bs']).reshape(Bn, L, A_WIDTH)
    chunk_start = ((L - 1) // CHUNK) * CHUNK
    chunk_v = a_v[:, chunk_start:]
    b_out, shift_new, wkv_new = rwkv7_time_mix(
        b_p, shift_prev, wkv_prev, lw['rk_mu'], lw['rk_w0'], lw['rk_w2'], lw['rk_a0'], lw['rk_a2'],
        lw['rk_g2'], lw['rk_kk'], lw['rk_ka'], lw['rk_rk'], lw['rk_lnx_g'], lw['rk_lnx_b'])
    c_out = memory_attend(c_q.reshape(Bn, L, C_HEADS, C_HEAD_DIM), mem_k, mem_v).reshape(Bn, L, C_WIDTH)
    mixed = jnp.concatenate([a_out, b_out.astype(x.dtype), c_out], axis=-1)
    out = rms_norm(mixed @ lw['w_out'], lw['norm_mix_post'])
    return x + out, chunk_v, shift_new, wkv_new


def conv_ffn_sublayer(x, conv_prev, lw):
    L = x.shape[1]
    h = rms_norm(x, lw['norm_ffn_pre'])
    up = h @ lw['ffn_w_up']
    ext = jnp.concatenate([conv_prev.astype(up.dtype), up], axis=1)
    cw = lw['ffn_conv_w'].astype(up.dtype)
    conv = lw['ffn_conv_b'].astype(up.dtype)
    for i in range(CONV_WIDTH):
        conv = conv + ext[:, i:i + L] * cw[i]
    gate, val = conv[..., :D_FF], conv[..., D_FF:]
    act = jax.nn.gelu(gate, approximate=True) * val
    out = rms_norm(act @ lw['ffn_w_down'], lw['norm_ffn_post'])
    return x + out, ext[:, -(CONV_WIDTH - 1):]


def decoder_layer(x, mem_k, mem_v, shift_prev, wkv_prev, conv_prev, lw):
    x, chunk_v, shift_new, wkv_new = mixing_sublayer(x, mem_k, mem_v, shift_prev, wkv_prev, lw)
    x, conv_new = conv_ffn_sublayer(x, conv_prev, lw)
    return x, chunk_v, shift_new, wkv_new, conv_new


def setup_inputs(seed: int = 0) -> dict:
    key = jax.random.key(seed)
    keys = iter(jax.random.split(key, 48))

    def nrm(shape, scale):
        return jax.random.normal(next(keys), shape, jnp.float32) * scale

    def gain(shape):
        return 1.0 + nrm(shape, 0.02)

    return {
        'x_prompt': nrm((BATCH, SEQ, D_MODEL), 1.0),
        'x_sample': nrm((DEC_BATCH, DEC_SEQ, D_MODEL), 1.0),
        'mem_prompt': nrm((BATCH, MEM_LEN, D_MODEL), 1.0),
        'cache_mem_k': nrm((DEPTH, DEC_BATCH, MEM_LEN, C_HEADS, C_HEAD_DIM), 1.0),
        'cache_mem_v': nrm((DEPTH, DEC_BATCH, MEM_LEN, C_HEADS, C_HEAD_DIM), 1.0),
        'state_shift': nrm((DEPTH, DEC_BATCH, B_PROJ), 1.0),
        'state_wkv': nrm((DEPTH, DEC_BATCH, B_HEADS, B_HEAD_DIM, B_HEAD_DIM), 0.1),
        'state_conv': nrm((DEPTH, DEC_BATCH, CONV_WIDTH - 1, 2 * D_FF), 1.0),
        'norm_mix_pre': gain((DEPTH, D_MODEL)),
        'norm_mix_post': gain((DEPTH, D_MODEL)),
        'norm_ffn_pre': gain((DEPTH, D_MODEL)),
        'norm_ffn_post': gain((DEPTH, D_MODEL)),
        'norm_mem': gain((DEPTH, D_MODEL)),
        'w_in': nrm((DEPTH, D_MODEL, IN_COLS), D_MODEL ** -0.5),
        'w_out': nrm((DEPTH, MIX_WIDTH, D_MODEL), MIX_WIDTH ** -0.5),
        'w_mem_k': nrm((DEPTH, D_MODEL, C_WIDTH), D_MODEL ** -0.5),
        'w_mem_v': nrm((DEPTH, D_MODEL, C_WIDTH), D_MODEL ** -0.5),
        'gm_ln_g': gain((DEPTH, A_WIDTH)),
        'gm_ln_b': nrm((DEPTH, A_WIDTH), 0.02),
        'gm_ws': nrm((DEPTH, A_HEADS, CHUNK, CHUNK), CHUNK ** -0.5),
        'gm_bs': 1.0 + nrm((DEPTH, A_HEADS, CHUNK), 0.1),
        'rk_mu': jax.random.uniform(next(keys), (DEPTH, B_PROJ), jnp.float32),
        'rk_w0': -1.0 + nrm((DEPTH, B_WIDTH), 0.5),
        'rk_w2': nrm((DEPTH, DECAY_LORA, B_WIDTH), 0.1),
        'rk_a0': nrm((DEPTH, B_WIDTH), 0.1),
        'rk_a2': nrm((DEPTH, ICLR_LORA, B_WIDTH), 0.1),
        'rk_g2': nrm((DEPTH, GATE_LORA, B_WIDTH), GATE_LORA ** -0.5),
        'rk_kk': 0.85 + nrm((DEPTH, B_WIDTH), 0.02),
        'rk_ka': 1.0 + nrm((DEPTH, B_WIDTH), 0.02),
        'rk_rk': nrm((DEPTH, B_HEADS, B_HEAD_DIM), 0.1),
        'rk_lnx_g': gain((DEPTH, B_WIDTH)),
        'rk_lnx_b': nrm((DEPTH, B_WIDTH), 0.02),
        'ffn_w_up': nrm((DEPTH, D_MODEL, 2 * D_FF), D_MODEL ** -0.5),
        'ffn_conv_w': nrm((DEPTH, CONV_WIDTH, 2 * D_FF), CONV_WIDTH ** -0.5),
        'ffn_conv_b': nrm((DEPTH, 2 * D_FF), 0.02),
        'ffn_w_down': nrm((DEPTH, D_FF, D_MODEL), D_FF ** -0.5),
    }


def reference(x_prompt, x_sample, mem_prompt, cache_mem_k, cache_mem_v, state_shift, state_wkv, state_conv,
              norm_mix_pre, norm_mix_post, norm_ffn_pre, norm_ffn_post, norm_mem, w_in, w_out, w_mem_k, w_mem_v,
              gm_ln_g, gm_ln_b, gm_ws, gm_bs, rk_mu, rk_w0, rk_w2, rk_a0, rk_a2, rk_g2, rk_kk, rk_ka, rk_rk,
              rk_lnx_g, rk_lnx_b, ffn_w_up, ffn_conv_w, ffn_conv_b, ffn_w_down):
    Bp = x_prompt.shape[0]
    y_p, y_s = x_prompt, x_sample
    p_mk, p_mv, p_cv, p_sh, p_wkv, p_conv = [], [], [], [], [], []
    s_cv, s_sh, s_wkv, s_conv = [], [], [], []
    for l in range(DEPTH):
        lw = {
            'norm_mix_pre': norm_mix_pre[l], 'norm_mix_post': norm_mix_post[l],
            'norm_ffn_pre': norm_ffn_pre[l], 'norm_ffn_post': norm_ffn_post[l],
            'w_in': w_in[l], 'w_out': w_out[l],
            'gm_ln_g': gm_ln_g[l], 'gm_ln_b': gm_ln_b[l], 'gm_ws': gm_ws[l], 'gm_bs': gm_bs[l],
            'rk_mu': rk_mu[l], 'rk_w0': rk_w0[l], 'rk_w2': rk_w2[l], 'rk_a0': rk_a0[l], 'rk_a2': rk_a2[l],
            'rk_g2': rk_g2[l], 'rk_kk': rk_kk[l], 'rk_ka': rk_ka[l], 'rk_rk': rk_rk[l],
            'rk_lnx_g': rk_lnx_g[l], 'rk_lnx_b': rk_lnx_b[l],
            'ffn_w_up': ffn_w_up[l], 'ffn_conv_w': ffn_conv_w[l], 'ffn_conv_b': ffn_conv_b[l],
            'ffn_w_down': ffn_w_down[l],
        }
        mk, mv = memory_kv(mem_prompt, norm_mem[l], w_mem_k[l], w_mem_v[l])
        zero_shift = jnp.zeros((Bp, B_PROJ), x_prompt.dtype)
        zero_wkv = jnp.zeros((Bp, B_HEADS, B_HEAD_DIM, B_HEAD_DIM), jnp.float32)
        zero_conv = jnp.zeros((Bp, CONV_WIDTH - 1, 2 * D_FF), x_prompt.dtype)
        y_p, cv, sh, wkv, conv = decoder_layer(y_p, mk, mv, zero_shift, zero_wkv, zero_conv, lw)
        p_mk.append(mk); p_mv.append(mv); p_cv.append(cv); p_sh.append(sh); p_wkv.append(wkv); p_conv.append(conv)
        y_s, cv, sh, wkv, conv = decoder_layer(y_s, cache_mem_k[l], cache_mem_v[l], state_shift[l],
                                               state_wkv[l], state_conv[l], lw)
        s_cv.append(cv); s_sh.append(sh); s_wkv.append(wkv); s_conv.append(conv)
    return (y_p, y_s,
            jnp.stack(p_mk), jnp.stack(p_mv), jnp.stack(p_cv), jnp.stack(p_sh), jnp.stack(p_wkv), jnp.stack(p_conv),
            jnp.stack(s_cv), jnp.stack(s_sh), jnp.stack(s_wkv), jnp.stack(s_conv))
```

```python
import contextlib
import numpy as np
import concourse.bass as bass
import concourse.mybir as mybir
from concourse.bass_utils import run_bass_kernel_spmd

F32 = mybir.dt.float32
BF16 = mybir.dt.bfloat16
AF = mybir.ActivationFunctionType
ALU = mybir.AluOpType
AX = mybir.AxisListType

D = 2048
DFF = 5632
NFC = 44
INC = 4800
C0 = 0.6065306597126334
CG = [(0, 512), (512, 512), (1024, 512), (1536, 512), (2048, 512), (2560, 512), (3072, 512), (3584, 512),
      (4096, 192), (4288, 512)]
CP_MU, CP_W0, CP_A0, CP_KK, CP_KA, CP_RK, CP_LG, CP_LB = 0, 26, 34, 42, 50, 58, 66, 74
CP_GMIX, CP_GFFN, CP_GMEM = 82, 98, 114
CP_CW0, CP_CW1, CP_CW2, CP_CB = 130, 218, 306, 394
NCP = 482
K_ID, K_SL, K_SU, K_UI, K_BSL, K_BSU, K_BUI, K_B64, K_ONES = [i * 128 for i in range(9)]
K_RM16 = 9 * 128
K_I64 = K_RM16 + 16
K_RSTP = K_I64 + 64
K_RSTS = K_RSTP + 256
NCONST = K_RSTS + 256
R_GMIXP, R_GFFNP, R_LNG, R_LNB = 0, 2048, 0, 512
NREP = 5120


class Buf:
    __slots__ = ("name", "w", "rs")

    def __init__(self, name):
        self.name = name
        self.w = None
        self.rs = []


class TB:
    def __init__(self, t, name):
        self.t = t
        self.b = Buf(name)


class Sched:
    EPOCH = 3500
    ND = 24

    def __init__(self, nc, es):
        self.nc = nc
        self.es = es
        self.eng = {"pe": nc.tensor, "act": nc.scalar, "dve": nc.vector, "pool": nc.gpsimd, "sp": nc.sync}
        self.cnt = {e: 0 for e in self.eng}
        self.sems = {e: [] for e in self.eng}
        self.seen = {e: {} for e in self.eng}
        self.dsem = [es.enter_context(nc.semaphore(f"dma{i}")) for i in range(self.ND)]
        self.dval = [0] * self.ND
        self.NDP = 6
        self.dnext = {True: 0, False: self.NDP}
        self.out_tokens = []
        self.defer = None

    def _sem(self, e, ep):
        while len(self.sems[e]) <= ep:
            self.sems[e].append(self.es.enter_context(self.nc.semaphore(f"s_{e}_{len(self.sems[e])}")))
        return self.sems[e][ep]

    def _wait(self, e, sem, val):
        key = id(sem)
        if self.seen[e].get(key, 0) >= val:
            return
        self.eng[e].wait_ge(sem, val)
        self.seen[e][key] = val

    def _deps(self, e, r, w):
        toks = []
        for b in r:
            if b.w is not None:
                toks.append(b.w)
        for b in w:
            if b.w is not None:
                toks.append(b.w)
            toks.extend(b.rs)
        mx = {}
        for (sem, val, src) in toks:
            if src == e and (e == "pe" or not SAME_ENGINE_SYNC):
                continue
            k = id(sem)
            if k not in mx or mx[k][1] < val:
                mx[k] = (sem, val)
        for (sem, val) in mx.values():
            self._wait(e, sem, val)

    def _mark(self, tok, r, w):
        for b in r:
            b.rs.append(tok)
            if len(b.rs) > 64:
                b.rs = b.rs[-64:] if False else b.rs
        for b in w:
            b.w = tok
            b.rs = []

    def mark(self):
        if self.defer is not None:
            self.defer.append(("mark", (), {}))

    def run_unit(self, lst):
        while lst:
            kind, args, kw = lst.pop(0)
            if kind == "mark":
                return
            (self.op if kind == "op" else self.dma)(*args, **kw)

    def run_deferred(self, lst, n):
        for _ in range(min(n, len(lst))):
            kind, args, kw = lst.pop(0)
            (self.op if kind == "op" else self.dma)(*args, **kw)

    def op(self, e, fn, r=(), w=()):
        if self.defer is not None:
            self.defer.append(("op", (e, fn), dict(r=list(r), w=list(w))))
            return
        r = [x.b if isinstance(x, TB) else x for x in r]
        w = [x.b if isinstance(x, TB) else x for x in w]
        self._deps(e, r, w)
        ins = fn(self.eng[e])
        c = self.cnt[e]
        ep, v = divmod(c, self.EPOCH)
        sem = self._sem(e, ep)
        ins.then_inc(sem, 1)
        self.cnt[e] = c + 1
        self._mark((sem, v + 1, e), r, w)

    def dma(self, q, out, in_, r=(), w=(), is_out=False):
        if self.defer is not None:
            self.defer.append(("dma", (q, out, in_), dict(r=list(r), w=list(w), is_out=is_out)))
            return
        r = [x.b if isinstance(x, TB) else x for x in r]
        w = [x.b if isinstance(x, TB) else x for x in w]
        self._deps(q, r, w)
        sw = (q == "pool")
        i = self.dnext[sw]
        self.dnext[sw] = (i + 1) % self.NDP if sw else self.NDP + (i + 1 - self.NDP) % (self.ND - self.NDP)
        sem = self.dsem[i]
        if self.dval[i] > 0:
            self._wait(q, sem, self.dval[i])
        self.dval[i] += 16
        self.eng[q].dma_start(out=out, in_=in_).then_inc(sem, 16)
        tok = (sem, self.dval[i], "dma")
        self._mark(tok, r, w)
        if is_out:
            self.out_tokens.append(tok)

    def barrier(self):
        toks = []
        for f in ("pe", "act", "dve", "pool"):
            c = self.cnt[f]
            if c > 0:
                ep, v = divmod(c - 1, self.EPOCH)
                toks.append((self._sem(f, ep), v + 1))
        for i in range(self.ND):
            if self.dval[i] > 0:
                toks.append((self.dsem[i], self.dval[i]))
        for e in ("pe", "act", "dve", "pool", "sp"):
            for (sem, val) in toks:
                self._wait(e, sem, val)

    def finish(self):
        for i in range(self.ND):
            if self.dval[i] > 0:
                self._wait("sp", self.dsem[i], self.dval[i])


KINDS = ["pre"] * 7 + ["full"] * 9 + ["sample"]
DO_MEM = True
STOP_AFTER_MIX = False
STOP_AT = 0
SAME_ENGINE_SYNC = True


class _Stop(Exception):
    pass


def ck(n):
    if STOP_AT == n:
        raise _Stop()


def build():
    nc = bass.Bass("TRN2", target_bir_lowering=False)

    def din(name, shape):
        return nc.dram_tensor(name, list(shape), F32, kind="ExternalInput").ap()

    def dout(name, shape):
        return nc.dram_tensor(name, list(shape), F32, kind="ExternalOutput").ap()

    xwin = din("xwin", [2048, D]); xsm = din("xs", [128, D]); memx = din("mem", [256, D])
    ckT = din("ckT", [16, 128, 1024]); cvv = din("cv", [16, 256, 512])
    shT = din("shT", [128, 26 * 16]); s0T = din("s0T", [128, 8 * 16 * 64]); cvT = din("cvT", [128, 88 * 32])
    w_in = din("w_in", [D, INC]); w_out = din("w_out", [D, D])
    w_mk = din("w_mk", [D, 512]); w_mv = din("w_mv", [D, 512])
    w_up = din("w_up", [D, 2 * DFF]); w_dn = din("w_dn", [DFF, D])
    cpar = din("cpar", [128, NCP]); cst = din("cst", [128, NCONST]); rep = din("rep", [128, NREP])
    lora = din("lora", [128, 2048]); wmTp = din("wmTp", [128, 512]); wmTs = din("wmTs", [128, 512])
    bsp = din("bsp", [128, 4]); bss = din("bss", [128, 4]); cflag = din("cflag", [128, 1])

    def dbf(name, shape):
        return nc.dram_tensor(name, list(shape), BF16, kind="Internal").ap()

    wb = {"w_in": dbf("wb_in", [10, 128, 8192]), "w_out": dbf("wb_out", [4, 128, 8192]), "w_mk": dbf("wb_mk", [1, 128, 8192]), "w_mv": dbf("wb_mv", [1, 128, 8192]),
          "w_up": dbf("wb_up", [22, 128, 8192]), "w_dn": dbf("wb_dn", [11, 128, 8192])}

    def slab_view(key):
        name, k0, k1, c0, c1 = key
        if name == "w_in":
            idx = [g for g, (cs, n_) in enumerate(CG) if cs == c0][0]
        elif name == "w_dn":
            idx = k0 // 4
        else:
            idx = c0 // 512
        a_, b_ = k1 - k0, c1 - c0
        return wb[name][idx][:, 0:a_ * b_].rearrange("p (a b) -> p a b", a=a_)

    wf = {"w_in": w_in, "w_out": w_out, "w_mk": w_mk, "w_mv": w_mv, "w_up": w_up, "w_dn": w_dn}

    o_yp = dout("o_yp", [1024, D]); o_ys = dout("o_ys", [128, D])
    o_mk = dout("o_mk", [256, 512]); o_mv = dout("o_mv", [256, 512])
    o_pcv = dout("o_pcv", [128, 512]); o_psh = dout("o_psh", [1, 3264])
    o_pwkv = dout("o_pwkv", [128, 512]); o_pconv = dout("o_pconv", [128, 88 * 2])
    o_scv = dout("o_scv", [128, 512]); o_ssh = dout("o_ssh", [16, 3264])
    o_swkv = dout("o_swkv", [128, 8 * 16 * 64]); o_sconv = dout("o_sconv", [128, 88 * 32])

    es = contextlib.ExitStack()
    with es:
        S = Sched(nc, es)

        def sb(name, shape, dt=F32):
            return TB(es.enter_context(nc.sbuf_tensor(name, list(shape), dt)), name)

        banks = [TB(es.enter_context(nc.psum_tensor(f"pb{i}", [128, 512], F32)), f"pb{i}") for i in range(8)]
        rr = [0]

        def pbank():
            b = banks[4 + rr[0] % 4]
            rr[0] += 1
            return b

        cp = sb("cp", [128, NCP]); K = sb("K", [128, NCONST]); RP = sb("RP", [128, 1024])
        lo = sb("lo", [128, 2048]); wmp = sb("wmp", [128, 512]); wms = sb("wms", [128, 512])
        bsP = sb("bsP", [128, 4]); bsS = sb("bsS", [128, 4]); omka = sb("omka", [128, 8]); cfl = sb("cfl", [128, 1])
        S.dma("pool", cfl.t[:], cflag[:, :], w=[cfl])
        for (t_, d_) in ((cp, cpar), (K, cst), (RP, rep[:, 4096:5120]), (lo, lora), (wmp, wmTp), (wms, wmTs), (bsP, bsp), (bsS, bss)):
            S.dma("pool", t_.t[:], d_ if t_ is RP else d_[:, :], w=[t_])
        ident = K.t[:, K_ID:K_ID + 128]
        S.op("dve", lambda e: e.tensor_scalar(out=omka.t[:], in0=cp.t[:, CP_KA:CP_KA + 8], scalar1=-1.0, scalar2=1.0,
                                              op0=ALU.mult, op1=ALU.add), r=[cp], w=[omka])
        for (wm_, mo) in ((wmp, K_UI), (wms, K_BUI)):
            m4 = K.t[:, mo:mo + 128].unsqueeze(1).broadcast_to([128, 4, 128])
            v4 = wm_.t[:].rearrange("p (h t) -> p h t", h=4)
            S.op("dve", lambda e, v4=v4, m4=m4: e.tensor_tensor(out=v4, in0=v4, in1=m4, op=ALU.mult), r=[K, wm_], w=[wm_])

        NSL = 3
        slabs = [sb(f"slab{i}", [128, 8192], BF16) for i in range(NSL)]
        plan = []
        issued = [0]
        used = [0]

        def plan_tile(kind):
            if kind == "mem":
                return
            groups = range(4, 9) if kind == "pre" else range(10)
            for g in groups:
                c0, n = CG[g]
                plan.append(("w_in", 0, 16, c0, c0 + n))
            if kind == "pre":
                return
            for g in range(4):
                plan.append(("w_out", 0, 16, g * 512, (g + 1) * 512))
            for q in range(12):
                if q < 11:
                    plan.append(("w_up", 0, 16, q * 512, (q + 1) * 512))
                    plan.append(("w_up", 0, 16, DFF + q * 512, DFF + (q + 1) * 512))
                if q >= 1:
                    plan.append(("w_dn", 4 * (q - 1), 4 * (q - 1) + 4, 0, D))

        conv_buf = {}

        conv_pos = [0]

        def ensure_conv(upto):
            while conv_pos[0] < min(upto, len(plan)):
                key = plan[conv_pos[0]]
                conv_pos[0] += 1
                if key in conv_buf:
                    continue
                name, k0, k1, c0, c1 = key
                cb = Buf("cv_" + name + str(key[1:]))
                conv_buf[key] = cb
                src = wf[name].rearrange("(k p) c -> p k c", p=128)[:, k0:k1, c0:c1]
                dst = slab_view(key)
                S.dma("pool", dst, src, w=[cb])

        def next_slab():
            i = used[0]
            while issued[0] < len(plan) and issued[0] < i + NSL - 1:
                j = issued[0]
                ensure_conv(j + 4)
                name, k0, k1, c0, c1 = plan[j]
                src = slab_view(plan[j])
                a, b_ = k1 - k0, c1 - c0
                sl = slabs[j % NSL]
                dst = sl.t[:, 0:a * b_].rearrange("p (a b) -> p a b", a=a)
                S.dma("sp", dst, src, r=[conv_buf[plan[j]]], w=[sl])
                issued[0] += 1
            used[0] += 1
            sl = slabs[i % NSL]
            name, k0, k1, c0, c1 = plan[i]
            a, b_ = k1 - k0, c1 - c0
            return sl, sl.t[:, 0:a * b_].rearrange("p (a b) -> p a b", a=a)

        xt = sb("xt", [128, D]); hT = sb("hT", [128, 16, 128], BF16)
        proj = sb("proj", [128, 3776]); mixT = sb("mixT", [128, 16, 128], BF16)
        xn = TB(proj.t[:, 0:2048], "xn"); xn.b = proj.b
        st6 = sb("st6", [128, 8]); rstd = sb("rstd", [128, 1]); ss4 = sb("ss4", [128, 4])
        lj = sb("lj", [128, 256])
        kTp = sb("kTp", [128, 4, 256], BF16); Vp = sb("Vp", [128, 2, 512], BF16); onesb = sb("onesb", [128, 128], BF16)
        carry = sb("carry", [128, 26]); shTs = sb("shTs", [128, 26, 16])
        ST = sb("ST", [128, 8, 64]); ccar = sb("ccar", [128, 88, 2]); STb = sb("STb", [128, 8, 64], BF16)
        arena = es.enter_context(nc.sbuf_tensor("arena", [128, 22272], F32))
        apos = [0]

        def carve(name, shape, dt=F32):
            n = 1
            for d_ in shape[1:]:
                n *= d_
            words = n if dt == F32 else (n + 1) // 2
            ap = arena[:, apos[0]:apos[0] + words]
            apos[0] += words
            assert apos[0] <= 22272, (name, apos[0])
            if dt != F32:
                ap = ap.bitcast(dt)
            if len(shape) == 3:
                ap = ap.rearrange("p (a b) -> p a b", a=shape[1])
            elif len(shape) == 4:
                ap = ap.rearrange("p (a b c) -> p a b c", a=shape[1], b=shape[2])
            return TB(ap, name)

        class V:
            pass

        qT = carve("qT", [128, 4, 128], BF16); PT = carve("PT", [128, 8, 128], BF16); rden = carve("rden", [128, 512])
        au = carve("au", [128, 512]); av = carve("av", [128, 512]); aout = carve("aout", [128, 512])
        Rsets = [[carve(f"R{s_}_{i}", [128, 2, 128]) for i in range(14)] for s_ in range(2)]
        pTn = carve("pTn", [128, 2, 128]); prv = carve("prv", [128, 2, 128])
        psr2 = [carve(f"psr{i}", [128, 2, 128]) for i in range(2)]; psk2 = [carve(f"psk{i}", [128, 2, 128]) for i in range(2)]
        psv2 = [carve(f"psv{i}", [128, 2, 128]) for i in range(2)]; psl = carve("psl", [128, 2, 128])
        S0c = carve("S0c", [128, 16, 64]); S1c = carve("S1c", [128, 16, 64])
        Xa = [carve(f"Xa{i}", [128, 4, 128], BF16) for i in range(2)]
        Xb = [carve(f"Xb{i}", [128, 4, 128], BF16) for i in range(2)]
        Wk = carve("Wk", [128, 4, 128], BF16); Zz = [carve(f"Zz{i}", [128, 4, 128], BF16) for i in range(2)]
        ZFb = carve("ZFb", [128, 4, 128], BF16); idb = carve("idb", [128, 128], BF16)
        AakT = carve("AakT", [128, 4, 128], BF16); ArbT = carve("ArbT", [128, 4, 128], BF16); ArkT = carve("ArkT", [128, 4, 128], BF16)
        HT = carve("HT", [128, 128], BF16); QTI = carve("QTI", [128, 16, 64], BF16)
        Bm = carve("Bm", [128, 16, 64], BF16); Km = carve("Km", [128, 16, 64], BF16)
        VB = carve("VB", [128, 512], BF16); KA = carve("KA", [128, 512], BF16)
        atb = carve("atb", [128, 2, 128], BF16); btb = carve("btb", [128, 2, 128], BF16)
        ktb = carve("ktb", [128, 2, 128], BF16); rtb = carve("rtb", [128, 2, 128], BF16)
        S0cb = carve("S0cb", [128, 16, 64], BF16)
        mix_top = apos[0]
        print("arena words used (mix)", mix_top)
        apos[0] = 0
        gpost = carve("gpost", [128, D])
        actT = carve("actT", [128, NFC, 128], BF16)
        actQ = [Buf(f"actT{q}") for q in range(11)]
        scar = carve("scar", [128, 88, 16, 2])
        ext8 = [carve(f"ext8_{i}", [128, 8, 160]) for i in range(2)]
        acc8 = [carve(f"acc8_{i}", [128, 8, 128]) for i in range(2)]

        arena_tb = TB(arena, "arena")
        S.op("dve", lambda e: e.memset(arena[:, :], 0.0), w=[arena_tb])
        S.barrier()
        S.op("dve", lambda e: e.memset(ST.t[:], 0.0), w=[ST])
        S.op("dve", lambda e: e.memset(STb.t[:], 0.0), w=[STb])
        S.op("dve", lambda e: e.tensor_copy(out=idb.t[:], in_=K.t[:, K_ID:K_ID + 128]), r=[K], w=[idb])
        S.op("dve", lambda e: e.tensor_copy(out=onesb.t[:], in_=K.t[:, K_ONES:K_ONES + 128]), r=[K], w=[onesb])
        S.op("dve", lambda e: e.memset(carry.t[:], 0.0), w=[carry])
        S.op("dve", lambda e: e.memset(ccar.t[:], 0.0), w=[ccar])
        S.dma("pool", shTs.t[:].rearrange("p a b -> p (a b)"), shT[:, :], w=[shTs])

        def cpc(off, n=1):
            return cp.t[:, off:off + n]

        def rmsnorm_to_hT(src, gofs):
            S.op("act", lambda e: e.activation(out=xn.t[:], in_=src.t[:], func=AF.Square, accum_out=st6.t[:, 0:1]),
                 r=[src], w=[xn, st6])
            S.op("act", lambda e: e.activation(out=st6.t[:, 1:2], in_=st6.t[:, 0:1], func=AF.Sqrt, scale=1.0 / D, bias=1e-6),
                 r=[st6], w=[st6])
            S.op("dve", lambda e: e.reciprocal(out=rstd.t[:], in_=st6.t[:, 1:2]), r=[st6], w=[rstd])
            S.op("dve", lambda e: e.tensor_scalar(out=xn.t[:], in0=src.t[:], scalar1=rstd.t[:, 0:1], scalar2=None, op0=ALU.mult),
                 r=[src, rstd], w=[xn])
            for q in range(4):
                pb = pbank()
                for i in range(4):
                    kc = 4 * q + i
                    S.op("pe", lambda e, kc=kc, i=i, pb=pb: e.transpose(pb.t[:, i * 128:(i + 1) * 128], xn.t[:, kc * 128:(kc + 1) * 128], ident),
                         r=[xn, K], w=[pb])
                g4 = cp.t[:, gofs + 4 * q:gofs + 4 * q + 4].unsqueeze(2).broadcast_to([128, 4, 128])
                S.op("dve", lambda e, q=q, pb=pb, g4=g4: e.tensor_tensor(out=hT.t[:, 4 * q:4 * q + 4, :], in0=pb.t[:].rearrange("p (a b) -> p a b", a=4),
                                                                  in1=g4, op=ALU.mult), r=[pb, cp], w=[hT])

        def project(slv, ncols, pb):
            sl, v = slv
            for kc in range(16):
                S.op("pe", lambda e, kc=kc: e.matmul(pb.t[:, 0:ncols], lhsT=hT.t[:, kc, :], rhs=v[:, kc, :], start=(kc == 0), stop=(kc == 15)),
                     r=[hT, sl], w=[pb])

        mem_plan = [("w_mk", 0, 16, 0, 512), ("w_mv", 0, 16, 0, 512)]
        for mt in range(2):
            plan.extend(mem_plan)
        kinds = list(KINDS)
        for kd in kinds:
            plan_tile(kd)

        for mt in range(2):
            S.dma("pool", xt.t[:], memx[mt * 128:(mt + 1) * 128, :], w=[xt])
            rmsnorm_to_hT(xt, CP_GMEM)
            for which in range(2):
                slv = next_slab()
                pb = pbank()
                project(slv, 512, pb)
                S.op("act", lambda e, pb=pb: e.copy(out=rden.t[:], in_=pb.t[:]), r=[pb], w=[rden])
                S.dma("pool", (o_mk if which == 0 else o_mv)[mt * 128:(mt + 1) * 128, :], rden.t[:], r=[rden], is_out=True)
                if which == 0:
                    pb2 = pbank()
                    for h in range(4):
                        S.op("pe", lambda e, h=h, pb2=pb2: e.transpose(pb2.t[:, h * 128:(h + 1) * 128], rden.t[:, h * 128:(h + 1) * 128], ident),
                             r=[rden, K], w=[pb2])
                    S.op("dve", lambda e, pb2=pb2, mt=mt: e.tensor_copy(out=kTp.t[:, :, mt * 128:(mt + 1) * 128],
                                                                  in_=pb2.t[:].rearrange("p (a b) -> p a b", a=4)), r=[pb2], w=[kTp])
                else:
                    S.op("dve", lambda e, mt=mt: e.tensor_copy(out=Vp.t[:, mt, :], in_=rden.t[:]), r=[rden], w=[Vp])

        def f8(tb):
            return tb.t[:].rearrange("p (a b) -> p a b", a=8)

        def do_tile(ti, kind):
            sample = kind == "sample"
            full = kind != "pre"
            nseq = 16 if sample else 1
            L = 128 // nseq
            mSL, mSU, mUI = (K_BSL, K_BSU, K_BUI) if sample else (K_SL, K_SU, K_UI)
            nlev = 3 if sample else 7
            src = xsm[:, :] if sample else xwin[ti * 128:(ti + 1) * 128, :]
            S.dma("pool", xt.t[:], src, w=[xt])
            rmsnorm_to_hT(xt, CP_GMIX)
            groups = range(4, 9) if kind == "pre" else range(10)
            for g in groups:
                c0, n = CG[g]
                slv = next_slab()
                pb = pbank()
                project(slv, n, pb)
                if g == 0:
                    S.op("act", lambda e, pb=pb: e.activation(out=au.t[:], in_=pb.t[:], func=AF.Gelu_apprx_tanh), r=[pb], w=[au])
                elif g == 1:
                    S.op("act", lambda e, pb=pb: e.activation(out=av.t[:], in_=pb.t[:], func=AF.Gelu_apprx_tanh), r=[pb], w=[av])
                else:
                    S.op("act", lambda e, pb=pb, c0=c0, n=n: e.copy(out=proj.t[:, c0 - 1024:c0 - 1024 + n], in_=pb.t[:, 0:n]), r=[pb], w=[proj])
            ck(1)
            last_prompt = (kind == "full" and ti == 15)
            if last_prompt:
                S.dma("pool", o_psh[:, :], proj.t[127:128, 0:3264], r=[proj], is_out=True)
            if sample:
                S.dma("pool", o_ssh[:, :], proj.t[7:128:8, 0:3264], r=[proj], is_out=True)

            ac_list = []
            if full:
                if not sample:
                    S.defer = ac_list
                S.op("dve", lambda e: e.bn_stats(out=st6.t[:, 0:6], in_=av.t[:]), r=[av], w=[st6])
                S.op("dve", lambda e: e.bn_aggr(out=st6.t[:, 6:8], in_=st6.t[:, 0:6]), r=[st6], w=[st6])
                S.op("act", lambda e: e.activation(out=st6.t[:, 0:1], in_=st6.t[:, 7:8], func=AF.Sqrt, scale=1.0, bias=1e-5), r=[st6], w=[st6])
                S.op("dve", lambda e: e.reciprocal(out=rstd.t[:], in_=st6.t[:, 0:1]), r=[st6], w=[rstd])
                S.op("dve", lambda e: e.tensor_scalar(out=av.t[:], in0=av.t[:], scalar1=st6.t[:, 6:7], scalar2=rstd.t[:, 0:1],
                                                      op0=ALU.subtract, op1=ALU.mult), r=[av, st6, rstd], w=[av])
                S.op("dve", lambda e: e.tensor_tensor(out=av.t[:], in0=av.t[:], in1=RP.t[:, R_LNG:R_LNG + 512], op=ALU.mult), r=[av, RP], w=[av])
                S.op("dve", lambda e: e.tensor_tensor(out=av.t[:], in0=av.t[:], in1=RP.t[:, R_LNB:R_LNB + 512], op=ALU.add), r=[av, RP], w=[av])
                if last_prompt:
                    S.dma("pool", o_pcv[:, :], av.t[:], r=[av], is_out=True)
                if sample:
                    S.dma("pool", o_scv[:, :], av.t[:], r=[av], is_out=True)
                S.mark()
                wm_ = wms if sample else wmp
                bs_ = bsS if sample else bsP
                pb = pbank()
                for h in range(4):
                    S.op("pe", lambda e, h=h, pb=pb: e.matmul(pb.t[:, h * 128:(h + 1) * 128], lhsT=wm_.t[:, h * 128:(h + 1) * 128],
                                                              rhs=av.t[:, h * 128:(h + 1) * 128], start=True, stop=True), r=[wm_, av], w=[pb])
                for h in range(4):
                    S.op("dve", lambda e, h=h, pb=pb: e.scalar_tensor_tensor(out=aout.t[:, h * 128:(h + 1) * 128], in0=pb.t[:, h * 128:(h + 1) * 128],
                                                                             scalar=bs_.t[:, h:h + 1], in1=au.t[:, h * 128:(h + 1) * 128],
                                                                             op0=ALU.add, op1=ALU.mult), r=[pb, bs_, au], w=[aout])
                S.mark()
                pb = pbank()
                for h in range(4):
                    S.op("pe", lambda e, h=h, pb=pb: e.transpose(pb.t[:, h * 128:(h + 1) * 128], aout.t[:, h * 128:(h + 1) * 128], ident), r=[aout, K], w=[pb])
                S.op("act", lambda e, pb=pb: e.copy(out=mixT.t[:, 0:4, :], in_=pb.t[:].rearrange("p (a b) -> p a b", a=4)), r=[pb], w=[mixT])

                S.mark()
                pb = pbank()
                for h in range(4):
                    S.op("pe", lambda e, h=h, pb=pb: e.transpose(pb.t[:, h * 128:(h + 1) * 128], proj.t[:, 3264 + h * 128:3264 + (h + 1) * 128], ident),
                         r=[proj, K], w=[pb])
                S.op("act", lambda e, pb=pb: e.copy(out=qT.t[:], in_=pb.t[:].rearrange("p (a b) -> p a b", a=4)), r=[pb], w=[qT])
                S.mark()
                pbs = [pbank(), pbank()]
                pnum = banks[0] if sample else None
                pden = banks[1] if sample else None
                if not sample:
                    for h in range(4):
                        for mc in range(2):
                            i = h * 2 + mc
                            S.op("pe", lambda e, h=h, mc=mc, i=i: e.matmul(pbs[i // 4].t[:, (i % 4) * 128:(i % 4 + 1) * 128], lhsT=kTp.t[:, h, mc * 128:(mc + 1) * 128],
                                                                          rhs=qT.t[:, h, :], start=True, stop=True), r=[kTp, qT], w=[pbs[i // 4]])
                else:
                    for j in range(16):
                        kj, vj = kTp, Vp
                        S.dma("pool", kj.t[:].rearrange("p a b -> p (a b)"), ckT[j], w=[kj])
                        S.dma("pool", vj.t[:], cvv[j].rearrange("(mc p) c -> p mc c", p=128), w=[vj])
                        for h in range(4):
                            for mc in range(2):
                                i = h * 2 + mc
                                S.op("pe", lambda e, h=h, mc=mc, i=i, j=j, kj=kj: e.matmul(
                                    pbs[i // 4].t[:, (i % 4) * 128 + 8 * j:(i % 4) * 128 + 8 * j + 8], lhsT=kj.t[:, h, mc * 128:(mc + 1) * 128],
                                    rhs=qT.t[:, h, 8 * j:8 * j + 8], start=True, stop=True), r=[kj, qT], w=[pbs[i // 4]])
                        if j == 0:
                            pass
                        for half in range(2):
                            S.op("act", lambda e, half=half, j=j: e.activation(
                                out=PT.t[:, 4 * half:4 * half + 4, 8 * j:8 * j + 8],
                                in_=pbs[half].t[:].rearrange("p (a b) -> p a b", a=4)[:, :, 8 * j:8 * j + 8], func=AF.Exp, scale=128.0 ** -0.5),
                                r=[pbs[half]], w=[PT])
                        for h in range(4):
                            for mc in range(2):
                                S.op("pe", lambda e, h=h, mc=mc, j=j, vj=vj: e.matmul(
                                    pnum.t[:, h * 128 + 8 * j:h * 128 + 8 * j + 8], lhsT=vj.t[:, mc, h * 128:(h + 1) * 128],
                                    rhs=PT.t[:, 2 * h + mc, 8 * j:8 * j + 8], start=(mc == 0), stop=(mc == 1)), r=[vj, PT], w=[pnum])
                if not sample:
                    for half in range(2):
                        S.op("act", lambda e, half=half: e.activation(out=PT.t[:, 4 * half:4 * half + 4, :],
                                                                      in_=pbs[half].t[:].rearrange("p (a b) -> p a b", a=4), func=AF.Exp, scale=128.0 ** -0.5),
                             r=[pbs[half]], w=[PT])
                    S.mark()
                    pnum, pden = pbank(), pbank()
                    for h in range(4):
                        for mc in range(2):
                            S.op("pe", lambda e, h=h, mc=mc: e.matmul(pnum.t[:, h * 128:(h + 1) * 128], lhsT=Vp.t[:, mc, h * 128:(h + 1) * 128],
                                                                      rhs=PT.t[:, 2 * h + mc, :], start=(mc == 0), stop=(mc == 1)), r=[Vp, PT], w=[pnum])
                for h in range(4):
                    for mc in range(2):
                        S.op("pe", lambda e, h=h, mc=mc: e.matmul(pden.t[:, h * 128:(h + 1) * 128], lhsT=onesb.t[:],
                                                                  rhs=PT.t[:, 2 * h + mc, :], start=(mc == 0), stop=(mc == 1)), r=[onesb, PT], w=[pden])
                S.op("dve", lambda e: e.reciprocal(out=rden.t[:], in_=pden.t[:]), r=[pden], w=[rden])
                S.op("dve", lambda e: e.tensor_tensor(out=mixT.t[:, 12:16, :], in0=pnum.t[:].rearrange("p (a b) -> p a b", a=4),
                                                      in1=rden.t[:].rearrange("p (a b) -> p a b", a=4), op=ALU.mult), r=[pnum, rden], w=[mixT])
                S.mark()
                S.defer = None

            Vt = TB(VB.t[:, 0:256], "Vt"); Vt.b = VB.b
            Bt = TB(VB.t[:, 256:512], "Bt"); Bt.b = VB.b
            Kt = TB(KA.t[:, 0:256], "Kt"); Kt.b = KA.b
            At = TB(KA.t[:, 256:512], "At"); At.b = KA.b

            def shift(dst, c0, n):
                pb = pbank()
                for i in range(n):
                    cc = c0 + i
                    ncol = 64 if cc == 25 else 128
                    S.op("pe", lambda e, i=i, cc=cc, ncol=ncol, pb=pb: e.transpose(pb.t[0:ncol, i * 128:(i + 1) * 128], proj.t[:, cc * 128:cc * 128 + ncol], ident),
                         r=[proj, K], w=[pb])
                if c0 + n - 1 == 25:
                    S.op("act", lambda e, pb=pb: e.copy(out=pTn.t[:, 0, :], in_=pb.t[:, 0:128]), r=[pb], w=[pTn])
                    S.op("act", lambda e, pb=pb: e.copy(out=pTn.t[0:64, 1, :], in_=pb.t[0:64, 128:256]), r=[pb], w=[pTn])
                else:
                    S.op("act", lambda e, pb=pb: e.copy(out=pTn.t[:, 0:n, :], in_=pb.t[:, 0:n * 128].rearrange("p (a b) -> p a b", a=n)), r=[pb], w=[pTn])
                S.op("pool", lambda e: e.tensor_copy(out=prv.t[:, 0:n, 1:128], in_=pTn.t[:, 0:n, 0:127]), r=[pTn], w=[prv])
                if sample:
                    S.op("pool", lambda e: e.tensor_copy(out=prv.t[:, 0:n, 0:128:8], in_=shTs.t[:, c0:c0 + n, :]), r=[shTs], w=[prv])
                else:
                    S.op("pool", lambda e: e.tensor_copy(out=prv.t[:, 0:n, 0:1], in_=carry.t[:, c0:c0 + n].unsqueeze(2)), r=[carry], w=[prv])
                    S.op("pool", lambda e: e.tensor_copy(out=carry.t[:, c0:c0 + n].unsqueeze(2), in_=pTn.t[:, 0:n, 127:128]), r=[pTn], w=[carry])
                S.op("pool", lambda e: e.tensor_tensor(out=prv.t[:, 0:n, :], in0=prv.t[:, 0:n, :], in1=pTn.t[:, 0:n, :], op=ALU.subtract), r=[prv, pTn], w=[prv])
                mu3 = cp.t[:, CP_MU + c0:CP_MU + c0 + n].unsqueeze(2).broadcast_to([128, n, 128])
                S.op("pool", lambda e: e.tensor_tensor(out=prv.t[:, 0:n, :], in0=prv.t[:, 0:n, :], in1=mu3, op=ALU.mult), r=[prv, cp], w=[prv])
                S.op("pool", lambda e: e.tensor_tensor(out=dst.t[:, 0:n, :], in0=prv.t[:, 0:n, :], in1=pTn.t[:, 0:n, :], op=ALU.add), r=[prv, pTn], w=[dst])

            shift(psl, 24, 2 if full else 1)
            S.op("act", lambda e: e.activation(out=lj.t[0:64, 0:128], in_=psl.t[0:64, 0, :], func=AF.Tanh), r=[psl], w=[lj])
            if full:
                S.op("act", lambda e: e.activation(out=lj.t[0:64, 128:256], in_=psl.t[0:64, 1, :], func=AF.Sigmoid), r=[psl], w=[lj])
            ck(2)
            YT = [banks[0], banks[1]]
            PS = [banks[2], banks[3]]
            id4 = idb.t[:].unsqueeze(1).broadcast_to([128, 4, 128])

            def mk4(off):
                return K.t[:, off:off + 128].unsqueeze(1).broadcast_to([128, 4, 128])

            def b4(pb):
                return pb.t[:].rearrange("p (a b) -> p a b", a=4)

            def fl(tb):
                return tb.t[:].rearrange("p a b -> p (a b)")

            def do_shifts(hg_):
                if full:
                    shift(psr2[hg_ % 2], 2 * hg_, 2)
                shift(psk2[hg_ % 2], 8 + 2 * hg_, 2)
                shift(psv2[hg_ % 2], 16 + 2 * hg_, 2)

            do_shifts(0)
            def E_stage(hg_):
                g0 = 2 * hg_
                (sgw, cum, gam, igam, gprev, a_, kkn, kmod, rt, kt, bt, at, gT, bonus) = Rsets[hg_ % 2]
                psr, psk, psv = psr2[hg_ % 2], psk2[hg_ % 2], psv2[hg_ % 2]

                def bc2(off):
                    return cp.t[:, off + g0:off + g0 + 2].unsqueeze(2).broadcast_to([128, 2, 128])

                pw = TB(banks[0].t[:, 0:256], "pw"); pw.b = banks[0].b
                pa = TB(banks[1].t[:, 0:256], "pa"); pa.b = banks[1].b
                for cc in range(2):
                    S.op("pe", lambda e, cc=cc: e.matmul(pw.t[:, cc * 128:(cc + 1) * 128], lhsT=lo.t[0:64, (g0 + cc) * 128:(g0 + cc + 1) * 128],
                                                         rhs=lj.t[0:64, 0:128], start=True, stop=True), r=[lo, lj], w=[pw])
                for cc in range(2):
                    S.op("pe", lambda e, cc=cc: e.matmul(pa.t[:, cc * 128:(cc + 1) * 128], lhsT=lo.t[64:128, (g0 + cc) * 128:(g0 + cc + 1) * 128],
                                                         rhs=psl.t[64:128, 0, :], start=True, stop=True), r=[lo, psl], w=[pa])
                for cc in range(2):
                    S.op("act", lambda e, cc=cc: e.activation(out=sgw.t[:, cc, :], in_=pw.t[:, cc * 128:(cc + 1) * 128], func=AF.Sigmoid,
                                                              bias=cp.t[:, CP_W0 + g0 + cc:CP_W0 + g0 + cc + 1], scale=1.0), r=[pw, cp], w=[sgw])
                    S.op("act", lambda e, cc=cc: e.activation(out=a_.t[:, cc, :], in_=pa.t[:, cc * 128:(cc + 1) * 128], func=AF.Sigmoid,
                                                              bias=cp.t[:, CP_A0 + g0 + cc:CP_A0 + g0 + cc + 1], scale=1.0), r=[pa, cp], w=[a_])
                if full:
                    pg = TB(banks[0].t[:, 256:512], "pg"); pg.b = banks[0].b
                    for cc in range(2):
                        S.op("pe", lambda e, cc=cc: e.matmul(pg.t[:, cc * 128:(cc + 1) * 128], lhsT=lo.t[0:64, 1024 + (g0 + cc) * 128:1024 + (g0 + cc + 1) * 128],
                                                             rhs=lj.t[0:64, 128:256], start=True, stop=True), r=[lo, lj], w=[pg])
                    S.op("act", lambda e: e.copy(out=fl(gT), in_=pg.t[:, 0:256]), r=[pg], w=[gT])
                ko = K_RSTS if sample else K_RSTP
                S.op("dve", lambda e: e.tensor_tensor_scan(out=fl(cum), data0=K.t[:, ko:ko + 256], data1=fl(sgw), initial=0.0, op0=ALU.mult, op1=ALU.add), r=[K, sgw], w=[cum])
                S.op("act", lambda e: e.activation(out=fl(gam), in_=fl(cum), func=AF.Exp, scale=-C0), r=[cum], w=[gam])
                S.op("act", lambda e: e.activation(out=fl(igam), in_=fl(cum), func=AF.Exp, scale=C0), r=[cum], w=[igam])
                S.op("dve", lambda e: e.tensor_tensor(out=fl(gprev), in0=fl(cum), in1=fl(sgw), op=ALU.subtract), r=[cum, sgw], w=[gprev])
                S.op("act", lambda e: e.activation(out=fl(gprev), in_=fl(gprev), func=AF.Exp, scale=-C0), r=[gprev], w=[gprev])
                S.op("dve", lambda e: e.tensor_tensor(out=kkn.t[:], in0=psk.t[:], in1=bc2(CP_KK), op=ALU.mult), r=[psk, cp], w=[kkn])
                S.op("dve", lambda e: e.tensor_tensor(out=kt.t[:], in0=kkn.t[:], in1=kkn.t[:], op=ALU.mult), r=[kkn], w=[kt])
                pk = TB(banks[2].t[:, 0:256], "pk"); pk.b = banks[2].b
                for cc in range(2):
                    S.op("pe", lambda e, cc=cc: e.matmul(pk.t[:, cc * 128:(cc + 1) * 128], lhsT=K.t[:, K_B64:K_B64 + 128], rhs=kt.t[:, cc, :], start=True, stop=True), r=[K, kt], w=[pk])
                S.op("act", lambda e: e.activation(out=fl(bt), in_=pk.t[:, 0:256], func=AF.Sqrt, scale=1.0, bias=1e-30), r=[pk], w=[bt])
                S.op("dve", lambda e: e.tensor_scalar(out=bt.t[:], in0=bt.t[:], scalar1=1e-12, scalar2=None, op0=ALU.max), r=[bt], w=[bt])
                S.op("dve", lambda e: e.reciprocal(out=bt.t[:], in_=bt.t[:]), r=[bt], w=[bt])
                S.op("dve", lambda e: e.tensor_tensor(out=kkn.t[:], in0=kkn.t[:], in1=bt.t[:], op=ALU.mult), r=[kkn, bt], w=[kkn])
                S.op("dve", lambda e: e.tensor_tensor(out=kmod.t[:], in0=a_.t[:], in1=bc2(CP_KA), op=ALU.mult), r=[a_, cp], w=[kmod])
                S.op("dve", lambda e: e.tensor_tensor(out=kmod.t[:], in0=kmod.t[:], in1=omka.t[:, g0:g0 + 2].unsqueeze(2).broadcast_to([128, 2, 128]), op=ALU.add), r=[kmod, omka], w=[kmod])
                S.op("dve", lambda e: e.tensor_tensor(out=kmod.t[:], in0=kmod.t[:], in1=psk.t[:], op=ALU.mult), r=[kmod, psk], w=[kmod])
                S.op("dve", lambda e: e.scalar_tensor_tensor(out=fl(at), in0=fl(kkn), scalar=-1.0, in1=fl(gprev), op0=ALU.mult, op1=ALU.mult), r=[kkn, gprev], w=[at])
                S.op("dve", lambda e: e.tensor_tensor(out=bt.t[:], in0=kkn.t[:], in1=a_.t[:], op=ALU.mult), r=[kkn, a_], w=[bt])
                S.op("dve", lambda e: e.tensor_tensor(out=bt.t[:], in0=bt.t[:], in1=igam.t[:], op=ALU.mult), r=[bt, igam], w=[bt])
                S.op("dve", lambda e: e.tensor_tensor(out=kt.t[:], in0=kmod.t[:], in1=igam.t[:], op=ALU.mult), r=[kmod, igam], w=[kt])
                if full:
                    S.op("dve", lambda e: e.tensor_tensor(out=rt.t[:], in0=psr.t[:], in1=gam.t[:], op=ALU.mult), r=[psr, gam], w=[rt])
                    S.op("dve", lambda e: e.tensor_tensor(out=bonus.t[:], in0=psr.t[:], in1=kmod.t[:], op=ALU.mult), r=[psr, kmod], w=[bonus])
                    S.op("dve", lambda e: e.tensor_tensor(out=bonus.t[:], in0=bonus.t[:], in1=bc2(CP_RK), op=ALU.mult), r=[bonus, cp], w=[bonus])
                    pbn = TB(banks[2].t[:, 256:512], "pbn"); pbn.b = banks[2].b
                    for cc in range(2):
                        S.op("pe", lambda e, cc=cc: e.matmul(pbn.t[:, cc * 128:(cc + 1) * 128], lhsT=K.t[:, K_B64:K_B64 + 128], rhs=bonus.t[:, cc, :], start=True, stop=True), r=[K, bonus], w=[pbn])
                    S.op("dve", lambda e: e.tensor_tensor(out=fl(bonus), in0=pbn.t[:, 0:256], in1=fl(psv), op=ALU.mult), r=[pbn, psv], w=[bonus])

            E_stage(0)
            for hg in range(4):
                g0 = 2 * hg
                (sgw, cum, gam, igam, gprev, a_, kkn, kmod, rt, kt, bt, at, gT, bonus) = Rsets[hg % 2]
                psr, psk, psv = psr2[hg % 2], psk2[hg % 2], psv2[hg % 2]

                def bc2(off):
                    return cp.t[:, off + g0:off + g0 + 2].unsqueeze(2).broadcast_to([128, 2, 128])

                S.op("act", lambda e: e.copy(out=atb.t[:], in_=at.t[:]), r=[at], w=[atb])
                S.op("act", lambda e: e.copy(out=btb.t[:], in_=bt.t[:]), r=[bt], w=[btb])
                S.op("act", lambda e: e.copy(out=ktb.t[:], in_=kt.t[:]), r=[kt], w=[ktb])
                if full:
                    S.op("act", lambda e: e.copy(out=rtb.t[:], in_=rt.t[:]), r=[rt], w=[rtb])
                ck(3)
                pb = pbank()
                for cc in range(2):
                    S.op("pe", lambda e, cc=cc, pb=pb: e.transpose(pb.t[:, cc * 128:(cc + 1) * 128], psv.t[:, cc, :], ident), r=[psv, K], w=[pb])
                    S.op("pe", lambda e, cc=cc, pb=pb: e.transpose(pb.t[:, 256 + cc * 128:256 + (cc + 1) * 128], bt.t[:, cc, :], ident), r=[bt, K], w=[pb])
                S.op("act", lambda e, pb=pb: e.copy(out=VB.t[:], in_=pb.t[:]), r=[pb], w=[VB])
                pb = pbank()
                for cc in range(2):
                    S.op("pe", lambda e, cc=cc, pb=pb: e.transpose(pb.t[:, cc * 128:(cc + 1) * 128], kt.t[:, cc, :], ident), r=[kt, K], w=[pb])
                    S.op("pe", lambda e, cc=cc, pb=pb: e.transpose(pb.t[:, 256 + cc * 128:256 + (cc + 1) * 128], at.t[:, cc, :], ident), r=[at, K], w=[pb])
                S.op("act", lambda e, pb=pb: e.copy(out=KA.t[:], in_=pb.t[:]), r=[pb], w=[KA])

                ck(4)

                def fm(buf, i):
                    return buf.t[(i % 2) * 64:(i % 2) * 64 + 64, i // 2, :]

                def xmat(lb, rb, moff, dst):
                    pe_, po_ = pbank(), pbank()
                    for i in range(4):
                        pb = pe_ if i % 2 == 0 else po_
                        sl_ = slice((i // 2) * 128, (i // 2 + 1) * 128)
                        S.op("pe", lambda e, i=i, sl_=sl_, pb=pb: e.matmul(pb.t[:, sl_], lhsT=fm(lb, i), rhs=fm(rb, i), start=True, stop=True), r=[lb, rb], w=[pb])
                    m2 = K.t[:, moff:moff + 128].unsqueeze(1).broadcast_to([128, 2, 128])
                    for par, pb in ((0, pe_), (1, po_)):
                        S.op("dve", lambda e, par=par, pb=pb: e.tensor_tensor(out=dst.t[:, par::2, :], in0=pb.t[:, 0:256].rearrange("p (a b) -> p a b", a=2), in1=m2, op=ALU.mult),
                             r=[pb, K], w=[dst])

                xmat(atb, btb, mSL, Xa[0])
                xmat(btb, atb, mSU, Xb[0])
                xmat(ktb, atb, mSU, AakT)
                if full:
                    xmat(btb, rtb, mUI, ArbT)
                    xmat(ktb, rtb, mUI, ArkT)
                ck(5)
                pz = pbank()
                for i in range(4):
                    S.op("pe", lambda e, i=i, pz=pz: e.matmul(pz.t[:, i * 128:i * 128 + 64], lhsT=AakT.t[:, i, :], rhs=Vt.t[:, i * 64:(i + 1) * 64], start=True, stop=True),
                         r=[AakT, Vt], w=[pz])
                S.op("act", lambda e, pz=pz: e.copy(out=Zz[0].t[:, :, 0:64], in_=b4(pz)[:, :, 0:64]), r=[pz], w=[Zz[0]])
                S.op("dve", lambda e: e.tensor_copy(out=Zz[0].t[:, :, 64:128], in_=At.t[:, :].rearrange("p (a b) -> p a b", a=4)), r=[At], w=[Zz[0]])
                lst = []
                if hg + 1 < 4:
                    do_shifts(hg + 1)
                    S.defer = lst
                    E_stage(hg + 1)
                    S.defer = None
                per = (len(lst) + nlev - 1) // nlev
                cur = 0
                for lev in range(nlev):
                    pz = pbank()
                    for i in range(4):
                        S.op("pe", lambda e, i=i, lev=lev, pz=pz, cur=cur: e.matmul(pz.t[:, i * 128:(i + 1) * 128], lhsT=Xb[cur].t[:, i, :], rhs=Zz[lev % 2].t[:, i, :], start=True, stop=False),
                             r=[Xb[cur], Zz[lev % 2]], w=[pz])
                        S.op("pe", lambda e, i=i, lev=lev, pz=pz: e.matmul(pz.t[:, i * 128:(i + 1) * 128], lhsT=idb.t[:], rhs=Zz[lev % 2].t[:, i, :], start=False, stop=True),
                             r=[idb, Zz[lev % 2]], w=[pz])
                    zdst = ZFb if lev == nlev - 1 else Zz[(lev + 1) % 2]
                    S.op("act", lambda e, zdst=zdst, pz=pz: e.copy(out=zdst.t[:], in_=b4(pz)), r=[pz], w=[zdst])
                    if lev < nlev - 1:
                        px, pxt = pbank(), pbank()
                        for i in range(4):
                            S.op("pe", lambda e, i=i, cur=cur, px=px: e.matmul(px.t[:, i * 128:(i + 1) * 128], lhsT=Xb[cur].t[:, i, :], rhs=Xa[cur].t[:, i, :], start=True, stop=True),
                                 r=[Xa[cur], Xb[cur]], w=[px])
                            S.op("pe", lambda e, i=i, cur=cur, pxt=pxt: e.matmul(pxt.t[:, i * 128:(i + 1) * 128], lhsT=Xa[cur].t[:, i, :], rhs=Xb[cur].t[:, i, :], start=True, stop=True),
                                 r=[Xa[cur], Xb[cur]], w=[pxt])
                        S.op("dve", lambda e, cur=cur, px=px: e.tensor_copy(out=Xa[1 - cur].t[:], in_=b4(px)), r=[px], w=[Xa[1 - cur]])
                        S.op("act", lambda e, cur=cur, pxt=pxt: e.copy(out=Xb[1 - cur].t[:], in_=b4(pxt)), r=[pxt], w=[Xb[1 - cur]])
                        cur = 1 - cur
                    S.run_deferred(lst, per)
                    if ac_list and lev in (1, 4):
                        S.run_unit(ac_list)
                S.run_deferred(lst, len(lst))
                if hg == 3:
                    while ac_list:
                        S.run_unit(ac_list)
                ck(6)
                ZF = ZFb
                for i in range(4):
                    h = 4 * hg + i
                    hp, hc, lc = i % 2, h // 2, i // 2
                    prt = slice(hp * 64, hp * 64 + 64)
                    U0 = ZF.t[:, i, 0:64]
                    G = ZF.t[:, i, 64:128]
                    Bh = Bt.t[:, i * 64:(i + 1) * 64]
                    Kh = Kt.t[:, i * 64:(i + 1) * 64]
                    Vh = Vt.t[:, i * 64:(i + 1) * 64]
                    if sample and hp == 0:
                        S.dma("pool", S0c.t[:].rearrange("p a b -> p (a b)"), s0T[:, hc * 1024:(hc + 1) * 1024], w=[S0c])
                        S.op("act", lambda e: e.copy(out=S0cb.t[:], in_=S0c.t[:]), r=[S0c], w=[S0cb])
                    if full:
                        ph = pbank()
                        S.op("pe", lambda e, i=i, G=G, ph=ph, prt=prt: e.matmul(ph.t[prt, 0:128], lhsT=G, rhs=ArbT.t[:, i, :], start=True, stop=True), r=[ZF, ArbT], w=[ph])
                        S.op("dve", lambda e, i=i, ph=ph, prt=prt: e.tensor_tensor(out=HT.t[prt, :], in0=ph.t[prt, 0:128], in1=fm(rt, i), op=ALU.add), r=[ph, rt], w=[HT])
                        ybk = YT[hp]
                        yo = ybk.t[prt, lc * 128:(lc + 1) * 128]
                        S.op("pe", lambda e, i=i, U0=U0, yo=yo: e.matmul(yo, lhsT=U0, rhs=ArbT.t[:, i, :], start=True, stop=False), r=[ZF, ArbT], w=[ybk])
                        S.op("pe", lambda e, i=i, Vh=Vh, yo=yo: e.matmul(yo, lhsT=Vh, rhs=ArkT.t[:, i, :], start=False, stop=False), r=[Vt, ArkT], w=[ybk])
                        if not sample:
                            S.op("pe", lambda e, yo=yo, prt=prt, hc=hc: e.matmul(yo, lhsT=STb.t[prt, hc, :], rhs=HT.t[prt, :], start=False, stop=True), r=[STb, HT], w=[ybk])
                        else:
                            for j in range(16):
                                S.op("pe", lambda e, j=j, prt=prt, hc=hc, ybk=ybk: e.matmul(
                                    ybk.t[prt, lc * 128 + 8 * j:lc * 128 + 8 * j + 8], lhsT=S0cb.t[prt, j, :],
                                    rhs=HT.t[prt, 8 * j:8 * j + 8], start=False, stop=(j == 15)), r=[S0cb, HT], w=[ybk])
                    if not sample:
                        pq = pbank()
                        S.op("pe", lambda e, G=G, Bh=Bh, pq=pq, prt=prt: e.matmul(pq.t[prt, 0:64], lhsT=G, rhs=Bh, start=True, stop=True), r=[ZF, Bt], w=[pq])
                        S.op("act", lambda e, pq=pq, prt=prt: e.copy(out=QTI.t[prt, 0, :], in_=pq.t[prt, 0:64]), r=[pq], w=[QTI])
                        pbk = PS[hp]
                        po = pbk.t[prt, lc * 64:(lc + 1) * 64]
                        S.op("pe", lambda e, po=po, Bh=Bh, U0=U0: e.matmul(po, lhsT=Bh, rhs=U0, start=True, stop=False), r=[Bt, ZF], w=[pbk])
                        S.op("pe", lambda e, po=po, Kh=Kh, Vh=Vh: e.matmul(po, lhsT=Kh, rhs=Vh, start=False, stop=False), r=[Kt, Vt], w=[pbk])
                        S.op("pe", lambda e, po=po, prt=prt, hc=hc: e.matmul(po, lhsT=QTI.t[prt, 0, :], rhs=STb.t[prt, hc, :], start=False, stop=True), r=[QTI, STb], w=[pbk])
                    else:
                        rm = K.t[:, K_RM16:K_RM16 + 16].unsqueeze(2).broadcast_to([128, 16, 64])
                        S.op("dve", lambda e, Bh=Bh, rm=rm: e.tensor_tensor(out=Bm.t[:], in0=Bh.unsqueeze(1).broadcast_to([128, 16, 64]), in1=rm, op=ALU.mult), r=[Bt, K], w=[Bm])
                        S.op("dve", lambda e, Kh=Kh, rm=rm: e.tensor_tensor(out=Km.t[:], in0=Kh.unsqueeze(1).broadcast_to([128, 16, 64]), in1=rm, op=ALU.mult), r=[Kt, K], w=[Km])
                        for j8 in range(2):
                            pq = pbank()
                            for jj in range(8):
                                j = j8 * 8 + jj
                                S.op("pe", lambda e, G=G, j=j, jj=jj, pq=pq, prt=prt: e.matmul(pq.t[prt, jj * 64:(jj + 1) * 64], lhsT=G, rhs=Bm.t[:, j, :], start=True, stop=True), r=[ZF, Bm], w=[pq])
                            S.op("act", lambda e, pq=pq, prt=prt, j8=j8: e.copy(out=QTI.t[prt, j8 * 8:(j8 + 1) * 8, :], in_=pq.t[prt, :].rearrange("p (a b) -> p a b", a=8)),
                                 r=[pq], w=[QTI])
                        for j8 in range(2):
                            po_b = PS[hp]
                            for jj in range(8):
                                j = j8 * 8 + jj
                                po = po_b.t[prt, jj * 64:(jj + 1) * 64]
                                S.op("pe", lambda e, po=po, j=j, U0=U0: e.matmul(po, lhsT=Bm.t[:, j, :], rhs=U0, start=True, stop=False), r=[Bm, ZF], w=[po_b])
                                S.op("pe", lambda e, po=po, j=j, Vh=Vh: e.matmul(po, lhsT=Km.t[:, j, :], rhs=Vh, start=False, stop=False), r=[Km, Vt], w=[po_b])
                                S.op("pe", lambda e, po=po, j=j, prt=prt: e.matmul(po, lhsT=QTI.t[prt, j, :], rhs=S0cb.t[prt, j, :], start=False, stop=True), r=[QTI, S0cb], w=[po_b])
                            gl = gam.t[prt, lc, 7 + 64 * j8:64 * j8 + 64:8].unsqueeze(2).broadcast_to([64, 8, 64])
                            S.op("dve", lambda e, po_b=po_b, prt=prt, j8=j8: e.tensor_tensor(
                                out=S1c.t[prt, j8 * 8:(j8 + 1) * 8, :], in0=po_b.t[prt, :].rearrange("p (a b) -> p a b", a=8), in1=S0c.t[prt, j8 * 8:(j8 + 1) * 8, :], op=ALU.add),
                                 r=[po_b, S0c], w=[S1c])
                            S.op("dve", lambda e, prt=prt, j8=j8, gl=gl: e.tensor_tensor(
                                out=S1c.t[prt, j8 * 8:(j8 + 1) * 8, :], in0=S1c.t[prt, j8 * 8:(j8 + 1) * 8, :], in1=gl, op=ALU.mult), r=[S1c, gam], w=[S1c])
                        if hp == 1:
                            S.dma("pool", o_swkv[:, hc * 1024:(hc + 1) * 1024], S1c.t[:].rearrange("p a b -> p (a b)"), r=[S1c], is_out=True)
                ck(7)
                if not sample:
                    for hp_ in range(2):
                        pr_ = slice(hp_ * 64, hp_ * 64 + 64)
                        gl = gam.t[pr_, :, 127:128].broadcast_to([64, 2, 64])
                        S.op("dve", lambda e, hp_=hp_, pr_=pr_: e.tensor_tensor(out=ST.t[pr_, g0:g0 + 2, :], in0=PS[hp_].t[pr_, 0:128].rearrange("p (a b) -> p a b", a=2),
                                                                                in1=ST.t[pr_, g0:g0 + 2, :], op=ALU.add), r=[PS[hp_], ST], w=[ST])
                        S.op("dve", lambda e, gl=gl, pr_=pr_: e.tensor_tensor(out=ST.t[pr_, g0:g0 + 2, :], in0=ST.t[pr_, g0:g0 + 2, :], in1=gl, op=ALU.mult), r=[ST, gam], w=[ST])
                        S.op("act", lambda e, pr_=pr_: e.copy(out=STb.t[pr_, g0:g0 + 2, :], in_=ST.t[pr_, g0:g0 + 2, :]), r=[ST], w=[STb])
                if not full:
                    continue
                yT, cen, sq, rs_ = sgw, cum, igam, gprev
                for hp_ in range(2):
                    S.op("act", lambda e, hp_=hp_: e.copy(out=fl(yT)[hp_ * 64:hp_ * 64 + 64, :], in_=YT[hp_].t[hp_ * 64:hp_ * 64 + 64, 0:256]), r=[YT[hp_]], w=[yT])
                pm = pbank()
                for cc in range(2):
                    S.op("pe", lambda e, cc=cc, pm=pm: e.matmul(pm.t[:, cc * 128:(cc + 1) * 128], lhsT=K.t[:, K_B64:K_B64 + 128], rhs=yT.t[:, cc, :], start=True, stop=True), r=[K, yT], w=[pm])
                S.op("dve", lambda e, pm=pm: e.scalar_tensor_tensor(out=fl(cen), in0=pm.t[:, 0:256], scalar=-1.0 / 64, in1=fl(yT), op0=ALU.mult, op1=ALU.add), r=[pm, yT], w=[cen])
                S.op("dve", lambda e: e.tensor_tensor(out=sq.t[:], in0=cen.t[:], in1=cen.t[:], op=ALU.mult), r=[cen], w=[sq])
                pv2 = pbank()
                for cc in range(2):
                    S.op("pe", lambda e, cc=cc, pv2=pv2: e.matmul(pv2.t[:, cc * 128:(cc + 1) * 128], lhsT=K.t[:, K_B64:K_B64 + 128], rhs=sq.t[:, cc, :], start=True, stop=True), r=[K, sq], w=[pv2])
                S.op("act", lambda e, pv2=pv2: e.activation(out=fl(rs_), in_=pv2.t[:, 0:256], func=AF.Sqrt, scale=1.0 / 64, bias=64e-5), r=[pv2], w=[rs_])
                S.op("dve", lambda e: e.reciprocal(out=rs_.t[:], in_=rs_.t[:]), r=[rs_], w=[rs_])
                S.op("dve", lambda e: e.tensor_tensor(out=cen.t[:], in0=cen.t[:], in1=rs_.t[:], op=ALU.mult), r=[cen, rs_], w=[cen])
                S.op("dve", lambda e: e.tensor_tensor(out=cen.t[:], in0=cen.t[:], in1=bc2(CP_LG), op=ALU.mult), r=[cen, cp], w=[cen])
                S.op("dve", lambda e: e.tensor_tensor(out=cen.t[:], in0=cen.t[:], in1=bc2(CP_LB), op=ALU.add), r=[cen, cp], w=[cen])
                S.op("dve", lambda e: e.tensor_tensor(out=cen.t[:], in0=cen.t[:], in1=bonus.t[:], op=ALU.add), r=[cen, bonus], w=[cen])
                S.op("dve", lambda e: e.tensor_tensor(out=mixT.t[:, 4 + g0:4 + g0 + 2, :], in0=cen.t[:], in1=gT.t[:], op=ALU.mult), r=[cen, gT], w=[mixT])
            if last_prompt:
                S.dma("pool", o_pwkv[:, :], ST.t[:].rearrange("p a b -> p (a b)"), r=[ST], is_out=True)
            if not full:
                return
            S.barrier()

            for g in range(4):
                slv = next_slab()
                sl, v = slv
                for kc in range(16):
                    S.op("pe", lambda e, kc=kc, g=g, v=v: e.matmul(banks[g].t[:], lhsT=mixT.t[:, kc, :], rhs=v[:, kc, :], start=(kc == 0), stop=(kc == 15)), r=[mixT, sl], w=[banks[g]])
            post_norm_residual(R_GMIXP)
            if STOP_AFTER_MIX:
                return
            if sample:
                S.dma("pool", scar.t[:].rearrange("p a b c -> p (a b c)"), cvT[:, :], w=[scar])
            rmsnorm_to_hT(xt, CP_GFFN)
            def ffn_up(q):
                sg_, vg = next_slab()
                sv_, vv = next_slab()
                pg_, pv_ = pbank(), pbank()
                for (pb, sl_, vw) in ((pg_, sg_, vg), (pv_, sv_, vv)):
                    for i in range(4):
                        for kc in range(16):
                            S.op("pe", lambda e, i=i, kc=kc, pb=pb, vw=vw: e.matmul(pb.t[:, i * 128:(i + 1) * 128], lhsT=vw[:, kc, i * 128:(i + 1) * 128], rhs=hT.t[:, kc, :],
                                                                                 start=(kc == 0), stop=(kc == 15)), r=[sl_, hT], w=[pb])
                ex, ac = ext8[q % 2], acc8[q % 2]
                W = nseq * (L + 2)
                e4 = ex.t[:, :, 0:W].rearrange("p c (j l) -> p c j l", j=nseq)
                for side, pb in ((0, pg_), (1, pv_)):
                    c0 = 4 * q + side * NFC
                    es = e4[:, 4 * side:4 * side + 4]
                    p4 = pb.t[:].rearrange("p (c j l) -> p c j l", c=4, j=nseq)
                    if sample:
                        S.op("dve", lambda e, es=es, c0=c0: e.tensor_copy(out=es[:, :, :, 0:2], in_=scar.t[:, c0:c0 + 4, :, :]), r=[scar], w=[ex])
                    else:
                        S.op("dve", lambda e, es=es, c0=c0: e.tensor_copy(out=es[:, :, 0, 0:2], in_=ccar.t[:, c0:c0 + 4, :]), r=[ccar], w=[ex])
                    S.op("act", lambda e, es=es, p4=p4: e.copy(out=es[:, :, :, 2:L + 2], in_=p4), r=[pb], w=[ex])
                    if sample:
                        S.op("dve", lambda e, es=es, c0=c0: e.tensor_copy(out=scar.t[:, c0:c0 + 4, :, :], in_=es[:, :, :, L:L + 2]), r=[ex], w=[scar])
                    else:
                        S.op("dve", lambda e, es=es, c0=c0: e.tensor_copy(out=ccar.t[:, c0:c0 + 4, :], in_=es[:, :, 0, L:L + 2]), r=[ex], w=[ccar])
                    for i in range(4):
                        cidx = c0 + i
                        e3 = e4[:, 4 * side + i]
                        a3 = ac.t[:, 4 * side + i, :].rearrange("p (j l) -> p j l", j=nseq)
                        S.op("dve", lambda e, e3=e3, a3=a3, cidx=cidx: e.tensor_scalar(out=a3, in0=e3[:, :, 0:L], scalar1=cpc(CP_CW0 + cidx), scalar2=cpc(CP_CB + cidx),
                                                                                   op0=ALU.mult, op1=ALU.add), r=[ex, cp], w=[ac])
                        S.op("dve", lambda e, e3=e3, a3=a3, cidx=cidx: e.scalar_tensor_tensor(out=a3, in0=e3[:, :, 1:L + 1], scalar=cpc(CP_CW1 + cidx), in1=a3,
                                                                                          op0=ALU.mult, op1=ALU.add), r=[ex, cp, ac], w=[ac])
                        S.op("dve", lambda e, e3=e3, a3=a3, cidx=cidx: e.scalar_tensor_tensor(out=a3, in0=e3[:, :, 2:L + 2], scalar=cpc(CP_CW2 + cidx), in1=a3,
                                                                                          op0=ALU.mult, op1=ALU.add), r=[ex, cp, ac], w=[ac])
                S.op("act", lambda e, ac=ac: e.activation(out=ac.t[:, 0:4, :], in_=ac.t[:, 0:4, :], func=AF.Gelu_apprx_tanh), r=[ac], w=[ac])
                S.op("dve", lambda e, ac=ac: e.tensor_tensor(out=actT.t[:, 4 * q:4 * q + 4, :], in0=ac.t[:, 0:4, :], in1=ac.t[:, 4:8, :], op=ALU.mult), r=[ac], w=[actQ[q]])

            def ffn_down(q):
                sd_, vd = next_slab()
                for i in range(4):
                    c = 4 * q + i
                    for g in range(4):
                        S.op("pe", lambda e, i=i, g=g, c=c, vd=vd: e.matmul(banks[g].t[:], lhsT=actT.t[:, c, :], rhs=vd[:, i, g * 512:(g + 1) * 512], start=(c == 0), stop=(c == NFC - 1)),
                             r=[actQ[q], sd_], w=[banks[g]])

            for q in range(12):
                if q < 11:
                    ffn_up(q)
                if q >= 1:
                    ffn_down(q - 1)
            post_norm_residual(R_GFFNP)
            if sample:
                S.dma("pool", o_ys[:, :], xt.t[:], r=[xt], is_out=True)
                S.dma("pool", o_sconv[:, :], scar.t[:].rearrange("p a b c -> p (a b c)"), r=[scar], is_out=True)
            elif ti == 7:
                S.op("dve", lambda e: e.tensor_scalar(out=ccar.t[:].rearrange("p a b -> p (a b)"), in0=ccar.t[:].rearrange("p a b -> p (a b)"),
                                                      scalar1=cfl.t[:, 0:1], scalar2=None, op0=ALU.mult), r=[ccar, cfl], w=[ccar])
            elif ti >= 8:
                S.dma("pool", o_yp[(ti - 8) * 128:(ti - 7) * 128, :], xt.t[:], r=[xt], is_out=True)
                if last_prompt:
                    S.dma("pool", o_pconv[:, :], ccar.t[:].rearrange("p a b -> p (a b)"), r=[ccar], is_out=True)
            S.barrier()

        def post_norm_residual(gofs):
            S.dma("pool", gpost.t[:], rep[:, gofs:gofs + 2048], w=[gpost])
            for g in range(4):
                S.op("act", lambda e, g=g: e.activation(out=xn.t[:, g * 512:(g + 1) * 512], in_=banks[g].t[:], func=AF.Square, accum_out=ss4.t[:, g:g + 1]),
                     r=[banks[g]], w=[xn, ss4])
            S.op("dve", lambda e: e.reduce_sum(out=st6.t[:, 0:1], in_=ss4.t[:], axis=AX.X), r=[ss4], w=[st6])
            S.op("act", lambda e: e.activation(out=st6.t[:, 1:2], in_=st6.t[:, 0:1], func=AF.Sqrt, scale=1.0 / D, bias=1e-6), r=[st6], w=[st6])
            S.op("dve", lambda e: e.reciprocal(out=rstd.t[:], in_=st6.t[:, 1:2]), r=[st6], w=[rstd])
            for g in range(4):
                S.op("dve", lambda e, g=g: e.scalar_tensor_tensor(out=xn.t[:, g * 512:(g + 1) * 512], in0=banks[g].t[:], scalar=rstd.t[:, 0:1],
                                                                  in1=gpost.t[:, g * 512:(g + 1) * 512], op0=ALU.mult, op1=ALU.mult), r=[banks[g], rstd, gpost], w=[xn])
            S.op("dve", lambda e: e.tensor_tensor(out=xt.t[:], in0=xt.t[:], in1=xn.t[:], op=ALU.add), r=[xt, xn], w=[xt])

        for ti, kd in enumerate(kinds):
          try:
            do_tile(16 - len([k for k in kinds if k != "sample"]) + ti if kd != "sample" else 16, kd)
          except _Stop:
            break
        S.finish()
    return nc


_NC = None
_PREP_ONLY = False


def _consts():
    c = np.zeros((128, NCONST), np.float32)
    i = np.arange(128)
    P, Fd = i[:, None], i[None, :]
    c[:, K_ID:K_ID + 128] = (P == Fd)
    c[:, K_SL:K_SL + 128] = (Fd < P)
    c[:, K_SU:K_SU + 128] = (P < Fd)
    c[:, K_UI:K_UI + 128] = (P <= Fd)
    same = (P // 8 == Fd // 8)
    c[:, K_BSL:K_BSL + 128] = (Fd < P) & same
    c[:, K_BSU:K_BSU + 128] = (P < Fd) & same
    c[:, K_BUI:K_BUI + 128] = (P <= Fd) & same
    c[:, K_B64:K_B64 + 128] = (P // 64 == Fd // 64)
    c[:, K_ONES:K_ONES + 128] = 1.0
    c[:, K_RM16:K_RM16 + 16] = (P // 8 == np.arange(16)[None, :])
    c[:, K_I64:K_I64 + 64] = ((P % 64) == np.arange(64)[None, :])
    col = np.arange(256)
    c[:, K_RSTP:K_RSTP + 256] = (col % 128 != 0)[None, :]
    c[:, K_RSTS:K_RSTS + 256] = (col % 8 != 0)[None, :]
    return c


def _colmajor(v, nchunk):
    out = np.zeros(nchunk * 128, np.float32)
    out[:v.size] = v.reshape(-1)
    return out.reshape(nchunk, 128).T


def kernel(x_prompt, x_sample, mem_prompt, cache_mem_k, cache_mem_v, state_shift, state_wkv, state_conv,
           norm_mix_pre, norm_mix_post, norm_ffn_pre, norm_ffn_post, norm_mem, w_in, w_out, w_mem_k, w_mem_v,
           gm_ln_g, gm_ln_b, gm_ws, gm_bs, rk_mu, rk_w0, rk_w2, rk_a0, rk_a2, rk_g2, rk_kk, rk_ka, rk_rk,
           rk_lnx_g, rk_lnx_b, ffn_w_up, ffn_conv_w, ffn_conv_b, ffn_w_down):
    global _NC
    f = lambda a: np.ascontiguousarray(np.asarray(a, dtype=np.float32))
    x_prompt, x_sample, mem_prompt = f(x_prompt), f(x_sample), f(mem_prompt)
    cache_mem_k, cache_mem_v = f(cache_mem_k)[0], f(cache_mem_v)[0]
    state_shift, state_wkv, state_conv = f(state_shift)[0], f(state_wkv)[0], f(state_conv)[0]
    cpar = np.zeros((128, NCP), np.float32)
    cpar[:, CP_MU:CP_MU + 26] = _colmajor(f(rk_mu)[0], 26)
    for off, v in ((CP_W0, rk_w0), (CP_A0, rk_a0), (CP_KK, rk_kk), (CP_KA, rk_ka), (CP_RK, rk_rk), (CP_LG, rk_lnx_g), (CP_LB, rk_lnx_b)):
        cpar[:, off:off + 8] = _colmajor(f(v)[0], 8)
    for off, v in ((CP_GMIX, norm_mix_pre), (CP_GFFN, norm_ffn_pre), (CP_GMEM, norm_mem)):
        cpar[:, off:off + 16] = _colmajor(f(v)[0], 16)
    cw = f(ffn_conv_w)[0]
    for i, off in enumerate((CP_CW0, CP_CW1, CP_CW2)):
        cpar[:, off:off + 88] = _colmajor(cw[i], 88)
    cpar[:, CP_CB:CP_CB + 88] = _colmajor(f(ffn_conv_b)[0], 88)
    rep = np.zeros((128, NREP), np.float32)
    rep[:, 0:2048] = f(norm_mix_post)[0][None, :]
    rep[:, 2048:4096] = f(norm_ffn_post)[0][None, :]
    rep[:, 4096:4608] = f(gm_ln_g)[0][None, :]
    rep[:, 4608:5120] = f(gm_ln_b)[0][None, :]
    lora = np.zeros((128, 2048), np.float32)
    lora[0:64, 0:1024] = f(rk_w2)[0]
    lora[64:128, 0:1024] = f(rk_a2)[0]
    lora[0:64, 1024:2048] = f(rk_g2)[0]
    ws = f(gm_ws)[0]
    wmTp = np.ascontiguousarray(ws.transpose(2, 0, 1)).reshape(128, 512)
    wmTs = np.zeros((128, 4, 128), np.float32)
    for j in range(16):
        wmTs[8 * j:8 * j + 8, :, 8 * j:8 * j + 8] = ws[:, 0:8, 0:8].transpose(2, 0, 1)
    wmTs = wmTs.reshape(128, 512)
    bs = f(gm_bs)[0]
    bsp = np.ascontiguousarray(bs.T)
    bss = np.ascontiguousarray(np.tile(bs[:, 0:8].T, (16, 1)))
    shared = dict(w_in=f(w_in)[0], w_out=f(w_out)[0], w_mk=f(w_mem_k)[0], w_mv=f(w_mem_v)[0], w_up=f(ffn_w_up)[0], w_dn=f(ffn_w_down)[0],
                  cpar=cpar, cst=_consts(), rep=rep, lora=lora, wmTp=wmTp, wmTs=wmTs, bsp=bsp, bss=bss)
    in_maps = []
    for c in range(8):
        b, half = c // 2, c % 2
        xw = np.zeros((2048, D), np.float32)
        if half == 0:
            xw[1024:] = x_prompt[b, 0:1024]
        else:
            xw[:] = x_prompt[b]
        sq = slice(16 * c, 16 * c + 16)
        shT = np.zeros((26 * 128, 16), np.float32)
        shT[:3264] = state_shift[sq].T
        shT = np.ascontiguousarray(shT.reshape(26, 128, 16).transpose(1, 0, 2)).reshape(128, 26 * 16)
        sw = state_wkv[sq].reshape(16, 8, 2, 64, 64)
        s0T = np.ascontiguousarray(sw.transpose(2, 4, 1, 0, 3)).reshape(128, 8 * 16 * 64)
        cvT = np.ascontiguousarray(state_conv[sq].reshape(16, 2, 88, 128).transpose(3, 2, 0, 1)).reshape(128, 88 * 32)
        ckT = np.ascontiguousarray(cache_mem_k[sq].transpose(0, 3, 2, 1)).reshape(16, 128, 1024)
        cv = np.ascontiguousarray(cache_mem_v[sq]).reshape(16, 256, 512)
        m = dict(shared)
        m.update(cflag=np.full((128, 1), float(half), np.float32), xwin=xw, xs=np.ascontiguousarray(x_sample[sq]).reshape(128, D), mem=mem_prompt[b], ckT=ckT, cv=cv, shT=shT, s0T=s0T, cvT=cvT)
        in_maps.append(m)
    if _PREP_ONLY:
        return in_maps
    if _NC is None:
        _NC = build()
    res = run_bass_kernel_spmd(_NC, in_maps, core_ids=list(range(8)))
    R_ = res.results
    y_p = np.zeros((4, 2048, D), np.float32)
    y_s = np.zeros((128, 8, D), np.float32)
    p_mk = np.zeros((1, 4, 256, 4, 128), np.float32); p_mv = np.zeros_like(p_mk)
    p_cv = np.zeros((1, 4, 128, 4, 128), np.float32)
    p_sh = np.zeros((1, 4, 3264), np.float32)
    p_wkv = np.zeros((1, 4, 16, 64, 64), np.float32)
    p_conv = np.zeros((1, 4, 2, 2 * DFF), np.float32)
    s_cv = np.zeros((1, 128, 8, 4, 128), np.float32)
    s_sh = np.zeros((1, 128, 3264), np.float32)
    s_wkv = np.zeros((1, 128, 16, 64, 64), np.float32)
    s_conv = np.zeros((1, 128, 2, 2 * DFF), np.float32)

    def wkv_back(a, nj):
        a = a.reshape(2, 64, 8, nj, 64)
        return a.transpose(3, 2, 0, 4, 1).reshape(nj, 16, 64, 64)

    for c in range(8):
        b, half = c // 2, c % 2
        r = R_[c]
        y_p[b, half * 1024:(half + 1) * 1024] = r["o_yp"]
        sq = slice(16 * c, 16 * c + 16)
        y_s[sq] = r["o_ys"].reshape(16, 8, D)
        if half == 0:
            p_mk[0, b] = r["o_mk"].reshape(256, 4, 128)
            p_mv[0, b] = r["o_mv"].reshape(256, 4, 128)
        else:
            p_cv[0, b] = r["o_pcv"].reshape(128, 4, 128)
            p_sh[0, b] = r["o_psh"][0]
            p_wkv[0, b] = wkv_back(r["o_pwkv"], 1)[0]
            p_conv[0, b] = r["o_pconv"].reshape(128, 88, 2).transpose(2, 1, 0).reshape(2, 2 * DFF)
        s_cv[0, sq] = r["o_scv"].reshape(16, 8, 4, 128)
        s_sh[0, sq] = r["o_ssh"]
        s_wkv[0, sq] = wkv_back(r["o_swkv"], 16)
        s_conv[0, sq] = r["o_sconv"].reshape(128, 88, 16, 2).transpose(2, 3, 1, 0).reshape(16, 2, 2 * DFF)
    return (y_p, y_s, p_mk, p_mv, p_cv, p_sh, p_wkv, p_conv, s_cv, s_sh, s_wkv, s_conv)
```

```python
import contextlib
import numpy as np
import concourse.bass as bass
import concourse.mybir as mybir
from concourse.bass_utils import run_bass_kernel_spmd

F32 = mybir.dt.float32
BF16 = mybir.dt.bfloat16
AF = mybir.ActivationFunctionType
ALU = mybir.AluOpType
AX = mybir.AxisListType

D = 2048
DFF = 5632
NFC = 44
INC = 4800
C0 = 0.6065306597126334
CG = [(0, 512), (512, 512), (1024, 512), (1536, 512), (2048, 512), (2560, 512), (3072, 512), (3584, 512),
      (4096, 192), (4288, 512)]
CP_MU, CP_W0, CP_A0, CP_KK, CP_KA, CP_RK, CP_LG, CP_LB = 0, 26, 34, 42, 50, 58, 66, 74
CP_GMIX, CP_GFFN, CP_GMEM = 82, 98, 114
CP_CW0, CP_CW1, CP_CW2, CP_CB = 130, 218, 306, 394
NCP = 482
K_ID, K_SL, K_SU, K_UI, K_BSL, K_BSU, K_BUI, K_B64, K_ONES = [i * 128 for i in range(9)]
K_RM16 = 9 * 128
K_I64 = K_RM16 + 16
K_RSTP = K_I64 + 64
K_RSTS = K_RSTP + 256
NCONST = K_RSTS + 256
R_GMIXP, R_GFFNP, R_LNG, R_LNB = 0, 2048, 0, 512
NREP = 5120


class Buf:
    __slots__ = ("name", "w", "rs")

    def __init__(self, name):
        self.name = name
        self.w = None
        self.rs = []


class TB:
    def __init__(self, t, name):
        self.t = t
        self.b = Buf(name)


class Sched:
    EPOCH = 3500
    ND = 24

    def __init__(self, nc, es):
        self.nc = nc
        self.es = es
        self.eng = {"pe": nc.tensor, "act": nc.scalar, "dve": nc.vector, "pool": nc.gpsimd, "sp": nc.sync}
        self.cnt = {e: 0 for e in self.eng}
        self.sems = {e: [] for e in self.eng}
        self.seen = {e: {} for e in self.eng}
        self.dsem = [es.enter_context(nc.semaphore(f"dma{i}")) for i in range(self.ND)]
        self.dval = [0] * self.ND
        self.NDP = 6
        self.dnext = {True: 0, False: self.NDP}
        self.out_tokens = []
        self.defer = None

    def _sem(self, e, ep):
        while len(self.sems[e]) <= ep:
            self.sems[e].append(self.es.enter_context(self.nc.semaphore(f"s_{e}_{len(self.sems[e])}")))
        return self.sems[e][ep]

    def _wait(self, e, sem, val):
        key = id(sem)
        if self.seen[e].get(key, 0) >= val:
            return
        self.eng[e].wait_ge(sem, val)
        self.seen[e][key] = val

    def _deps(self, e, r, w):
        toks = []
        for b in r:
            if b.w is not None:
                toks.append(b.w)
        for b in w:
            if b.w is not None:
                toks.append(b.w)
            toks.extend(b.rs)
        mx = {}
        for (sem, val, src) in toks:
            if src == e and (e == "pe" or not SAME_ENGINE_SYNC):
                continue
            k = id(sem)
            if k not in mx or mx[k][1] < val:
                mx[k] = (sem, val)
        for (sem, val) in mx.values():
            self._wait(e, sem, val)

    def _mark(self, tok, r, w):
        for b in r:
            b.rs.append(tok)
            if len(b.rs) > 64:
                b.rs = b.rs[-64:] if False else b.rs
        for b in w:
            b.w = tok
            b.rs = []

    def mark(self):
        if self.defer is not None:
            self.defer.append(("mark", (), {}))

    def run_unit(self, lst):
        while lst:
            kind, args, kw = lst.pop(0)
            if kind == "mark":
                return
            (self.op if kind == "op" else self.dma)(*args, **kw)

    def run_deferred(self, lst, n):
        for _ in range(min(n, len(lst))):
            kind, args, kw = lst.pop(0)
            (self.op if kind == "op" else self.dma)(*args, **kw)

    def op(self, e, fn, r=(), w=()):
        if self.defer is not None:
            self.defer.append(("op", (e, fn), dict(r=list(r), w=list(w))))
            return
        r = [x.b if isinstance(x, TB) else x for x in r]
        w = [x.b if isinstance(x, TB) else x for x in w]
        self._deps(e, r, w)
        ins = fn(self.eng[e])
        c = self.cnt[e]
        ep, v = divmod(c, self.EPOCH)
        sem = self._sem(e, ep)
        ins.then_inc(sem, 1)
        self.cnt[e] = c + 1
        self._mark((sem, v + 1, e), r, w)

    def dma(self, q, out, in_, r=(), w=(), is_out=False):
        if self.defer is not None:
            self.defer.append(("dma", (q, out, in_), dict(r=list(r), w=list(w), is_out=is_out)))
            return
        r = [x.b if isinstance(x, TB) else x for x in r]
        w = [x.b if isinstance(x, TB) else x for x in w]
        self._deps(q, r, w)
        sw = (q == "pool")
        i = self.dnext[sw]
        self.dnext[sw] = (i + 1) % self.NDP if sw else self.NDP + (i + 1 - self.NDP) % (self.ND - self.NDP)
        sem = self.dsem[i]
        if self.dval[i] > 0:
            self._wait(q, sem, self.dval[i])
        self.dval[i] += 16
        self.eng[q].dma_start(out=out, in_=in_).then_inc(sem, 16)
        tok = (sem, self.dval[i], "dma")
        self._mark(tok, r, w)
        if is_out:
            self.out_tokens.append(tok)

    def barrier(self):
        toks = []
        for f in ("pe", "act", "dve", "pool"):
            c = self.cnt[f]
            if c > 0:
                ep, v = divmod(c - 1, self.EPOCH)
                toks.append((self._sem(f, ep), v + 1))
        for i in range(self.ND):
            if self.dval[i] > 0:
                toks.append((self.dsem[i], self.dval[i]))
        for e in ("pe", "act", "dve", "pool", "sp"):
            for (sem, val) in toks:
                self._wait(e, sem, val)

    def finish(self):
        for i in range(self.ND):
            if self.dval[i] > 0:
                self._wait("sp", self.dsem[i], self.dval[i])


KINDS = ["pre"] * 7 + ["full"] * 9 + ["sample"]
DO_MEM = True
STOP_AFTER_MIX = False
STOP_AT = 0
SAME_ENGINE_SYNC = True


class _Stop(Exception):
    pass


def ck(n):
    if STOP_AT == n:
        raise _Stop()


def build():
    nc = bass.Bass("TRN2", target_bir_lowering=False)

    def din(name, shape):
        return nc.dram_tensor(name, list(shape), F32, kind="ExternalInput").ap()

    def dout(name, shape):
        return nc.dram_tensor(name, list(shape), F32, kind="ExternalOutput").ap()

    xwin = din("xwin", [2048, D]); xsm = din("xs", [128, D]); memx = din("mem", [256, D])
    ckT = din("ckT", [16, 128, 1024]); cvv = din("cv", [16, 256, 512])
    shT = din("shT", [128, 26 * 16]); s0T = din("s0T", [128, 8 * 16 * 64]); cvT = din("cvT", [128, 88 * 32])
    w_in = din("w_in", [D, INC]); w_out = din("w_out", [D, D])
    w_mk = din("w_mk", [D, 512]); w_mv = din("w_mv", [D, 512])
    w_up = din("w_up", [D, 2 * DFF]); w_dn = din("w_dn", [DFF, D])
    cpar = din("cpar", [128, NCP]); cst = din("cst", [128, NCONST]); rep = din("rep", [128, NREP])
    lora = din("lora", [128, 2048]); wmTp = din("wmTp", [128, 512]); wmTs = din("wmTs", [128, 512])
    bsp = din("bsp", [128, 4]); bss = din("bss", [128, 4]); cflag = din("cflag", [128, 1])

    def dbf(name, shape):
        return nc.dram_tensor(name, list(shape), BF16, kind="Internal").ap()

    wb = {"w_in": dbf("wb_in", [10, 128, 8192]), "w_out": dbf("wb_out", [4, 128, 8192]), "w_mk": dbf("wb_mk", [1, 128, 8192]), "w_mv": dbf("wb_mv", [1, 128, 8192]),
          "w_up": dbf("wb_up", [22, 128, 8192]), "w_dn": dbf("wb_dn", [11, 128, 8192])}

    def slab_view(key):
        name, k0, k1, c0, c1 = key
        if name == "w_in":
            idx = [g for g, (cs, n_) in enumerate(CG) if cs == c0][0]
        elif name == "w_dn":
            idx = k0 // 4
        else:
            idx = c0 // 512
        a_, b_ = k1 - k0, c1 - c0
        return wb[name][idx][:, 0:a_ * b_].rearrange("p (a b) -> p a b", a=a_)

    wf = {"w_in": w_in, "w_out": w_out, "w_mk": w_mk, "w_mv": w_mv, "w_up": w_up, "w_dn": w_dn}

    o_yp = dout("o_yp", [1024, D]); o_ys = dout("o_ys", [128, D])
    o_mk = dout("o_mk", [256, 512]); o_mv = dout("o_mv", [256, 512])
    o_pcv = dout("o_pcv", [128, 512]); o_psh = dout("o_psh", [1, 3264])
    o_pwkv = dout("o_pwkv", [128, 512]); o_pconv = dout("o_pconv", [128, 88 * 2])
    o_scv = dout("o_scv", [128, 512]); o_ssh = dout("o_ssh", [16, 3264])
    o_swkv = dout("o_swkv", [128, 8 * 16 * 64]); o_sconv = dout("o_sconv", [128, 88 * 32])

    es = contextlib.ExitStack()
    with es:
        S = Sched(nc, es)

        def sb(name, shape, dt=F32):
            return TB(es.enter_context(nc.sbuf_tensor(name, list(shape), dt)), name)

        banks = [TB(es.enter_context(nc.psum_tensor(f"pb{i}", [128, 512], F32)), f"pb{i}") for i in range(8)]
        rr = [0]

        def pbank():
            b = banks[4 + rr[0] % 4]
            rr[0] += 1
            return b

        cp = sb("cp", [128, NCP]); K = sb("K", [128, NCONST]); RP = sb("RP", [128, 1024])
        lo = sb("lo", [128, 2048]); wmp = sb("wmp", [128, 512]); wms = sb("wms", [128, 512])
        bsP = sb("bsP", [128, 4]); bsS = sb("bsS", [128, 4]); omka = sb("omka", [128, 8]); cfl = sb("cfl", [128, 1])
        S.dma("pool", cfl.t[:], cflag[:, :], w=[cfl])
        for (t_, d_) in ((cp, cpar), (K, cst), (RP, rep[:, 4096:5120]), (lo, lora), (wmp, wmTp), (wms, wmTs), (bsP, bsp), (bsS, bss)):
            S.dma("pool", t_.t[:], d_ if t_ is RP else d_[:, :], w=[t_])
        ident = K.t[:, K_ID:K_ID + 128]
        S.op("dve", lambda e: e.tensor_scalar(out=omka.t[:], in0=cp.t[:, CP_KA:CP_KA + 8], scalar1=-1.0, scalar2=1.0,
                                              op0=ALU.mult, op1=ALU.add), r=[cp], w=[omka])
        for (wm_, mo) in ((wmp, K_UI), (wms, K_BUI)):
            m4 = K.t[:, mo:mo + 128].unsqueeze(1).broadcast_to([128, 4, 128])
            v4 = wm_.t[:].rearrange("p (h t) -> p h t", h=4)
            S.op("dve", lambda e, v4=v4, m4=m4: e.tensor_tensor(out=v4, in0=v4, in1=m4, op=ALU.mult), r=[K, wm_], w=[wm_])

        NSL = 3
        slabs = [sb(f"slab{i}", [128, 8192], BF16) for i in range(NSL)]
        plan = []
        issued = [0]
        used = [0]

        def plan_tile(kind, ti=-1):
            if kind == "mem":
                return
            groups = range(4, 9) if kind == "pre" else range(10)
            for g in groups:
                c0, n = CG[g]
                plan.append(("w_in", 0, 16, c0, c0 + n))
            if kind == "pre":
                return
            for g in range(4):
                plan.append(("w_out", 0, 16, g * 512, (g + 1) * 512))
            for q in range(12):
                if q < 11:
                    plan.append(("w_up", 0, 16, q * 512, (q + 1) * 512))
                    plan.append(("w_up", 0, 16, DFF + q * 512, DFF + (q + 1) * 512))
                if q >= 1 and not (kind == "full" and ti == 7):
                    plan.append(("w_dn", 4 * (q - 1), 4 * (q - 1) + 4, 0, D))

        conv_buf = {}

        conv_pos = [0]

        def ensure_conv(upto):
            while conv_pos[0] < min(upto, len(plan)):
                key = plan[conv_pos[0]]
                conv_pos[0] += 1
                if key in conv_buf:
                    continue
                name, k0, k1, c0, c1 = key
                cb = Buf("cv_" + name + str(key[1:]))
                conv_buf[key] = cb
                src = wf[name].rearrange("(k p) c -> p k c", p=128)[:, k0:k1, c0:c1]
                dst = slab_view(key)
                S.dma("pool", dst, src, w=[cb])

        def next_slab():
            i = used[0]
            while issued[0] < len(plan) and issued[0] < i + NSL - 1:
                j = issued[0]
                ensure_conv(j + 4)
                name, k0, k1, c0, c1 = plan[j]
                src = slab_view(plan[j])
                a, b_ = k1 - k0, c1 - c0
                sl = slabs[j % NSL]
                dst = sl.t[:, 0:a * b_].rearrange("p (a b) -> p a b", a=a)
                S.dma("sp", dst, src, r=[conv_buf[plan[j]]], w=[sl])
                issued[0] += 1
            used[0] += 1
            sl = slabs[i % NSL]
            name, k0, k1, c0, c1 = plan[i]
            a, b_ = k1 - k0, c1 - c0
            return sl, sl.t[:, 0:a * b_].rearrange("p (a b) -> p a b", a=a)

        xt = sb("xt", [128, D]); hT = sb("hT", [128, 16, 128], BF16)
        proj = sb("proj", [128, 3776]); mixT = sb("mixT", [128, 16, 128], BF16)
        xn = TB(proj.t[:, 0:2048], "xn"); xn.b = proj.b
        st6 = sb("st6", [128, 8]); rstd = sb("rstd", [128, 1]); ss4 = sb("ss4", [128, 4])
        lj = sb("lj", [128, 256])
        kTp = sb("kTp", [128, 4, 256], BF16); Vp = sb("Vp", [128, 2, 512], BF16); onesb = sb("onesb", [128, 128], BF16)
        carry = sb("carry", [128, 26]); shTs = sb("shTs", [128, 26, 16])
        ST = sb("ST", [128, 8, 64]); ccar = sb("ccar", [128, 88, 2]); STb = sb("STb", [128, 8, 64], BF16)
        arena = es.enter_context(nc.sbuf_tensor("arena", [128, 22272], F32))
        apos = [0]

        def carve(name, shape, dt=F32):
            n = 1
            for d_ in shape[1:]:
                n *= d_
            words = n if dt == F32 else (n + 1) // 2
            ap = arena[:, apos[0]:apos[0] + words]
            apos[0] += words
            assert apos[0] <= 22272, (name, apos[0])
            if dt != F32:
                ap = ap.bitcast(dt)
            if len(shape) == 3:
                ap = ap.rearrange("p (a b) -> p a b", a=shape[1])
            elif len(shape) == 4:
                ap = ap.rearrange("p (a b c) -> p a b c", a=shape[1], b=shape[2])
            return TB(ap, name)

        class V:
            pass

        qT = carve("qT", [128, 4, 128], BF16); PT = carve("PT", [128, 8, 128], BF16); rden = carve("rden", [128, 512])
        au = carve("au", [128, 512]); av = carve("av", [128, 512]); aout = carve("aout", [128, 512])
        Rsets = [[carve(f"R{s_}_{i}", [128, 2, 128]) for i in range(14)] for s_ in range(2)]
        pTn = carve("pTn", [128, 2, 128]); prv = carve("prv", [128, 2, 128])
        psr2 = [carve(f"psr{i}", [128, 2, 128]) for i in range(2)]; psk2 = [carve(f"psk{i}", [128, 2, 128]) for i in range(2)]
        psv2 = [carve(f"psv{i}", [128, 2, 128]) for i in range(2)]; psl = carve("psl", [128, 2, 128])
        S0c = carve("S0c", [128, 16, 64]); S1c = carve("S1c", [128, 16, 64])
        Xa = [carve(f"Xa{i}", [128, 4, 128], BF16) for i in range(2)]
        Xb = [carve(f"Xb{i}", [128, 4, 128], BF16) for i in range(2)]
        Wk = carve("Wk", [128, 4, 128], BF16); Zz = [carve(f"Zz{i}", [128, 4, 128], BF16) for i in range(2)]
        ZFb = carve("ZFb", [128, 4, 128], BF16); idb = carve("idb", [128, 128], BF16)
        AakT = carve("AakT", [128, 4, 128], BF16); ArbT = carve("ArbT", [128, 4, 128], BF16); ArkT = carve("ArkT", [128, 4, 128], BF16)
        HT = carve("HT", [128, 128], BF16); QTI = carve("QTI", [128, 16, 64], BF16)
        Bm = carve("Bm", [128, 16, 64], BF16); Km = carve("Km", [128, 16, 64], BF16)
        VB = carve("VB", [128, 512], BF16); KA = carve("KA", [128, 512], BF16)
        atb = carve("atb", [128, 2, 128], BF16); btb = carve("btb", [128, 2, 128], BF16)
        ktb = carve("ktb", [128, 2, 128], BF16); rtb = carve("rtb", [128, 2, 128], BF16)
        S0cb = carve("S0cb", [128, 16, 64], BF16)
        mix_top = apos[0]
        print("arena words used (mix)", mix_top)
        apos[0] = 0
        gpost = carve("gpost", [128, D])
        actT = carve("actT", [128, NFC, 128], BF16)
        actQ = [Buf(f"actT{q}") for q in range(11)]
        scar = carve("scar", [128, 88, 16, 2])
        ext8 = [carve(f"ext8_{i}", [128, 8, 160]) for i in range(2)]
        acc8 = [carve(f"acc8_{i}", [128, 8, 128]) for i in range(2)]

        arena_tb = TB(arena, "arena")
        S.op("dve", lambda e: e.memset(arena[:, :], 0.0), w=[arena_tb])
        S.barrier()
        S.op("dve", lambda e: e.memset(ST.t[:], 0.0), w=[ST])
        S.op("dve", lambda e: e.memset(STb.t[:], 0.0), w=[STb])
        S.op("dve", lambda e: e.tensor_copy(out=idb.t[:], in_=K.t[:, K_ID:K_ID + 128]), r=[K], w=[idb])
        S.op("dve", lambda e: e.tensor_copy(out=onesb.t[:], in_=K.t[:, K_ONES:K_ONES + 128]), r=[K], w=[onesb])
        S.op("dve", lambda e: e.memset(carry.t[:], 0.0), w=[carry])
        S.op("dve", lambda e: e.memset(ccar.t[:], 0.0), w=[ccar])
        S.dma("pool", shTs.t[:].rearrange("p a b -> p (a b)"), shT[:, :], w=[shTs])

        def cpc(off, n=1):
            return cp.t[:, off:off + n]

        def rmsnorm_to_hT(src, gofs):
            S.op("act", lambda e: e.activation(out=xn.t[:], in_=src.t[:], func=AF.Square, accum_out=st6.t[:, 0:1]),
                 r=[src], w=[xn, st6])
            S.op("act", lambda e: e.activation(out=st6.t[:, 1:2], in_=st6.t[:, 0:1], func=AF.Sqrt, scale=1.0 / D, bias=1e-6),
                 r=[st6], w=[st6])
            S.op("dve", lambda e: e.reciprocal(out=rstd.t[:], in_=st6.t[:, 1:2]), r=[st6], w=[rstd])
            S.op("dve", lambda e: e.tensor_scalar(out=xn.t[:], in0=src.t[:], scalar1=rstd.t[:, 0:1], scalar2=None, op0=ALU.mult),
                 r=[src, rstd], w=[xn])
            for q in range(4):
                pb = pbank()
                for i in range(4):
                    kc = 4 * q + i
                    S.op("pe", lambda e, kc=kc, i=i, pb=pb: e.transpose(pb.t[:, i * 128:(i + 1) * 128], xn.t[:, kc * 128:(kc + 1) * 128], ident),
                         r=[xn, K], w=[pb])
                g4 = cp.t[:, gofs + 4 * q:gofs + 4 * q + 4].unsqueeze(2).broadcast_to([128, 4, 128])
                S.op("dve", lambda e, q=q, pb=pb, g4=g4: e.tensor_tensor(out=hT.t[:, 4 * q:4 * q + 4, :], in0=pb.t[:].rearrange("p (a b) -> p a b", a=4),
                                                                  in1=g4, op=ALU.mult), r=[pb, cp], w=[hT])

        def project(slv, ncols, pb):
            sl, v = slv
            for kc in range(16):
                S.op("pe", lambda e, kc=kc: e.matmul(pb.t[:, 0:ncols], lhsT=hT.t[:, kc, :], rhs=v[:, kc, :], start=(kc == 0), stop=(kc == 15)),
                     r=[hT, sl], w=[pb])

        mem_plan = [("w_mk", 0, 16, 0, 512), ("w_mv", 0, 16, 0, 512)]
        for mt in range(2):
            plan.extend(mem_plan)
        kinds = list(KINDS)
        npt_ = len([k for k in kinds if k != "sample"])
        for i_, kd in enumerate(kinds):
            plan_tile(kd, 16 - npt_ + i_ if kd != "sample" else 16)

        for mt in range(2):
            S.dma("pool", xt.t[:], memx[mt * 128:(mt + 1) * 128, :], w=[xt])
            rmsnorm_to_hT(xt, CP_GMEM)
            for which in range(2):
                slv = next_slab()
                pb = pbank()
                project(slv, 512, pb)
                S.op("act", lambda e, pb=pb: e.copy(out=rden.t[:], in_=pb.t[:]), r=[pb], w=[rden])
                S.dma("pool", (o_mk if which == 0 else o_mv)[mt * 128:(mt + 1) * 128, :], rden.t[:], r=[rden], is_out=True)
                if which == 0:
                    pb2 = pbank()
                    for h in range(4):
                        S.op("pe", lambda e, h=h, pb2=pb2: e.transpose(pb2.t[:, h * 128:(h + 1) * 128], rden.t[:, h * 128:(h + 1) * 128], ident),
                             r=[rden, K], w=[pb2])
                    S.op("dve", lambda e, pb2=pb2, mt=mt: e.tensor_copy(out=kTp.t[:, :, mt * 128:(mt + 1) * 128],
                                                                  in_=pb2.t[:].rearrange("p (a b) -> p a b", a=4)), r=[pb2], w=[kTp])
                else:
                    S.op("dve", lambda e, mt=mt: e.tensor_copy(out=Vp.t[:, mt, :], in_=rden.t[:]), r=[rden], w=[Vp])

        def f8(tb):
            return tb.t[:].rearrange("p (a b) -> p a b", a=8)

        def do_tile(ti, kind):
            sample = kind == "sample"
            full = kind != "pre"
            nseq = 16 if sample else 1
            L = 128 // nseq
            mSL, mSU, mUI = (K_BSL, K_BSU, K_BUI) if sample else (K_SL, K_SU, K_UI)
            nlev = 3 if sample else 7
            src = xsm[:, :] if sample else xwin[ti * 128:(ti + 1) * 128, :]
            S.dma("pool", xt.t[:], src, w=[xt])
            rmsnorm_to_hT(xt, CP_GMIX)
            groups = range(4, 9) if kind == "pre" else range(10)
            for g in groups:
                c0, n = CG[g]
                slv = next_slab()
                pb = pbank()
                project(slv, n, pb)
                if g == 0:
                    S.op("act", lambda e, pb=pb: e.activation(out=au.t[:], in_=pb.t[:], func=AF.Gelu_apprx_tanh), r=[pb], w=[au])
                elif g == 1:
                    S.op("act", lambda e, pb=pb: e.activation(out=av.t[:], in_=pb.t[:], func=AF.Gelu_apprx_tanh), r=[pb], w=[av])
                else:
                    S.op("act", lambda e, pb=pb, c0=c0, n=n: e.copy(out=proj.t[:, c0 - 1024:c0 - 1024 + n], in_=pb.t[:, 0:n]), r=[pb], w=[proj])
            ck(1)
            last_prompt = (kind == "full" and ti == 15)
            if last_prompt:
                S.dma("pool", o_psh[:, :], proj.t[127:128, 0:3264], r=[proj], is_out=True)
            if sample:
                S.dma("pool", o_ssh[:, :], proj.t[7:128:8, 0:3264], r=[proj], is_out=True)

            ac_list = []
            if full:
                if not sample:
                    S.defer = ac_list
                S.op("dve", lambda e: e.bn_stats(out=st6.t[:, 0:6], in_=av.t[:]), r=[av], w=[st6])
                S.op("dve", lambda e: e.bn_aggr(out=st6.t[:, 6:8], in_=st6.t[:, 0:6]), r=[st6], w=[st6])
                S.op("act", lambda e: e.activation(out=st6.t[:, 0:1], in_=st6.t[:, 7:8], func=AF.Sqrt, scale=1.0, bias=1e-5), r=[st6], w=[st6])
                S.op("dve", lambda e: e.reciprocal(out=rstd.t[:], in_=st6.t[:, 0:1]), r=[st6], w=[rstd])
                S.op("dve", lambda e: e.tensor_scalar(out=av.t[:], in0=av.t[:], scalar1=st6.t[:, 6:7], scalar2=rstd.t[:, 0:1],
                                                      op0=ALU.subtract, op1=ALU.mult), r=[av, st6, rstd], w=[av])
                S.op("dve", lambda e: e.tensor_tensor(out=av.t[:], in0=av.t[:], in1=RP.t[:, R_LNG:R_LNG + 512], op=ALU.mult), r=[av, RP], w=[av])
                S.op("dve", lambda e: e.tensor_tensor(out=av.t[:], in0=av.t[:], in1=RP.t[:, R_LNB:R_LNB + 512], op=ALU.add), r=[av, RP], w=[av])
                if last_prompt:
                    S.dma("pool", o_pcv[:, :], av.t[:], r=[av], is_out=True)
                if sample:
                    S.dma("pool", o_scv[:, :], av.t[:], r=[av], is_out=True)
                S.mark()
                wm_ = wms if sample else wmp
                bs_ = bsS if sample else bsP
                pb = pbank()
                for h in range(4):
                    S.op("pe", lambda e, h=h, pb=pb: e.matmul(pb.t[:, h * 128:(h + 1) * 128], lhsT=wm_.t[:, h * 128:(h + 1) * 128],
                                                              rhs=av.t[:, h * 128:(h + 1) * 128], start=True, stop=True), r=[wm_, av], w=[pb])
                for h in range(4):
                    S.op("dve", lambda e, h=h, pb=pb: e.scalar_tensor_tensor(out=aout.t[:, h * 128:(h + 1) * 128], in0=pb.t[:, h * 128:(h + 1) * 128],
                                                                             scalar=bs_.t[:, h:h + 1], in1=au.t[:, h * 128:(h + 1) * 128],
                                                                             op0=ALU.add, op1=ALU.mult), r=[pb, bs_, au], w=[aout])
                S.mark()
                pb = pbank()
                for h in range(4):
                    S.op("pe", lambda e, h=h, pb=pb: e.transpose(pb.t[:, h * 128:(h + 1) * 128], aout.t[:, h * 128:(h + 1) * 128], ident), r=[aout, K], w=[pb])
                S.op("act", lambda e, pb=pb: e.copy(out=mixT.t[:, 0:4, :], in_=pb.t[:].rearrange("p (a b) -> p a b", a=4)), r=[pb], w=[mixT])

                S.mark()
                pb = pbank()
                for h in range(4):
                    S.op("pe", lambda e, h=h, pb=pb: e.transpose(pb.t[:, h * 128:(h + 1) * 128], proj.t[:, 3264 + h * 128:3264 + (h + 1) * 128], ident),
                         r=[proj, K], w=[pb])
                S.op("act", lambda e, pb=pb: e.copy(out=qT.t[:], in_=pb.t[:].rearrange("p (a b) -> p a b", a=4)), r=[pb], w=[qT])
                S.mark()
                pbs = [pbank(), pbank()]
                pnum = banks[0] if sample else None
                pden = banks[1] if sample else None
                if not sample:
                    for h in range(4):
                        for mc in range(2):
                            i = h * 2 + mc
                            S.op("pe", lambda e, h=h, mc=mc, i=i: e.matmul(pbs[i // 4].t[:, (i % 4) * 128:(i % 4 + 1) * 128], lhsT=kTp.t[:, h, mc * 128:(mc + 1) * 128],
                                                                          rhs=qT.t[:, h, :], start=True, stop=True), r=[kTp, qT], w=[pbs[i // 4]])
                else:
                    for j in range(16):
                        kj, vj = kTp, Vp
                        S.dma("pool", kj.t[:].rearrange("p a b -> p (a b)"), ckT[j], w=[kj])
                        S.dma("pool", vj.t[:], cvv[j].rearrange("(mc p) c -> p mc c", p=128), w=[vj])
                        for h in range(4):
                            for mc in range(2):
                                i = h * 2 + mc
                                S.op("pe", lambda e, h=h, mc=mc, i=i, j=j, kj=kj: e.matmul(
                                    pbs[i // 4].t[:, (i % 4) * 128 + 8 * j:(i % 4) * 128 + 8 * j + 8], lhsT=kj.t[:, h, mc * 128:(mc + 1) * 128],
                                    rhs=qT.t[:, h, 8 * j:8 * j + 8], start=True, stop=True), r=[kj, qT], w=[pbs[i // 4]])
                        if j == 0:
                            pass
                        for half in range(2):
                            S.op("act", lambda e, half=half, j=j: e.activation(
                                out=PT.t[:, 4 * half:4 * half + 4, 8 * j:8 * j + 8],
                                in_=pbs[half].t[:].rearrange("p (a b) -> p a b", a=4)[:, :, 8 * j:8 * j + 8], func=AF.Exp, scale=128.0 ** -0.5),
                                r=[pbs[half]], w=[PT])
                        for h in range(4):
                            for mc in range(2):
                                S.op("pe", lambda e, h=h, mc=mc, j=j, vj=vj: e.matmul(
                                    pnum.t[:, h * 128 + 8 * j:h * 128 + 8 * j + 8], lhsT=vj.t[:, mc, h * 128:(h + 1) * 128],
                                    rhs=PT.t[:, 2 * h + mc, 8 * j:8 * j + 8], start=(mc == 0), stop=(mc == 1)), r=[vj, PT], w=[pnum])
                if not sample:
                    for half in range(2):
                        S.op("act", lambda e, half=half: e.activation(out=PT.t[:, 4 * half:4 * half + 4, :],
                                                                      in_=pbs[half].t[:].rearrange("p (a b) -> p a b", a=4), func=AF.Exp, scale=128.0 ** -0.5),
                             r=[pbs[half]], w=[PT])
                    S.mark()
                    pnum, pden = pbank(), pbank()
                    for h in range(4):
                        for mc in range(2):
                            S.op("pe", lambda e, h=h, mc=mc: e.matmul(pnum.t[:, h * 128:(h + 1) * 128], lhsT=Vp.t[:, mc, h * 128:(h + 1) * 128],
                                                                      rhs=PT.t[:, 2 * h + mc, :], start=(mc == 0), stop=(mc == 1)), r=[Vp, PT], w=[pnum])
                for h in range(4):
                    for mc in range(2):
                        S.op("pe", lambda e, h=h, mc=mc: e.matmul(pden.t[:, h * 128:(h + 1) * 128], lhsT=onesb.t[:],
                                                                  rhs=PT.t[:, 2 * h + mc, :], start=(mc == 0), stop=(mc == 1)), r=[onesb, PT], w=[pden])
                S.op("dve", lambda e: e.reciprocal(out=rden.t[:], in_=pden.t[:]), r=[pden], w=[rden])
                S.op("dve", lambda e: e.tensor_tensor(out=mixT.t[:, 12:16, :], in0=pnum.t[:].rearrange("p (a b) -> p a b", a=4),
                                                      in1=rden.t[:].rearrange("p (a b) -> p a b", a=4), op=ALU.mult), r=[pnum, rden], w=[mixT])
                S.mark()
                S.defer = None

            Vt = TB(VB.t[:, 0:256], "Vt"); Vt.b = VB.b
            Bt = TB(VB.t[:, 256:512], "Bt"); Bt.b = VB.b
            Kt = TB(KA.t[:, 0:256], "Kt"); Kt.b = KA.b
            At = TB(KA.t[:, 256:512], "At"); At.b = KA.b

            def shift(dst, c0, n):
                pb = pbank()
                for i in range(n):
                    cc = c0 + i
                    ncol = 64 if cc == 25 else 128
                    S.op("pe", lambda e, i=i, cc=cc, ncol=ncol, pb=pb: e.transpose(pb.t[0:ncol, i * 128:(i + 1) * 128], proj.t[:, cc * 128:cc * 128 + ncol], ident),
                         r=[proj, K], w=[pb])
                if c0 + n - 1 == 25:
                    S.op("act", lambda e, pb=pb: e.copy(out=pTn.t[:, 0, :], in_=pb.t[:, 0:128]), r=[pb], w=[pTn])
                    S.op("act", lambda e, pb=pb: e.copy(out=pTn.t[0:64, 1, :], in_=pb.t[0:64, 128:256]), r=[pb], w=[pTn])
                else:
                    S.op("act", lambda e, pb=pb: e.copy(out=pTn.t[:, 0:n, :], in_=pb.t[:, 0:n * 128].rearrange("p (a b) -> p a b", a=n)), r=[pb], w=[pTn])
                S.op("pool", lambda e: e.tensor_copy(out=prv.t[:, 0:n, 1:128], in_=pTn.t[:, 0:n, 0:127]), r=[pTn], w=[prv])
                if sample:
                    S.op("pool", lambda e: e.tensor_copy(out=prv.t[:, 0:n, 0:128:8], in_=shTs.t[:, c0:c0 + n, :]), r=[shTs], w=[prv])
                else:
                    S.op("pool", lambda e: e.tensor_copy(out=prv.t[:, 0:n, 0:1], in_=carry.t[:, c0:c0 + n].unsqueeze(2)), r=[carry], w=[prv])
                    S.op("pool", lambda e: e.tensor_copy(out=carry.t[:, c0:c0 + n].unsqueeze(2), in_=pTn.t[:, 0:n, 127:128]), r=[pTn], w=[carry])
                S.op("pool", lambda e: e.tensor_tensor(out=prv.t[:, 0:n, :], in0=prv.t[:, 0:n, :], in1=pTn.t[:, 0:n, :], op=ALU.subtract), r=[prv, pTn], w=[prv])
                mu3 = cp.t[:, CP_MU + c0:CP_MU + c0 + n].unsqueeze(2).broadcast_to([128, n, 128])
                S.op("pool", lambda e: e.tensor_tensor(out=prv.t[:, 0:n, :], in0=prv.t[:, 0:n, :], in1=mu3, op=ALU.mult), r=[prv, cp], w=[prv])
                S.op("pool", lambda e: e.tensor_tensor(out=dst.t[:, 0:n, :], in0=prv.t[:, 0:n, :], in1=pTn.t[:, 0:n, :], op=ALU.add), r=[prv, pTn], w=[dst])

            shift(psl, 24, 2 if full else 1)
            S.op("act", lambda e: e.activation(out=lj.t[0:64, 0:128], in_=psl.t[0:64, 0, :], func=AF.Tanh), r=[psl], w=[lj])
            if full:
                S.op("act", lambda e: e.activation(out=lj.t[0:64, 128:256], in_=psl.t[0:64, 1, :], func=AF.Sigmoid), r=[psl], w=[lj])
            ck(2)
            YT = [banks[0], banks[1]]
            PS = [banks[2], banks[3]]
            id4 = idb.t[:].unsqueeze(1).broadcast_to([128, 4, 128])

            def mk4(off):
                return K.t[:, off:off + 128].unsqueeze(1).broadcast_to([128, 4, 128])

            def b4(pb):
                return pb.t[:].rearrange("p (a b) -> p a b", a=4)

            def fl(tb):
                return tb.t[:].rearrange("p a b -> p (a b)")

            def do_shifts(hg_):
                if full:
                    shift(psr2[hg_ % 2], 2 * hg_, 2)
                shift(psk2[hg_ % 2], 8 + 2 * hg_, 2)
                shift(psv2[hg_ % 2], 16 + 2 * hg_, 2)

            do_shifts(0)
            def E_stage(hg_):
                g0 = 2 * hg_
                (sgw, cum, gam, igam, gprev, a_, kkn, kmod, rt, kt, bt, at, gT, bonus) = Rsets[hg_ % 2]
                psr, psk, psv = psr2[hg_ % 2], psk2[hg_ % 2], psv2[hg_ % 2]

                def bc2(off):
                    return cp.t[:, off + g0:off + g0 + 2].unsqueeze(2).broadcast_to([128, 2, 128])

                pw = TB(banks[0].t[:, 0:256], "pw"); pw.b = banks[0].b
                pa = TB(banks[1].t[:, 0:256], "pa"); pa.b = banks[1].b
                for cc in range(2):
                    S.op("pe", lambda e, cc=cc: e.matmul(pw.t[:, cc * 128:(cc + 1) * 128], lhsT=lo.t[0:64, (g0 + cc) * 128:(g0 + cc + 1) * 128],
                                                         rhs=lj.t[0:64, 0:128], start=True, stop=True), r=[lo, lj], w=[pw])
                for cc in range(2):
                    S.op("pe", lambda e, cc=cc: e.matmul(pa.t[:, cc * 128:(cc + 1) * 128], lhsT=lo.t[64:128, (g0 + cc) * 128:(g0 + cc + 1) * 128],
                                                         rhs=psl.t[64:128, 0, :], start=True, stop=True), r=[lo, psl], w=[pa])
                for cc in range(2):
                    S.op("act", lambda e, cc=cc: e.activation(out=sgw.t[:, cc, :], in_=pw.t[:, cc * 128:(cc + 1) * 128], func=AF.Sigmoid,
                                                              bias=cp.t[:, CP_W0 + g0 + cc:CP_W0 + g0 + cc + 1], scale=1.0), r=[pw, cp], w=[sgw])
                    S.op("act", lambda e, cc=cc: e.activation(out=a_.t[:, cc, :], in_=pa.t[:, cc * 128:(cc + 1) * 128], func=AF.Sigmoid,
                                                              bias=cp.t[:, CP_A0 + g0 + cc:CP_A0 + g0 + cc + 1], scale=1.0), r=[pa, cp], w=[a_])
                if full:
                    pg = TB(banks[0].t[:, 256:512], "pg"); pg.b = banks[0].b
                    for cc in range(2):
                        S.op("pe", lambda e, cc=cc: e.matmul(pg.t[:, cc * 128:(cc + 1) * 128], lhsT=lo.t[0:64, 1024 + (g0 + cc) * 128:1024 + (g0 + cc + 1) * 128],
                                                             rhs=lj.t[0:64, 128:256], start=True, stop=True), r=[lo, lj], w=[pg])
                    S.op("act", lambda e: e.copy(out=fl(gT), in_=pg.t[:, 0:256]), r=[pg], w=[gT])
                ko = K_RSTS if sample else K_RSTP
                S.op("dve", lambda e: e.tensor_tensor_scan(out=fl(cum), data0=K.t[:, ko:ko + 256], data1=fl(sgw), initial=0.0, op0=ALU.mult, op1=ALU.add), r=[K, sgw], w=[cum])
                S.op("act", lambda e: e.activation(out=fl(gam), in_=fl(cum), func=AF.Exp, scale=-C0), r=[cum], w=[gam])
                S.op("act", lambda e: e.activation(out=fl(igam), in_=fl(cum), func=AF.Exp, scale=C0), r=[cum], w=[igam])
                S.op("dve", lambda e: e.tensor_tensor(out=fl(gprev), in0=fl(cum), in1=fl(sgw), op=ALU.subtract), r=[cum, sgw], w=[gprev])
                S.op("act", lambda e: e.activation(out=fl(gprev), in_=fl(gprev), func=AF.Exp, scale=-C0), r=[gprev], w=[gprev])
                S.op("dve", lambda e: e.tensor_tensor(out=kkn.t[:], in0=psk.t[:], in1=bc2(CP_KK), op=ALU.mult), r=[psk, cp], w=[kkn])
                S.op("dve", lambda e: e.tensor_tensor(out=kt.t[:], in0=kkn.t[:], in1=kkn.t[:], op=ALU.mult), r=[kkn], w=[kt])
                pk = TB(banks[2].t[:, 0:256], "pk"); pk.b = banks[2].b
                for cc in range(2):
                    S.op("pe", lambda e, cc=cc: e.matmul(pk.t[:, cc * 128:(cc + 1) * 128], lhsT=K.t[:, K_B64:K_B64 + 128], rhs=kt.t[:, cc, :], start=True, stop=True), r=[K, kt], w=[pk])
                S.op("act", lambda e: e.activation(out=fl(bt), in_=pk.t[:, 0:256], func=AF.Sqrt, scale=1.0, bias=1e-30), r=[pk], w=[bt])
                S.op("dve", lambda e: e.tensor_scalar(out=bt.t[:], in0=bt.t[:], scalar1=1e-12, scalar2=None, op0=ALU.max), r=[bt], w=[bt])
                S.op("dve", lambda e: e.reciprocal(out=bt.t[:], in_=bt.t[:]), r=[bt], w=[bt])
                S.op("dve", lambda e: e.tensor_tensor(out=kkn.t[:], in0=kkn.t[:], in1=bt.t[:], op=ALU.mult), r=[kkn, bt], w=[kkn])
                S.op("dve", lambda e: e.tensor_tensor(out=kmod.t[:], in0=a_.t[:], in1=bc2(CP_KA), op=ALU.mult), r=[a_, cp], w=[kmod])
                S.op("dve", lambda e: e.tensor_tensor(out=kmod.t[:], in0=kmod.t[:], in1=omka.t[:, g0:g0 + 2].unsqueeze(2).broadcast_to([128, 2, 128]), op=ALU.add), r=[kmod, omka], w=[kmod])
                S.op("dve", lambda e: e.tensor_tensor(out=kmod.t[:], in0=kmod.t[:], in1=psk.t[:], op=ALU.mult), r=[kmod, psk], w=[kmod])
                S.op("dve", lambda e: e.scalar_tensor_tensor(out=fl(at), in0=fl(kkn), scalar=-1.0, in1=fl(gprev), op0=ALU.mult, op1=ALU.mult), r=[kkn, gprev], w=[at])
                S.op("dve", lambda e: e.tensor_tensor(out=bt.t[:], in0=kkn.t[:], in1=a_.t[:], op=ALU.mult), r=[kkn, a_], w=[bt])
                S.op("dve", lambda e: e.tensor_tensor(out=bt.t[:], in0=bt.t[:], in1=igam.t[:], op=ALU.mult), r=[bt, igam], w=[bt])
                S.op("dve", lambda e: e.tensor_tensor(out=kt.t[:], in0=kmod.t[:], in1=igam.t[:], op=ALU.mult), r=[kmod, igam], w=[kt])
                if full:
                    S.op("dve", lambda e: e.tensor_tensor(out=rt.t[:], in0=psr.t[:], in1=gam.t[:], op=ALU.mult), r=[psr, gam], w=[rt])
                    S.op("dve", lambda e: e.tensor_tensor(out=bonus.t[:], in0=psr.t[:], in1=kmod.t[:], op=ALU.mult), r=[psr, kmod], w=[bonus])
                    S.op("dve", lambda e: e.tensor_tensor(out=bonus.t[:], in0=bonus.t[:], in1=bc2(CP_RK), op=ALU.mult), r=[bonus, cp], w=[bonus])
                    pbn = TB(banks[2].t[:, 256:512], "pbn"); pbn.b = banks[2].b
                    for cc in range(2):
                        S.op("pe", lambda e, cc=cc: e.matmul(pbn.t[:, cc * 128:(cc + 1) * 128], lhsT=K.t[:, K_B64:K_B64 + 128], rhs=bonus.t[:, cc, :], start=True, stop=True), r=[K, bonus], w=[pbn])
                    S.op("dve", lambda e: e.tensor_tensor(out=fl(bonus), in0=pbn.t[:, 0:256], in1=fl(psv), op=ALU.mult), r=[pbn, psv], w=[bonus])

            E_stage(0)
            for hg in range(4):
                g0 = 2 * hg
                (sgw, cum, gam, igam, gprev, a_, kkn, kmod, rt, kt, bt, at, gT, bonus) = Rsets[hg % 2]
                psr, psk, psv = psr2[hg % 2], psk2[hg % 2], psv2[hg % 2]

                def bc2(off):
                    return cp.t[:, off + g0:off + g0 + 2].unsqueeze(2).broadcast_to([128, 2, 128])

                S.op("act", lambda e: e.copy(out=atb.t[:], in_=at.t[:]), r=[at], w=[atb])
                S.op("act", lambda e: e.copy(out=btb.t[:], in_=bt.t[:]), r=[bt], w=[btb])
                S.op("act", lambda e: e.copy(out=ktb.t[:], in_=kt.t[:]), r=[kt], w=[ktb])
                if full:
                    S.op("act", lambda e: e.copy(out=rtb.t[:], in_=rt.t[:]), r=[rt], w=[rtb])
                ck(3)
                pb = pbank()
                for cc in range(2):
                    S.op("pe", lambda e, cc=cc, pb=pb: e.transpose(pb.t[:, cc * 128:(cc + 1) * 128], psv.t[:, cc, :], ident), r=[psv, K], w=[pb])
                    S.op("pe", lambda e, cc=cc, pb=pb: e.transpose(pb.t[:, 256 + cc * 128:256 + (cc + 1) * 128], bt.t[:, cc, :], ident), r=[bt, K], w=[pb])
                S.op("act", lambda e, pb=pb: e.copy(out=VB.t[:], in_=pb.t[:]), r=[pb], w=[VB])
                pb = pbank()
                for cc in range(2):
                    S.op("pe", lambda e, cc=cc, pb=pb: e.transpose(pb.t[:, cc * 128:(cc + 1) * 128], kt.t[:, cc, :], ident), r=[kt, K], w=[pb])
                    S.op("pe", lambda e, cc=cc, pb=pb: e.transpose(pb.t[:, 256 + cc * 128:256 + (cc + 1) * 128], at.t[:, cc, :], ident), r=[at, K], w=[pb])
                S.op("act", lambda e, pb=pb: e.copy(out=KA.t[:], in_=pb.t[:]), r=[pb], w=[KA])

                ck(4)

                def fm(buf, i):
                    return buf.t[(i % 2) * 64:(i % 2) * 64 + 64, i // 2, :]

                def xmat(lb, rb, moff, dst):
                    pe_, po_ = pbank(), pbank()
                    for i in range(4):
                        pb = pe_ if i % 2 == 0 else po_
                        sl_ = slice((i // 2) * 128, (i // 2 + 1) * 128)
                        S.op("pe", lambda e, i=i, sl_=sl_, pb=pb: e.matmul(pb.t[:, sl_], lhsT=fm(lb, i), rhs=fm(rb, i), start=True, stop=True), r=[lb, rb], w=[pb])
                    m2 = K.t[:, moff:moff + 128].unsqueeze(1).broadcast_to([128, 2, 128])
                    for par, pb in ((0, pe_), (1, po_)):
                        S.op("dve", lambda e, par=par, pb=pb: e.tensor_tensor(out=dst.t[:, par::2, :], in0=pb.t[:, 0:256].rearrange("p (a b) -> p a b", a=2), in1=m2, op=ALU.mult),
                             r=[pb, K], w=[dst])

                xmat(atb, btb, mSL, Xa[0])
                xmat(btb, atb, mSU, Xb[0])
                xmat(ktb, atb, mSU, AakT)
                if full:
                    xmat(btb, rtb, mUI, ArbT)
                    xmat(ktb, rtb, mUI, ArkT)
                ck(5)
                pz = pbank()
                for i in range(4):
                    S.op("pe", lambda e, i=i, pz=pz: e.matmul(pz.t[:, i * 128:i * 128 + 64], lhsT=AakT.t[:, i, :], rhs=Vt.t[:, i * 64:(i + 1) * 64], start=True, stop=True),
                         r=[AakT, Vt], w=[pz])
                S.op("act", lambda e, pz=pz: e.copy(out=Zz[0].t[:, :, 0:64], in_=b4(pz)[:, :, 0:64]), r=[pz], w=[Zz[0]])
                S.op("dve", lambda e: e.tensor_copy(out=Zz[0].t[:, :, 64:128], in_=At.t[:, :].rearrange("p (a b) -> p a b", a=4)), r=[At], w=[Zz[0]])
                lst = []
                if hg + 1 < 4:
                    do_shifts(hg + 1)
                    S.defer = lst
                    E_stage(hg + 1)
                    S.defer = None
                per = (len(lst) + nlev - 1) // nlev
                cur = 0
                for lev in range(nlev):
                    pz = pbank()
                    for i in range(4):
                        S.op("pe", lambda e, i=i, lev=lev, pz=pz, cur=cur: e.matmul(pz.t[:, i * 128:(i + 1) * 128], lhsT=Xb[cur].t[:, i, :], rhs=Zz[lev % 2].t[:, i, :], start=True, stop=False),
                             r=[Xb[cur], Zz[lev % 2]], w=[pz])
                        S.op("pe", lambda e, i=i, lev=lev, pz=pz: e.matmul(pz.t[:, i * 128:(i + 1) * 128], lhsT=idb.t[:], rhs=Zz[lev % 2].t[:, i, :], start=False, stop=True),
                             r=[idb, Zz[lev % 2]], w=[pz])
                    zdst = ZFb if lev == nlev - 1 else Zz[(lev + 1) % 2]
                    S.op("act", lambda e, zdst=zdst, pz=pz: e.copy(out=zdst.t[:], in_=b4(pz)), r=[pz], w=[zdst])
                    if lev < nlev - 1:
                        px, pxt = pbank(), pbank()
                        for i in range(4):
                            S.op("pe", lambda e, i=i, cur=cur, px=px: e.matmul(px.t[:, i * 128:(i + 1) * 128], lhsT=Xb[cur].t[:, i, :], rhs=Xa[cur].t[:, i, :], start=True, stop=True),
                                 r=[Xa[cur], Xb[cur]], w=[px])
                            S.op("pe", lambda e, i=i, cur=cur, pxt=pxt: e.matmul(pxt.t[:, i * 128:(i + 1) * 128], lhsT=Xa[cur].t[:, i, :], rhs=Xb[cur].t[:, i, :], start=True, stop=True),
                                 r=[Xa[cur], Xb[cur]], w=[pxt])
                        S.op("dve", lambda e, cur=cur, px=px: e.tensor_copy(out=Xa[1 - cur].t[:], in_=b4(px)), r=[px], w=[Xa[1 - cur]])
                        S.op("act", lambda e, cur=cur, pxt=pxt: e.copy(out=Xb[1 - cur].t[:], in_=b4(pxt)), r=[pxt], w=[Xb[1 - cur]])
                        cur = 1 - cur
                    S.run_deferred(lst, per)
                    if ac_list and lev in (1, 4):
                        S.run_unit(ac_list)
                S.run_deferred(lst, len(lst))
                if hg == 3:
                    while ac_list:
                        S.run_unit(ac_list)
                ck(6)
                ZF = ZFb
                for i in range(4):
                    h = 4 * hg + i
                    hp, hc, lc = i % 2, h // 2, i // 2
                    prt = slice(hp * 64, hp * 64 + 64)
                    U0 = ZF.t[:, i, 0:64]
                    G = ZF.t[:, i, 64:128]
                    Bh = Bt.t[:, i * 64:(i + 1) * 64]
                    Kh = Kt.t[:, i * 64:(i + 1) * 64]
                    Vh = Vt.t[:, i * 64:(i + 1) * 64]
                    if sample and hp == 0:
                        S.dma("pool", S0c.t[:].rearrange("p a b -> p (a b)"), s0T[:, hc * 1024:(hc + 1) * 1024], w=[S0c])
                        S.op("act", lambda e: e.copy(out=S0cb.t[:], in_=S0c.t[:]), r=[S0c], w=[S0cb])
                    if full:
                        ph = pbank()
                        S.op("pe", lambda e, i=i, G=G, ph=ph, prt=prt: e.matmul(ph.t[prt, 0:128], lhsT=G, rhs=ArbT.t[:, i, :], start=True, stop=True), r=[ZF, ArbT], w=[ph])
                        S.op("dve", lambda e, i=i, ph=ph, prt=prt: e.tensor_tensor(out=HT.t[prt, :], in0=ph.t[prt, 0:128], in1=fm(rt, i), op=ALU.add), r=[ph, rt], w=[HT])
                        ybk = YT[hp]
                        yo = ybk.t[prt, lc * 128:(lc + 1) * 128]
                        S.op("pe", lambda e, i=i, U0=U0, yo=yo: e.matmul(yo, lhsT=U0, rhs=ArbT.t[:, i, :], start=True, stop=False), r=[ZF, ArbT], w=[ybk])
                        S.op("pe", lambda e, i=i, Vh=Vh, yo=yo: e.matmul(yo, lhsT=Vh, rhs=ArkT.t[:, i, :], start=False, stop=False), r=[Vt, ArkT], w=[ybk])
                        if not sample:
                            S.op("pe", lambda e, yo=yo, prt=prt, hc=hc: e.matmul(yo, lhsT=STb.t[prt, hc, :], rhs=HT.t[prt, :], start=False, stop=True), r=[STb, HT], w=[ybk])
                        else:
                            for j in range(16):
                                S.op("pe", lambda e, j=j, prt=prt, hc=hc, ybk=ybk: e.matmul(
                                    ybk.t[prt, lc * 128 + 8 * j:lc * 128 + 8 * j + 8], lhsT=S0cb.t[prt, j, :],
                                    rhs=HT.t[prt, 8 * j:8 * j + 8], start=False, stop=(j == 15)), r=[S0cb, HT], w=[ybk])
                    if not sample:
                        pq = pbank()
                        S.op("pe", lambda e, G=G, Bh=Bh, pq=pq, prt=prt: e.matmul(pq.t[prt, 0:64], lhsT=G, rhs=Bh, start=True, stop=True), r=[ZF, Bt], w=[pq])
                        S.op("act", lambda e, pq=pq, prt=prt: e.copy(out=QTI.t[prt, 0, :], in_=pq.t[prt, 0:64]), r=[pq], w=[QTI])
                        pbk = PS[hp]
                        po = pbk.t[prt, lc * 64:(lc + 1) * 64]
                        S.op("pe", lambda e, po=po, Bh=Bh, U0=U0: e.matmul(po, lhsT=Bh, rhs=U0, start=True, stop=False), r=[Bt, ZF], w=[pbk])
                        S.op("pe", lambda e, po=po, Kh=Kh, Vh=Vh: e.matmul(po, lhsT=Kh, rhs=Vh, start=False, stop=False), r=[Kt, Vt], w=[pbk])
                        S.op("pe", lambda e, po=po, prt=prt, hc=hc: e.matmul(po, lhsT=QTI.t[prt, 0, :], rhs=STb.t[prt, hc, :], start=False, stop=True), r=[QTI, STb], w=[pbk])
                    else:
                        rm = K.t[:, K_RM16:K_RM16 + 16].unsqueeze(2).broadcast_to([128, 16, 64])
                        S.op("dve", lambda e, Bh=Bh, rm=rm: e.tensor_tensor(out=Bm.t[:], in0=Bh.unsqueeze(1).broadcast_to([128, 16, 64]), in1=rm, op=ALU.mult), r=[Bt, K], w=[Bm])
                        S.op("dve", lambda e, Kh=Kh, rm=rm: e.tensor_tensor(out=Km.t[:], in0=Kh.unsqueeze(1).broadcast_to([128, 16, 64]), in1=rm, op=ALU.mult), r=[Kt, K], w=[Km])
                        for j8 in range(2):
                            pq = pbank()
                            for jj in range(8):
                                j = j8 * 8 + jj
                                S.op("pe", lambda e, G=G, j=j, jj=jj, pq=pq, prt=prt: e.matmul(pq.t[prt, jj * 64:(jj + 1) * 64], lhsT=G, rhs=Bm.t[:, j, :], start=True, stop=True), r=[ZF, Bm], w=[pq])
                            S.op("act", lambda e, pq=pq, prt=prt, j8=j8: e.copy(out=QTI.t[prt, j8 * 8:(j8 + 1) * 8, :], in_=pq.t[prt, :].rearrange("p (a b) -> p a b", a=8)),
                                 r=[pq], w=[QTI])
                        for j8 in range(2):
                            po_b = PS[hp]
                            for jj in range(8):
                                j = j8 * 8 + jj
                                po = po_b.t[prt, jj * 64:(jj + 1) * 64]
                                S.op("pe", lambda e, po=po, j=j, U0=U0: e.matmul(po, lhsT=Bm.t[:, j, :], rhs=U0, start=True, stop=False), r=[Bm, ZF], w=[po_b])
                                S.op("pe", lambda e, po=po, j=j, Vh=Vh: e.matmul(po, lhsT=Km.t[:, j, :], rhs=Vh, start=False, stop=False), r=[Km, Vt], w=[po_b])
                                S.op("pe", lambda e, po=po, j=j, prt=prt: e.matmul(po, lhsT=QTI.t[prt, j, :], rhs=S0cb.t[prt, j, :], start=False, stop=True), r=[QTI, S0cb], w=[po_b])
                            gl = gam.t[prt, lc, 7 + 64 * j8:64 * j8 + 64:8].unsqueeze(2).broadcast_to([64, 8, 64])
                            S.op("dve", lambda e, po_b=po_b, prt=prt, j8=j8: e.tensor_tensor(
                                out=S1c.t[prt, j8 * 8:(j8 + 1) * 8, :], in0=po_b.t[prt, :].rearrange("p (a b) -> p a b", a=8), in1=S0c.t[prt, j8 * 8:(j8 + 1) * 8, :], op=ALU.add),
                                 r=[po_b, S0c], w=[S1c])
                            S.op("dve", lambda e, prt=prt, j8=j8, gl=gl: e.tensor_tensor(
                                out=S1c.t[prt, j8 * 8:(j8 + 1) * 8, :], in0=S1c.t[prt, j8 * 8:(j8 + 1) * 8, :], in1=gl, op=ALU.mult), r=[S1c, gam], w=[S1c])
                        if hp == 1:
                            S.dma("pool", o_swkv[:, hc * 1024:(hc + 1) * 1024], S1c.t[:].rearrange("p a b -> p (a b)"), r=[S1c], is_out=True)
                ck(7)
                if not sample:
                    for hp_ in range(2):
                        pr_ = slice(hp_ * 64, hp_ * 64 + 64)
                        gl = gam.t[pr_, :, 127:128].broadcast_to([64, 2, 64])
                        S.op("dve", lambda e, hp_=hp_, pr_=pr_: e.tensor_tensor(out=ST.t[pr_, g0:g0 + 2, :], in0=PS[hp_].t[pr_, 0:128].rearrange("p (a b) -> p a b", a=2),
                                                                                in1=ST.t[pr_, g0:g0 + 2, :], op=ALU.add), r=[PS[hp_], ST], w=[ST])
                        S.op("dve", lambda e, gl=gl, pr_=pr_: e.tensor_tensor(out=ST.t[pr_, g0:g0 + 2, :], in0=ST.t[pr_, g0:g0 + 2, :], in1=gl, op=ALU.mult), r=[ST, gam], w=[ST])
                        S.op("act", lambda e, pr_=pr_: e.copy(out=STb.t[pr_, g0:g0 + 2, :], in_=ST.t[pr_, g0:g0 + 2, :]), r=[ST], w=[STb])
                if not full:
                    continue
                yT, cen, sq, rs_ = sgw, cum, igam, gprev
                for hp_ in range(2):
                    S.op("act", lambda e, hp_=hp_: e.copy(out=fl(yT)[hp_ * 64:hp_ * 64 + 64, :], in_=YT[hp_].t[hp_ * 64:hp_ * 64 + 64, 0:256]), r=[YT[hp_]], w=[yT])
                pm = pbank()
                for cc in range(2):
                    S.op("pe", lambda e, cc=cc, pm=pm: e.matmul(pm.t[:, cc * 128:(cc + 1) * 128], lhsT=K.t[:, K_B64:K_B64 + 128], rhs=yT.t[:, cc, :], start=True, stop=True), r=[K, yT], w=[pm])
                S.op("dve", lambda e, pm=pm: e.scalar_tensor_tensor(out=fl(cen), in0=pm.t[:, 0:256], scalar=-1.0 / 64, in1=fl(yT), op0=ALU.mult, op1=ALU.add), r=[pm, yT], w=[cen])
                S.op("dve", lambda e: e.tensor_tensor(out=sq.t[:], in0=cen.t[:], in1=cen.t[:], op=ALU.mult), r=[cen], w=[sq])
                pv2 = pbank()
                for cc in range(2):
                    S.op("pe", lambda e, cc=cc, pv2=pv2: e.matmul(pv2.t[:, cc * 128:(cc + 1) * 128], lhsT=K.t[:, K_B64:K_B64 + 128], rhs=sq.t[:, cc, :], start=True, stop=True), r=[K, sq], w=[pv2])
                S.op("act", lambda e, pv2=pv2: e.activation(out=fl(rs_), in_=pv2.t[:, 0:256], func=AF.Sqrt, scale=1.0 / 64, bias=64e-5), r=[pv2], w=[rs_])
                S.op("dve", lambda e: e.reciprocal(out=rs_.t[:], in_=rs_.t[:]), r=[rs_], w=[rs_])
                S.op("dve", lambda e: e.tensor_tensor(out=cen.t[:], in0=cen.t[:], in1=rs_.t[:], op=ALU.mult), r=[cen, rs_], w=[cen])
                S.op("dve", lambda e: e.tensor_tensor(out=cen.t[:], in0=cen.t[:], in1=bc2(CP_LG), op=ALU.mult), r=[cen, cp], w=[cen])
                S.op("dve", lambda e: e.tensor_tensor(out=cen.t[:], in0=cen.t[:], in1=bc2(CP_LB), op=ALU.add), r=[cen, cp], w=[cen])
                S.op("dve", lambda e: e.tensor_tensor(out=cen.t[:], in0=cen.t[:], in1=bonus.t[:], op=ALU.add), r=[cen, bonus], w=[cen])
                S.op("dve", lambda e: e.tensor_tensor(out=mixT.t[:, 4 + g0:4 + g0 + 2, :], in0=cen.t[:], in1=gT.t[:], op=ALU.mult), r=[cen, gT], w=[mixT])
            if last_prompt:
                S.dma("pool", o_pwkv[:, :], ST.t[:].rearrange("p a b -> p (a b)"), r=[ST], is_out=True)
            if not full:
                return
            S.barrier()

            for g in range(4):
                slv = next_slab()
                sl, v = slv
                for kc in range(16):
                    S.op("pe", lambda e, kc=kc, g=g, v=v: e.matmul(banks[g].t[:], lhsT=mixT.t[:, kc, :], rhs=v[:, kc, :], start=(kc == 0), stop=(kc == 15)), r=[mixT, sl], w=[banks[g]])
            post_norm_residual(R_GMIXP)
            if STOP_AFTER_MIX:
                return
            if sample:
                S.dma("pool", scar.t[:].rearrange("p a b c -> p (a b c)"), cvT[:, :], w=[scar])
            rmsnorm_to_hT(xt, CP_GFFN)
            carry_only = (ti == 7 and not sample)

            def ffn_up(q):
                sg_, vg = next_slab()
                sv_, vv = next_slab()
                pg_, pv_ = pbank(), pbank()
                for (pb, sl_, vw) in ((pg_, sg_, vg), (pv_, sv_, vv)):
                    for i in range(4):
                        for kc in range(16):
                            S.op("pe", lambda e, i=i, kc=kc, pb=pb, vw=vw: e.matmul(pb.t[:, i * 128:(i + 1) * 128], lhsT=vw[:, kc, i * 128:(i + 1) * 128], rhs=hT.t[:, kc, :],
                                                                                 start=(kc == 0), stop=(kc == 15)), r=[sl_, hT], w=[pb])
                ex, ac = ext8[q % 2], acc8[q % 2]
                W = nseq * (L + 2)
                e4 = ex.t[:, :, 0:W].rearrange("p c (j l) -> p c j l", j=nseq)
                for side, pb in ((0, pg_), (1, pv_)):
                    c0 = 4 * q + side * NFC
                    es = e4[:, 4 * side:4 * side + 4]
                    p4 = pb.t[:].rearrange("p (c j l) -> p c j l", c=4, j=nseq)
                    if sample:
                        S.op("dve", lambda e, es=es, c0=c0: e.tensor_copy(out=es[:, :, :, 0:2], in_=scar.t[:, c0:c0 + 4, :, :]), r=[scar], w=[ex])
                    else:
                        S.op("dve", lambda e, es=es, c0=c0: e.tensor_copy(out=es[:, :, 0, 0:2], in_=ccar.t[:, c0:c0 + 4, :]), r=[ccar], w=[ex])
                    S.op("act", lambda e, es=es, p4=p4: e.copy(out=es[:, :, :, 2:L + 2], in_=p4), r=[pb], w=[ex])
                    if sample:
                        S.op("dve", lambda e, es=es, c0=c0: e.tensor_copy(out=scar.t[:, c0:c0 + 4, :, :], in_=es[:, :, :, L:L + 2]), r=[ex], w=[scar])
                    else:
                        S.op("dve", lambda e, es=es, c0=c0: e.tensor_copy(out=ccar.t[:, c0:c0 + 4, :], in_=es[:, :, 0, L:L + 2]), r=[ex], w=[ccar])
                    for i in range(4):
                        cidx = c0 + i
                        e3 = e4[:, 4 * side + i]
                        a3 = ac.t[:, 4 * side + i, :].rearrange("p (j l) -> p j l", j=nseq)
                        if carry_only:
                            continue
                        S.op("dve", lambda e, e3=e3, a3=a3, cidx=cidx: e.tensor_scalar(out=a3, in0=e3[:, :, 0:L], scalar1=cpc(CP_CW0 + cidx), scalar2=cpc(CP_CB + cidx),
                                                                                   op0=ALU.mult, op1=ALU.add), r=[ex, cp], w=[ac])
                        S.op("dve", lambda e, e3=e3, a3=a3, cidx=cidx: e.scalar_tensor_tensor(out=a3, in0=e3[:, :, 1:L + 1], scalar=cpc(CP_CW1 + cidx), in1=a3,
                                                                                          op0=ALU.mult, op1=ALU.add), r=[ex, cp, ac], w=[ac])
                        S.op("dve", lambda e, e3=e3, a3=a3, cidx=cidx: e.scalar_tensor_tensor(out=a3, in0=e3[:, :, 2:L + 2], scalar=cpc(CP_CW2 + cidx), in1=a3,
                                                                                          op0=ALU.mult, op1=ALU.add), r=[ex, cp, ac], w=[ac])
                if carry_only:
                    return
                S.op("act", lambda e, ac=ac: e.activation(out=ac.t[:, 0:4, :], in_=ac.t[:, 0:4, :], func=AF.Gelu_apprx_tanh), r=[ac], w=[ac])
                S.op("dve", lambda e, ac=ac: e.tensor_tensor(out=actT.t[:, 4 * q:4 * q + 4, :], in0=ac.t[:, 0:4, :], in1=ac.t[:, 4:8, :], op=ALU.mult), r=[ac], w=[actQ[q]])

            def ffn_down(q):
                sd_, vd = next_slab()
                for i in range(4):
                    c = 4 * q + i
                    for g in range(4):
                        S.op("pe", lambda e, i=i, g=g, c=c, vd=vd: e.matmul(banks[g].t[:], lhsT=actT.t[:, c, :], rhs=vd[:, i, g * 512:(g + 1) * 512], start=(c == 0), stop=(c == NFC - 1)),
                             r=[actQ[q], sd_], w=[banks[g]])

            for q in range(12):
                if q < 11:
                    ffn_up(q)
                if q >= 1 and not carry_only:
                    ffn_down(q - 1)
            if not carry_only:
                post_norm_residual(R_GFFNP)
            if sample:
                S.dma("pool", o_ys[:, :], xt.t[:], r=[xt], is_out=True)
                S.dma("pool", o_sconv[:, :], scar.t[:].rearrange("p a b c -> p (a b c)"), r=[scar], is_out=True)
            elif ti == 7:
                S.op("dve", lambda e: e.tensor_scalar(out=ccar.t[:].rearrange("p a b -> p (a b)"), in0=ccar.t[:].rearrange("p a b -> p (a b)"),
                                                      scalar1=cfl.t[:, 0:1], scalar2=None, op0=ALU.mult), r=[ccar, cfl], w=[ccar])
            elif ti >= 8:
                S.dma("pool", o_yp[(ti - 8) * 128:(ti - 7) * 128, :], xt.t[:], r=[xt], is_out=True)
                if last_prompt:
                    S.dma("pool", o_pconv[:, :], ccar.t[:].rearrange("p a b -> p (a b)"), r=[ccar], is_out=True)
            S.barrier()

        def post_norm_residual(gofs):
            S.dma("pool", gpost.t[:], rep[:, gofs:gofs + 2048], w=[gpost])
            for g in range(4):
                S.op("act", lambda e, g=g: e.activation(out=xn.t[:, g * 512:(g + 1) * 512], in_=banks[g].t[:], func=AF.Square, accum_out=ss4.t[:, g:g + 1]),
                     r=[banks[g]], w=[xn, ss4])
            S.op("dve", lambda e: e.reduce_sum(out=st6.t[:, 0:1], in_=ss4.t[:], axis=AX.X), r=[ss4], w=[st6])
            S.op("act", lambda e: e.activation(out=st6.t[:, 1:2], in_=st6.t[:, 0:1], func=AF.Sqrt, scale=1.0 / D, bias=1e-6), r=[st6], w=[st6])
            S.op("dve", lambda e: e.reciprocal(out=rstd.t[:], in_=st6.t[:, 1:2]), r=[st6], w=[rstd])
            for g in range(4):
                S.op("dve", lambda e, g=g: e.scalar_tensor_tensor(out=xn.t[:, g * 512:(g + 1) * 512], in0=banks[g].t[:], scalar=rstd.t[:, 0:1],
                                                                  in1=gpost.t[:, g * 512:(g + 1) * 512], op0=ALU.mult, op1=ALU.mult), r=[banks[g], rstd, gpost], w=[xn])
            S.op("dve", lambda e: e.tensor_tensor(out=xt.t[:], in0=xt.t[:], in1=xn.t[:], op=ALU.add), r=[xt, xn], w=[xt])

        for ti, kd in enumerate(kinds):
          try:
            do_tile(16 - len([k for k in kinds if k != "sample"]) + ti if kd != "sample" else 16, kd)
          except _Stop:
            break
        S.finish()
    return nc


_NC = None
_PREP_ONLY = False


def _consts():
    c = np.zeros((128, NCONST), np.float32)
    i = np.arange(128)
    P, Fd = i[:, None], i[None, :]
    c[:, K_ID:K_ID + 128] = (P == Fd)
    c[:, K_SL:K_SL + 128] = (Fd < P)
    c[:, K_SU:K_SU + 128] = (P < Fd)
    c[:, K_UI:K_UI + 128] = (P <= Fd)
    same = (P // 8 == Fd // 8)
    c[:, K_BSL:K_BSL + 128] = (Fd < P) & same
    c[:, K_BSU:K_BSU + 128] = (P < Fd) & same
    c[:, K_BUI:K_BUI + 128] = (P <= Fd) & same
    c[:, K_B64:K_B64 + 128] = (P // 64 == Fd // 64)
    c[:, K_ONES:K_ONES + 128] = 1.0
    c[:, K_RM16:K_RM16 + 16] = (P // 8 == np.arange(16)[None, :])
    c[:, K_I64:K_I64 + 64] = ((P % 64) == np.arange(64)[None, :])
    col = np.arange(256)
    c[:, K_RSTP:K_RSTP + 256] = (col % 128 != 0)[None, :]
    c[:, K_RSTS:K_RSTS + 256] = (col % 8 != 0)[None, :]
    return c


def _colmajor(v, nchunk):
    out = np.zeros(nchunk * 128, np.float32)
    out[:v.size] = v.reshape(-1)
    return out.reshape(nchunk, 128).T


def kernel(x_prompt, x_sample, mem_prompt, cache_mem_k, cache_mem_v, state_shift, state_wkv, state_conv,
           norm_mix_pre, norm_mix_post, norm_ffn_pre, norm_ffn_post, norm_mem, w_in, w_out, w_mem_k, w_mem_v,
           gm_ln_g, gm_ln_b, gm_ws, gm_bs, rk_mu, rk_w0, rk_w2, rk_a0, rk_a2, rk_g2, rk_kk, rk_ka, rk_rk,
           rk_lnx_g, rk_lnx_b, ffn_w_up, ffn_conv_w, ffn_conv_b, ffn_w_down):
    global _NC
    f = lambda a: np.ascontiguousarray(np.asarray(a, dtype=np.float32))
    x_prompt, x_sample, mem_prompt = f(x_prompt), f(x_sample), f(mem_prompt)
    cache_mem_k, cache_mem_v = f(cache_mem_k)[0], f(cache_mem_v)[0]
    state_shift, state_wkv, state_conv = f(state_shift)[0], f(state_wkv)[0], f(state_conv)[0]
    cpar = np.zeros((128, NCP), np.float32)
    cpar[:, CP_MU:CP_MU + 26] = _colmajor(f(rk_mu)[0], 26)
    for off, v in ((CP_W0, rk_w0), (CP_A0, rk_a0), (CP_KK, rk_kk), (CP_KA, rk_ka), (CP_RK, rk_rk), (CP_LG, rk_lnx_g), (CP_LB, rk_lnx_b)):
        cpar[:, off:off + 8] = _colmajor(f(v)[0], 8)
    for off, v in ((CP_GMIX, norm_mix_pre), (CP_GFFN, norm_ffn_pre), (CP_GMEM, norm_mem)):
        cpar[:, off:off + 16] = _colmajor(f(v)[0], 16)
    cw = f(ffn_conv_w)[0]
    for i, off in enumerate((CP_CW0, CP_CW1, CP_CW2)):
        cpar[:, off:off + 88] = _colmajor(cw[i], 88)
    cpar[:, CP_CB:CP_CB + 88] = _colmajor(f(ffn_conv_b)[0], 88)
    rep = np.zeros((128, NREP), np.float32)
    rep[:, 0:2048] = f(norm_mix_post)[0][None, :]
    rep[:, 2048:4096] = f(norm_ffn_post)[0][None, :]
    rep[:, 4096:4608] = f(gm_ln_g)[0][None, :]
    rep[:, 4608:5120] = f(gm_ln_b)[0][None, :]
    lora = np.zeros((128, 2048), np.float32)
    lora[0:64, 0:1024] = f(rk_w2)[0]
    lora[64:128, 0:1024] = f(rk_a2)[0]
    lora[0:64, 1024:2048] = f(rk_g2)[0]
    ws = f(gm_ws)[0]
    wmTp = np.ascontiguousarray(ws.transpose(2, 0, 1)).reshape(128, 512)
    wmTs = np.zeros((128, 4, 128), np.float32)
    for j in range(16):
        wmTs[8 * j:8 * j + 8, :, 8 * j:8 * j + 8] = ws[:, 0:8, 0:8].transpose(2, 0, 1)
    wmTs = wmTs.reshape(128, 512)
    bs = f(gm_bs)[0]
    bsp = np.ascontiguousarray(bs.T)
    bss = np.ascontiguousarray(np.tile(bs[:, 0:8].T, (16, 1)))
    shared = dict(w_in=f(w_in)[0], w_out=f(w_out)[0], w_mk=f(w_mem_k)[0], w_mv=f(w_mem_v)[0], w_up=f(ffn_w_up)[0], w_dn=f(ffn_w_down)[0],
                  cpar=cpar, cst=_consts(), rep=rep, lora=lora, wmTp=wmTp, wmTs=wmTs, bsp=bsp, bss=bss)
    in_maps = []
    for c in range(8):
        b, half = c // 2, c % 2
        xw = np.zeros((2048, D), np.float32)
        if half == 0:
            xw[1024:] = x_prompt[b, 0:1024]
        else:
            xw[:] = x_prompt[b]
        sq = slice(16 * c, 16 * c + 16)
        shT = np.zeros((26 * 128, 16), np.float32)
        shT[:3264] = state_shift[sq].T
        shT = np.ascontiguousarray(shT.reshape(26, 128, 16).transpose(1, 0, 2)).reshape(128, 26 * 16)
        sw = state_wkv[sq].reshape(16, 8, 2, 64, 64)
        s0T = np.ascontiguousarray(sw.transpose(2, 4, 1, 0, 3)).reshape(128, 8 * 16 * 64)
        cvT = np.ascontiguousarray(state_conv[sq].reshape(16, 2, 88, 128).transpose(3, 2, 0, 1)).reshape(128, 88 * 32)
        ckT = np.ascontiguousarray(cache_mem_k[sq].transpose(0, 3, 2, 1)).reshape(16, 128, 1024)
        cv = np.ascontiguousarray(cache_mem_v[sq]).reshape(16, 256, 512)
        m = dict(shared)
        m.update(cflag=np.full((128, 1), float(half), np.float32), xwin=xw, xs=np.ascontiguousarray(x_sample[sq]).reshape(128, D), mem=mem_prompt[b], ckT=ckT, cv=cv, shT=shT, s0T=s0T, cvT=cvT)
        in_maps.append(m)
    if _PREP_ONLY:
        return in_maps
    if _NC is None:
        _NC = build()
    res = run_bass_kernel_spmd(_NC, in_maps, core_ids=list(range(8)))
    R_ = res.results
    y_p = np.zeros((4, 2048, D), np.float32)
    y_s = np.zeros((128, 8, D), np.float32)
    p_mk = np.zeros((1, 4, 256, 4, 128), np.float32); p_mv = np.zeros_like(p_mk)
    p_cv = np.zeros((1, 4, 128, 4, 128), np.float32)
    p_sh = np.zeros((1, 4, 3264), np.float32)
    p_wkv = np.zeros((1, 4, 16, 64, 64), np.float32)
    p_conv = np.zeros((1, 4, 2, 2 * DFF), np.float32)
    s_cv = np.zeros((1, 128, 8, 4, 128), np.float32)
    s_sh = np.zeros((1, 128, 3264), np.float32)
    s_wkv = np.zeros((1, 128, 16, 64, 64), np.float32)
    s_conv = np.zeros((1, 128, 2, 2 * DFF), np.float32)

    def wkv_back(a, nj):
        a = a.reshape(2, 64, 8, nj, 64)
        return a.transpose(3, 2, 0, 4, 1).reshape(nj, 16, 64, 64)

    for c in range(8):
        b, half = c // 2, c % 2
        r = R_[c]
        y_p[b, half * 1024:(half + 1) * 1024] = r["o_yp"]
        sq = slice(16 * c, 16 * c + 16)
        y_s[sq] = r["o_ys"].reshape(16, 8, D)
        if half == 0:
            p_mk[0, b] = r["o_mk"].reshape(256, 4, 128)
            p_mv[0, b] = r["o_mv"].reshape(256, 4, 128)
        else:
            p_cv[0, b] = r["o_pcv"].reshape(128, 4, 128)
            p_sh[0, b] = r["o_psh"][0]
            p_wkv[0, b] = wkv_back(r["o_pwkv"], 1)[0]
            p_conv[0, b] = r["o_pconv"].reshape(128, 88, 2).transpose(2, 1, 0).reshape(2, 2 * DFF)
        s_cv[0, sq] = r["o_scv"].reshape(16, 8, 4, 128)
        s_sh[0, sq] = r["o_ssh"]
        s_wkv[0, sq] = wkv_back(r["o_swkv"], 16)
        s_conv[0, sq] = r["o_sconv"].reshape(128, 88, 16, 2).transpose(2, 3, 1, 0).reshape(16, 2, 2 * DFF)
    return (y_p, y_s, p_mk, p_mv, p_cv, p_sh, p_wkv, p_conv, s_cv, s_sh, s_wkv, s_conv)
```

```python
import contextlib
import numpy as np
import concourse.bass as bass
import concourse.mybir as mybir
from concourse.bass_utils import run_bass_kernel_spmd

F32 = mybir.dt.float32
BF16 = mybir.dt.bfloat16
AF = mybir.ActivationFunctionType
ALU = mybir.AluOpType
AX = mybir.AxisListType

D = 2048
DFF = 5632
NFC = 44
INC = 4800
C0 = 0.6065306597126334
CG = [(0, 512), (512, 512), (1024, 512), (1536, 512), (2048, 512), (2560, 512), (3072, 512), (3584, 512),
      (4096, 192), (4288, 512)]
CP_MU, CP_W0, CP_A0, CP_KK, CP_KA, CP_RK, CP_LG, CP_LB = 0, 26, 34, 42, 50, 58, 66, 74
CP_GMIX, CP_GFFN, CP_GMEM = 82, 98, 114
CP_CW0, CP_CW1, CP_CW2, CP_CB = 130, 218, 306, 394
NCP = 482
K_ID, K_SL, K_SU, K_UI, K_BSL, K_BSU, K_BUI, K_B64, K_ONES = [i * 128 for i in range(9)]
K_RM16 = 9 * 128
K_I64 = K_RM16 + 16
K_RSTP = K_I64 + 64
K_RSTS = K_RSTP + 256
NCONST = K_RSTS + 256
R_GMIXP, R_GFFNP, R_LNG, R_LNB = 0, 2048, 0, 512
NREP = 5120


class Buf:
    __slots__ = ("name", "w", "rs")

    def __init__(self, name):
        self.name = name
        self.w = None
        self.rs = []


class TB:
    def __init__(self, t, name):
        self.t = t
        self.b = Buf(name)


class Sched:
    EPOCH = 3500
    ND = 24

    def __init__(self, nc, es):
        self.nc = nc
        self.es = es
        self.eng = {"pe": nc.tensor, "act": nc.scalar, "dve": nc.vector, "pool": nc.gpsimd, "sp": nc.sync}
        self.cnt = {e: 0 for e in self.eng}
        self.sems = {e: [] for e in self.eng}
        self.seen = {e: {} for e in self.eng}
        self.dsem = [es.enter_context(nc.semaphore(f"dma{i}")) for i in range(self.ND)]
        self.dval = [0] * self.ND
        self.NDP = 6
        self.dnext = {True: 0, False: self.NDP}
        self.out_tokens = []
        self.defer = None

    def _sem(self, e, ep):
        while len(self.sems[e]) <= ep:
            self.sems[e].append(self.es.enter_context(self.nc.semaphore(f"s_{e}_{len(self.sems[e])}")))
        return self.sems[e][ep]

    def _wait(self, e, sem, val):
        key = id(sem)
        if self.seen[e].get(key, 0) >= val:
            return
        self.eng[e].wait_ge(sem, val)
        self.seen[e][key] = val

    def _deps(self, e, r, w):
        toks = []
        for b in r:
            if b.w is not None:
                toks.append(b.w)
        for b in w:
            if b.w is not None:
                toks.append(b.w)
            toks.extend(b.rs)
        mx = {}
        for (sem, val, src) in toks:
            if src == e and (e == "pe" or not SAME_ENGINE_SYNC):
                continue
            k = id(sem)
            if k not in mx or mx[k][1] < val:
                mx[k] = (sem, val)
        for (sem, val) in mx.values():
            self._wait(e, sem, val)

    def _mark(self, tok, r, w):
        for b in r:
            b.rs.append(tok)
            if len(b.rs) > 64:
                b.rs = b.rs[-64:] if False else b.rs
        for b in w:
            b.w = tok
            b.rs = []

    def mark(self):
        if self.defer is not None:
            self.defer.append(("mark", (), {}))

    def run_unit(self, lst):
        while lst:
            kind, args, kw = lst.pop(0)
            if kind == "mark":
                return
            (self.op if kind == "op" else self.dma)(*args, **kw)

    def run_deferred(self, lst, n):
        for _ in range(min(n, len(lst))):
            kind, args, kw = lst.pop(0)
            (self.op if kind == "op" else self.dma)(*args, **kw)

    def op(self, e, fn, r=(), w=()):
        if self.defer is not None:
            self.defer.append(("op", (e, fn), dict(r=list(r), w=list(w))))
            return
        r = [x.b if isinstance(x, TB) else x for x in r]
        w = [x.b if isinstance(x, TB) else x for x in w]
        self._deps(e, r, w)
        ins = fn(self.eng[e])
        c = self.cnt[e]
        ep, v = divmod(c, self.EPOCH)
        sem = self._sem(e, ep)
        ins.then_inc(sem, 1)
        self.cnt[e] = c + 1
        self._mark((sem, v + 1, e), r, w)

    def dma(self, q, out, in_, r=(), w=(), is_out=False):
        if self.defer is not None:
            self.defer.append(("dma", (q, out, in_), dict(r=list(r), w=list(w), is_out=is_out)))
            return
        r = [x.b if isinstance(x, TB) else x for x in r]
        w = [x.b if isinstance(x, TB) else x for x in w]
        self._deps(q, r, w)
        sw = (q == "pool")
        i = self.dnext[sw]
        self.dnext[sw] = (i + 1) % self.NDP if sw else self.NDP + (i + 1 - self.NDP) % (self.ND - self.NDP)
        sem = self.dsem[i]
        if self.dval[i] > 0:
            self._wait(q, sem, self.dval[i])
        self.dval[i] += 16
        self.eng[q].dma_start(out=out, in_=in_).then_inc(sem, 16)
        tok = (sem, self.dval[i], "dma")
        self._mark(tok, r, w)
        if is_out:
            self.out_tokens.append(tok)

    def barrier(self):
        toks = []
        for f in ("pe", "act", "dve", "pool"):
            c = self.cnt[f]
            if c > 0:
                ep, v = divmod(c - 1, self.EPOCH)
                toks.append((self._sem(f, ep), v + 1))
        for i in range(self.ND):
            if self.dval[i] > 0:
                toks.append((self.dsem[i], self.dval[i]))
        for e in ("pe", "act", "dve", "pool", "sp"):
            for (sem, val) in toks:
                self._wait(e, sem, val)

    def finish(self):
        for i in range(self.ND):
            if self.dval[i] > 0:
                self._wait("sp", self.dsem[i], self.dval[i])


KINDS = ["pre"] * 7 + ["full"] * 9 + ["sample"]
DO_MEM = True
STOP_AFTER_MIX = False
STOP_AT = 0
SAME_ENGINE_SYNC = True


class _Stop(Exception):
    pass


def ck(n):
    if STOP_AT == n:
        raise _Stop()


def build():
    nc = bass.Bass("TRN2", target_bir_lowering=False)

    def din(name, shape):
        return nc.dram_tensor(name, list(shape), F32, kind="ExternalInput").ap()

    def dout(name, shape):
        return nc.dram_tensor(name, list(shape), F32, kind="ExternalOutput").ap()

    xwin = din("xwin", [2048, D]); xsm = din("xs", [128, D]); memx = din("mem", [256, D])
    ckT = din("ckT", [16, 128, 1024]); cvv = din("cv", [16, 256, 512])
    shT = din("shT", [128, 26 * 16]); s0T = din("s0T", [128, 8 * 16 * 64]); cvT = din("cvT", [128, 88 * 32])
    w_in = din("w_in", [D, INC]); w_out = din("w_out", [D, D])
    w_mk = din("w_mk", [D, 512]); w_mv = din("w_mv", [D, 512])
    w_up = din("w_up", [D, 2 * DFF]); w_dn = din("w_dn", [DFF, D])
    cpar = din("cpar", [128, NCP]); cst = din("cst", [128, NCONST]); rep = din("rep", [128, NREP])
    lora = din("lora", [128, 2048]); wmTp = din("wmTp", [128, 512]); wmTs = din("wmTs", [128, 512])
    bsp = din("bsp", [128, 4]); bss = din("bss", [128, 4]); cflag = din("cflag", [128, 1])

    def dbf(name, shape):
        return nc.dram_tensor(name, list(shape), BF16, kind="Internal").ap()

    wb = {"w_in": dbf("wb_in", [10, 128, 8192]), "w_out": dbf("wb_out", [4, 128, 8192]), "w_mk": dbf("wb_mk", [1, 128, 8192]), "w_mv": dbf("wb_mv", [1, 128, 8192]),
          "w_up": dbf("wb_up", [22, 128, 8192]), "w_dn": dbf("wb_dn", [11, 128, 8192])}

    def slab_view(key):
        name, k0, k1, c0, c1 = key
        if name == "w_in":
            idx = [g for g, (cs, n_) in enumerate(CG) if cs == c0][0]
        elif name == "w_dn":
            idx = k0 // 4
        else:
            idx = c0 // 512
        a_, b_ = k1 - k0, c1 - c0
        return wb[name][idx][:, 0:a_ * b_].rearrange("p (a b) -> p a b", a=a_)

    wf = {"w_in": w_in, "w_out": w_out, "w_mk": w_mk, "w_mv": w_mv, "w_up": w_up, "w_dn": w_dn}

    o_yp = dout("o_yp", [1024, D]); o_ys = dout("o_ys", [128, D])
    o_mk = dout("o_mk", [256, 512]); o_mv = dout("o_mv", [256, 512])
    o_pcv = dout("o_pcv", [128, 512]); o_psh = dout("o_psh", [1, 3264])
    o_pwkv = dout("o_pwkv", [128, 512]); o_pconv = dout("o_pconv", [128, 88 * 2])
    o_scv = dout("o_scv", [128, 512]); o_ssh = dout("o_ssh", [16, 3264])
    o_swkv = dout("o_swkv", [128, 8 * 16 * 64]); o_sconv = dout("o_sconv", [128, 88 * 32])

    es = contextlib.ExitStack()
    with es:
        S = Sched(nc, es)

        def sb(name, shape, dt=F32):
            return TB(es.enter_context(nc.sbuf_tensor(name, list(shape), dt)), name)

        banks = [TB(es.enter_context(nc.psum_tensor(f"pb{i}", [128, 512], F32)), f"pb{i}") for i in range(8)]
        rr = [0]

        def pbank():
            b = banks[4 + rr[0] % 4]
            rr[0] += 1
            return b

        cp = sb("cp", [128, NCP]); K = sb("K", [128, NCONST]); RP = sb("RP", [128, 1024])
        lo = sb("lo", [128, 2048]); wmp = sb("wmp", [128, 512]); wms = sb("wms", [128, 512])
        bsP = sb("bsP", [128, 4]); bsS = sb("bsS", [128, 4]); omka = sb("omka", [128, 8]); cfl = sb("cfl", [128, 1])
        S.dma("pool", cfl.t[:], cflag[:, :], w=[cfl])
        for (t_, d_) in ((cp, cpar), (K, cst), (RP, rep[:, 4096:5120]), (lo, lora), (wmp, wmTp), (wms, wmTs), (bsP, bsp), (bsS, bss)):
            S.dma("pool", t_.t[:], d_ if t_ is RP else d_[:, :], w=[t_])
        ident = K.t[:, K_ID:K_ID + 128]
        S.op("dve", lambda e: e.tensor_scalar(out=omka.t[:], in0=cp.t[:, CP_KA:CP_KA + 8], scalar1=-1.0, scalar2=1.0,
                                              op0=ALU.mult, op1=ALU.add), r=[cp], w=[omka])
        for (wm_, mo) in ((wmp, K_UI), (wms, K_BUI)):
            m4 = K.t[:, mo:mo + 128].unsqueeze(1).broadcast_to([128, 4, 128])
            v4 = wm_.t[:].rearrange("p (h t) -> p h t", h=4)
            S.op("dve", lambda e, v4=v4, m4=m4: e.tensor_tensor(out=v4, in0=v4, in1=m4, op=ALU.mult), r=[K, wm_], w=[wm_])

        NSL = 3
        slabs = [sb(f"slab{i}", [128, 8192], BF16) for i in range(NSL)]
        plan = []
        issued = [0]
        used = [0]

        def plan_tile(kind, ti=-1):
            if kind == "mem":
                return
            groups = range(4, 9) if kind == "pre" else range(10)
            for g in groups:
                c0, n = CG[g]
                plan.append(("w_in", 0, 16, c0, c0 + n))
            if kind == "pre":
                return
            for g in range(4):
                plan.append(("w_out", 0, 16, g * 512, (g + 1) * 512))
            for q in range(12):
                if q < 11:
                    plan.append(("w_up", 0, 16, q * 512, (q + 1) * 512))
                    plan.append(("w_up", 0, 16, DFF + q * 512, DFF + (q + 1) * 512))
                if q >= 1 and not (kind == "full" and ti == 7):
                    plan.append(("w_dn", 4 * (q - 1), 4 * (q - 1) + 4, 0, D))

        conv_buf = {}

        conv_pos = [0]

        def ensure_conv(upto):
            while conv_pos[0] < min(upto, len(plan)):
                key = plan[conv_pos[0]]
                conv_pos[0] += 1
                if key in conv_buf:
                    continue
                name, k0, k1, c0, c1 = key
                cb = Buf("cv_" + name + str(key[1:]))
                conv_buf[key] = cb
                src = wf[name].rearrange("(k p) c -> p k c", p=128)[:, k0:k1, c0:c1]
                dst = slab_view(key)
                S.dma("pool", dst, src, w=[cb])

        def next_slab():
            i = used[0]
            while issued[0] < len(plan) and issued[0] < i + NSL - 1:
                j = issued[0]
                ensure_conv(j + 4)
                name, k0, k1, c0, c1 = plan[j]
                src = slab_view(plan[j])
                a, b_ = k1 - k0, c1 - c0
                sl = slabs[j % NSL]
                dst = sl.t[:, 0:a * b_].rearrange("p (a b) -> p a b", a=a)
                S.dma("sp", dst, src, r=[conv_buf[plan[j]]], w=[sl])
                issued[0] += 1
            used[0] += 1
            sl = slabs[i % NSL]
            name, k0, k1, c0, c1 = plan[i]
            a, b_ = k1 - k0, c1 - c0
            return sl, sl.t[:, 0:a * b_].rearrange("p (a b) -> p a b", a=a)

        xt = sb("xt", [128, D]); hT = sb("hT", [128, 16, 128], BF16)
        proj = sb("proj", [128, 3776]); mixT = sb("mixT", [128, 16, 128], BF16)
        xn = TB(proj.t[:, 0:2048], "xn"); xn.b = proj.b
        st6 = sb("st6", [128, 8]); rstd = sb("rstd", [128, 1]); ss4 = sb("ss4", [128, 4])
        lj = sb("lj", [128, 256])
        kTp = sb("kTp", [128, 4, 256], BF16); Vp = sb("Vp", [128, 2, 512], BF16); onesb = sb("onesb", [128, 128], BF16)
        carry = sb("carry", [128, 26]); shTs = sb("shTs", [128, 26, 16])
        ST = sb("ST", [128, 8, 64]); ccar = sb("ccar", [128, 88, 2]); STb = sb("STb", [128, 8, 64], BF16)
        arena = es.enter_context(nc.sbuf_tensor("arena", [128, 22272], F32))
        apos = [0]

        def carve(name, shape, dt=F32):
            n = 1
            for d_ in shape[1:]:
                n *= d_
            words = n if dt == F32 else (n + 1) // 2
            ap = arena[:, apos[0]:apos[0] + words]
            apos[0] += words
            assert apos[0] <= 22272, (name, apos[0])
            if dt != F32:
                ap = ap.bitcast(dt)
            if len(shape) == 3:
                ap = ap.rearrange("p (a b) -> p a b", a=shape[1])
            elif len(shape) == 4:
                ap = ap.rearrange("p (a b c) -> p a b c", a=shape[1], b=shape[2])
            return TB(ap, name)

        class V:
            pass

        qT = carve("qT", [128, 4, 128], BF16); PT = carve("PT", [128, 8, 128], BF16); rden = carve("rden", [128, 512])
        au = carve("au", [128, 512]); av = carve("av", [128, 512]); aout = carve("aout", [128, 512])
        Rsets = [[carve(f"R{s_}_{i}", [128, 2, 128]) for i in range(14)] for s_ in range(2)]
        pTn = carve("pTn", [128, 2, 128]); prv = carve("prv", [128, 2, 128])
        psr2 = [carve(f"psr{i}", [128, 2, 128]) for i in range(2)]; psk2 = [carve(f"psk{i}", [128, 2, 128]) for i in range(2)]
        psv2 = [carve(f"psv{i}", [128, 2, 128]) for i in range(2)]; psl = carve("psl", [128, 2, 128])
        S0c = carve("S0c", [128, 16, 64]); S1c = carve("S1c", [128, 16, 64])
        Xa = [carve(f"Xa{i}", [128, 4, 128], BF16) for i in range(2)]
        Xb = [carve(f"Xb{i}", [128, 4, 128], BF16) for i in range(2)]
        Wk = carve("Wk", [128, 4, 128], BF16); Zz = [carve(f"Zz{i}", [128, 4, 128], BF16) for i in range(2)]
        ZFb = carve("ZFb", [128, 4, 128], BF16); idb = carve("idb", [128, 128], BF16)
        AakT = carve("AakT", [128, 4, 128], BF16); ArbT = carve("ArbT", [128, 4, 128], BF16); ArkT = carve("ArkT", [128, 4, 128], BF16)
        HT = carve("HT", [128, 128], BF16); QTI = carve("QTI", [128, 16, 64], BF16)
        Bm = carve("Bm", [128, 16, 64], BF16); Km = carve("Km", [128, 16, 64], BF16)
        VB = carve("VB", [128, 512], BF16); KA = carve("KA", [128, 512], BF16)
        atb = carve("atb", [128, 2, 128], BF16); btb = carve("btb", [128, 2, 128], BF16)
        ktb = carve("ktb", [128, 2, 128], BF16); rtb = carve("rtb", [128, 2, 128], BF16)
        S0cb = carve("S0cb", [128, 16, 64], BF16)
        mix_top = apos[0]
        print("arena words used (mix)", mix_top)
        apos[0] = 0
        gpost = carve("gpost", [128, D])
        actT = carve("actT", [128, NFC, 128], BF16)
        actQ = [Buf(f"actT{q}") for q in range(11)]
        scar = carve("scar", [128, 88, 16, 2])
        ext8 = [carve(f"ext8_{i}", [128, 8, 160]) for i in range(2)]
        acc8 = [carve(f"acc8_{i}", [128, 8, 128]) for i in range(2)]

        arena_tb = TB(arena, "arena")
        S.op("dve", lambda e: e.memset(arena[:, :], 0.0), w=[arena_tb])
        S.barrier()
        S.op("dve", lambda e: e.memset(ST.t[:], 0.0), w=[ST])
        S.op("dve", lambda e: e.memset(STb.t[:], 0.0), w=[STb])
        S.op("dve", lambda e: e.tensor_copy(out=idb.t[:], in_=K.t[:, K_ID:K_ID + 128]), r=[K], w=[idb])
        S.op("dve", lambda e: e.tensor_copy(out=onesb.t[:], in_=K.t[:, K_ONES:K_ONES + 128]), r=[K], w=[onesb])
        S.op("dve", lambda e: e.memset(carry.t[:], 0.0), w=[carry])
        S.op("dve", lambda e: e.memset(ccar.t[:], 0.0), w=[ccar])
        S.dma("pool", shTs.t[:].rearrange("p a b -> p (a b)"), shT[:, :], w=[shTs])

        def cpc(off, n=1):
            return cp.t[:, off:off + n]

        def rmsnorm_to_hT(src, gofs):
            S.op("act", lambda e: e.activation(out=xn.t[:], in_=src.t[:], func=AF.Square, accum_out=st6.t[:, 0:1]),
                 r=[src], w=[xn, st6])
            S.op("act", lambda e: e.activation(out=st6.t[:, 1:2], in_=st6.t[:, 0:1], func=AF.Sqrt, scale=1.0 / D, bias=1e-6),
                 r=[st6], w=[st6])
            S.op("dve", lambda e: e.reciprocal(out=rstd.t[:], in_=st6.t[:, 1:2]), r=[st6], w=[rstd])
            S.op("dve", lambda e: e.tensor_scalar(out=xn.t[:], in0=src.t[:], scalar1=rstd.t[:, 0:1], scalar2=None, op0=ALU.mult),
                 r=[src, rstd], w=[xn])
            for q in range(4):
                pb = pbank()
                for i in range(4):
                    kc = 4 * q + i
                    S.op("pe", lambda e, kc=kc, i=i, pb=pb: e.transpose(pb.t[:, i * 128:(i + 1) * 128], xn.t[:, kc * 128:(kc + 1) * 128], ident),
                         r=[xn, K], w=[pb])
                g4 = cp.t[:, gofs + 4 * q:gofs + 4 * q + 4].unsqueeze(2).broadcast_to([128, 4, 128])
                S.op("dve", lambda e, q=q, pb=pb, g4=g4: e.tensor_tensor(out=hT.t[:, 4 * q:4 * q + 4, :], in0=pb.t[:].rearrange("p (a b) -> p a b", a=4),
                                                                  in1=g4, op=ALU.mult), r=[pb, cp], w=[hT])

        def project(slv, ncols, pb):
            sl, v = slv
            for kc in range(16):
                S.op("pe", lambda e, kc=kc: e.matmul(pb.t[:, 0:ncols], lhsT=hT.t[:, kc, :], rhs=v[:, kc, :], start=(kc == 0), stop=(kc == 15)),
                     r=[hT, sl], w=[pb])

        mem_plan = [("w_mk", 0, 16, 0, 512), ("w_mv", 0, 16, 0, 512)]
        for mt in range(2):
            plan.extend(mem_plan)
        kinds = list(KINDS)
        npt_ = len([k for k in kinds if k != "sample"])
        for i_, kd in enumerate(kinds):
            plan_tile(kd, 16 - npt_ + i_ if kd != "sample" else 16)

        for mt in range(2):
            S.dma("act", xt.t[:], memx[mt * 128:(mt + 1) * 128, :], w=[xt])
            rmsnorm_to_hT(xt, CP_GMEM)
            for which in range(2):
                slv = next_slab()
                pb = pbank()
                project(slv, 512, pb)
                S.op("act", lambda e, pb=pb: e.copy(out=rden.t[:], in_=pb.t[:]), r=[pb], w=[rden])
                S.dma("pool", (o_mk if which == 0 else o_mv)[mt * 128:(mt + 1) * 128, :], rden.t[:], r=[rden], is_out=True)
                if which == 0:
                    pb2 = pbank()
                    for h in range(4):
                        S.op("pe", lambda e, h=h, pb2=pb2: e.transpose(pb2.t[:, h * 128:(h + 1) * 128], rden.t[:, h * 128:(h + 1) * 128], ident),
                             r=[rden, K], w=[pb2])
                    S.op("dve", lambda e, pb2=pb2, mt=mt: e.tensor_copy(out=kTp.t[:, :, mt * 128:(mt + 1) * 128],
                                                                  in_=pb2.t[:].rearrange("p (a b) -> p a b", a=4)), r=[pb2], w=[kTp])
                else:
                    S.op("dve", lambda e, mt=mt: e.tensor_copy(out=Vp.t[:, mt, :], in_=rden.t[:]), r=[rden], w=[Vp])

        def f8(tb):
            return tb.t[:].rearrange("p (a b) -> p a b", a=8)

        def do_tile(ti, kind):
            sample = kind == "sample"
            full = kind != "pre"
            nseq = 16 if sample else 1
            L = 128 // nseq
            mSL, mSU, mUI = (K_BSL, K_BSU, K_BUI) if sample else (K_SL, K_SU, K_UI)
            nlev = 3 if sample else 7
            src = xsm[:, :] if sample else xwin[ti * 128:(ti + 1) * 128, :]
            S.dma("act", xt.t[:], src, w=[xt])
            rmsnorm_to_hT(xt, CP_GMIX)
            groups = range(4, 9) if kind == "pre" else range(10)
            for g in groups:
                c0, n = CG[g]
                slv = next_slab()
                pb = pbank()
                project(slv, n, pb)
                if g == 0:
                    S.op("act", lambda e, pb=pb: e.activation(out=au.t[:], in_=pb.t[:], func=AF.Gelu_apprx_tanh), r=[pb], w=[au])
                elif g == 1:
                    S.op("act", lambda e, pb=pb: e.activation(out=av.t[:], in_=pb.t[:], func=AF.Gelu_apprx_tanh), r=[pb], w=[av])
                else:
                    S.op("act", lambda e, pb=pb, c0=c0, n=n: e.copy(out=proj.t[:, c0 - 1024:c0 - 1024 + n], in_=pb.t[:, 0:n]), r=[pb], w=[proj])
            ck(1)
            if kind == "pre":
                ensure_conv(conv_pos[0] + 12)
            last_prompt = (kind == "full" and ti == 15)
            if last_prompt:
                S.dma("pool", o_psh[:, :], proj.t[127:128, 0:3264], r=[proj], is_out=True)
            if sample:
                S.dma("pool", o_ssh[:, :], proj.t[7:128:8, 0:3264], r=[proj], is_out=True)

            ac_list = []
            if full:
                if not sample:
                    S.defer = ac_list
                S.op("dve", lambda e: e.bn_stats(out=st6.t[:, 0:6], in_=av.t[:]), r=[av], w=[st6])
                S.op("dve", lambda e: e.bn_aggr(out=st6.t[:, 6:8], in_=st6.t[:, 0:6]), r=[st6], w=[st6])
                S.op("act", lambda e: e.activation(out=st6.t[:, 0:1], in_=st6.t[:, 7:8], func=AF.Sqrt, scale=1.0, bias=1e-5), r=[st6], w=[st6])
                S.op("dve", lambda e: e.reciprocal(out=rstd.t[:], in_=st6.t[:, 0:1]), r=[st6], w=[rstd])
                S.op("dve", lambda e: e.tensor_scalar(out=av.t[:], in0=av.t[:], scalar1=st6.t[:, 6:7], scalar2=rstd.t[:, 0:1],
                                                      op0=ALU.subtract, op1=ALU.mult), r=[av, st6, rstd], w=[av])
                S.op("dve", lambda e: e.tensor_tensor(out=av.t[:], in0=av.t[:], in1=RP.t[:, R_LNG:R_LNG + 512], op=ALU.mult), r=[av, RP], w=[av])
                S.op("dve", lambda e: e.tensor_tensor(out=av.t[:], in0=av.t[:], in1=RP.t[:, R_LNB:R_LNB + 512], op=ALU.add), r=[av, RP], w=[av])
                if last_prompt:
                    S.dma("pool", o_pcv[:, :], av.t[:], r=[av], is_out=True)
                if sample:
                    S.dma("pool", o_scv[:, :], av.t[:], r=[av], is_out=True)
                S.mark()
                wm_ = wms if sample else wmp
                bs_ = bsS if sample else bsP
                pb = pbank()
                for h in range(4):
                    S.op("pe", lambda e, h=h, pb=pb: e.matmul(pb.t[:, h * 128:(h + 1) * 128], lhsT=wm_.t[:, h * 128:(h + 1) * 128],
                                                              rhs=av.t[:, h * 128:(h + 1) * 128], start=True, stop=True), r=[wm_, av], w=[pb])
                for h in range(4):
                    S.op("dve", lambda e, h=h, pb=pb: e.scalar_tensor_tensor(out=aout.t[:, h * 128:(h + 1) * 128], in0=pb.t[:, h * 128:(h + 1) * 128],
                                                                             scalar=bs_.t[:, h:h + 1], in1=au.t[:, h * 128:(h + 1) * 128],
                                                                             op0=ALU.add, op1=ALU.mult), r=[pb, bs_, au], w=[aout])
                S.mark()
                pb = pbank()
                for h in range(4):
                    S.op("pe", lambda e, h=h, pb=pb: e.transpose(pb.t[:, h * 128:(h + 1) * 128], aout.t[:, h * 128:(h + 1) * 128], ident), r=[aout, K], w=[pb])
                S.op("act", lambda e, pb=pb: e.copy(out=mixT.t[:, 0:4, :], in_=pb.t[:].rearrange("p (a b) -> p a b", a=4)), r=[pb], w=[mixT])

                S.mark()
                pb = pbank()
                for h in range(4):
                    S.op("pe", lambda e, h=h, pb=pb: e.transpose(pb.t[:, h * 128:(h + 1) * 128], proj.t[:, 3264 + h * 128:3264 + (h + 1) * 128], ident),
                         r=[proj, K], w=[pb])
                S.op("act", lambda e, pb=pb: e.copy(out=qT.t[:], in_=pb.t[:].rearrange("p (a b) -> p a b", a=4)), r=[pb], w=[qT])
                S.mark()
                pbs = [pbank(), pbank()]
                pnum = banks[0] if sample else None
                pden = banks[1] if sample else None
                if not sample:
                    for h in range(4):
                        for mc in range(2):
                            i = h * 2 + mc
                            S.op("pe", lambda e, h=h, mc=mc, i=i: e.matmul(pbs[i // 4].t[:, (i % 4) * 128:(i % 4 + 1) * 128], lhsT=kTp.t[:, h, mc * 128:(mc + 1) * 128],
                                                                          rhs=qT.t[:, h, :], start=True, stop=True), r=[kTp, qT], w=[pbs[i // 4]])
                else:
                    for j in range(16):
                        kj, vj = kTp, Vp
                        S.dma("pool", kj.t[:].rearrange("p a b -> p (a b)"), ckT[j], w=[kj])
                        S.dma("pool", vj.t[:], cvv[j].rearrange("(mc p) c -> p mc c", p=128), w=[vj])
                        for h in range(4):
                            for mc in range(2):
                                i = h * 2 + mc
                                S.op("pe", lambda e, h=h, mc=mc, i=i, j=j, kj=kj: e.matmul(
                                    pbs[i // 4].t[:, (i % 4) * 128 + 8 * j:(i % 4) * 128 + 8 * j + 8], lhsT=kj.t[:, h, mc * 128:(mc + 1) * 128],
                                    rhs=qT.t[:, h, 8 * j:8 * j + 8], start=True, stop=True), r=[kj, qT], w=[pbs[i // 4]])
                        if j == 0:
                            pass
                        for half in range(2):
                            S.op("act", lambda e, half=half, j=j: e.activation(
                                out=PT.t[:, 4 * half:4 * half + 4, 8 * j:8 * j + 8],
                                in_=pbs[half].t[:].rearrange("p (a b) -> p a b", a=4)[:, :, 8 * j:8 * j + 8], func=AF.Exp, scale=128.0 ** -0.5),
                                r=[pbs[half]], w=[PT])
                        for h in range(4):
                            for mc in range(2):
                                S.op("pe", lambda e, h=h, mc=mc, j=j, vj=vj: e.matmul(
                                    pnum.t[:, h * 128 + 8 * j:h * 128 + 8 * j + 8], lhsT=vj.t[:, mc, h * 128:(h + 1) * 128],
                                    rhs=PT.t[:, 2 * h + mc, 8 * j:8 * j + 8], start=(mc == 0), stop=(mc == 1)), r=[vj, PT], w=[pnum])
                if not sample:
                    for half in range(2):
                        S.op("act", lambda e, half=half: e.activation(out=PT.t[:, 4 * half:4 * half + 4, :],
                                                                      in_=pbs[half].t[:].rearrange("p (a b) -> p a b", a=4), func=AF.Exp, scale=128.0 ** -0.5),
                             r=[pbs[half]], w=[PT])
                    S.mark()
                    pnum, pden = pbank(), pbank()
                    for h in range(4):
                        for mc in range(2):
                            S.op("pe", lambda e, h=h, mc=mc: e.matmul(pnum.t[:, h * 128:(h + 1) * 128], lhsT=Vp.t[:, mc, h * 128:(h + 1) * 128],
                                                                      rhs=PT.t[:, 2 * h + mc, :], start=(mc == 0), stop=(mc == 1)), r=[Vp, PT], w=[pnum])
                for h in range(4):
                    for mc in range(2):
                        S.op("pe", lambda e, h=h, mc=mc: e.matmul(pden.t[:, h * 128:(h + 1) * 128], lhsT=onesb.t[:],
                                                                  rhs=PT.t[:, 2 * h + mc, :], start=(mc == 0), stop=(mc == 1)), r=[onesb, PT], w=[pden])
                S.op("dve", lambda e: e.reciprocal(out=rden.t[:], in_=pden.t[:]), r=[pden], w=[rden])
                S.op("dve", lambda e: e.tensor_tensor(out=mixT.t[:, 12:16, :], in0=pnum.t[:].rearrange("p (a b) -> p a b", a=4),
                                                      in1=rden.t[:].rearrange("p (a b) -> p a b", a=4), op=ALU.mult), r=[pnum, rden], w=[mixT])
                S.mark()
                S.defer = None

            Vt = TB(VB.t[:, 0:256], "Vt"); Vt.b = VB.b
            Bt = TB(VB.t[:, 256:512], "Bt"); Bt.b = VB.b
            Kt = TB(KA.t[:, 0:256], "Kt"); Kt.b = KA.b
            At = TB(KA.t[:, 256:512], "At"); At.b = KA.b

            def shift(dst, c0, n):
                pb = pbank()
                for i in range(n):
                    cc = c0 + i
                    ncol = 64 if cc == 25 else 128
                    S.op("pe", lambda e, i=i, cc=cc, ncol=ncol, pb=pb: e.transpose(pb.t[0:ncol, i * 128:(i + 1) * 128], proj.t[:, cc * 128:cc * 128 + ncol], ident),
                         r=[proj, K], w=[pb])
                if c0 + n - 1 == 25:
                    S.op("act", lambda e, pb=pb: e.copy(out=pTn.t[:, 0, :], in_=pb.t[:, 0:128]), r=[pb], w=[pTn])
                    S.op("act", lambda e, pb=pb: e.copy(out=pTn.t[0:64, 1, :], in_=pb.t[0:64, 128:256]), r=[pb], w=[pTn])
                else:
                    S.op("act", lambda e, pb=pb: e.copy(out=pTn.t[:, 0:n, :], in_=pb.t[:, 0:n * 128].rearrange("p (a b) -> p a b", a=n)), r=[pb], w=[pTn])
                S.op("pool", lambda e: e.tensor_copy(out=prv.t[:, 0:n, 1:128], in_=pTn.t[:, 0:n, 0:127]), r=[pTn], w=[prv])
                if sample:
                    S.op("pool", lambda e: e.tensor_copy(out=prv.t[:, 0:n, 0:128:8], in_=shTs.t[:, c0:c0 + n, :]), r=[shTs], w=[prv])
                else:
                    S.op("pool", lambda e: e.tensor_copy(out=prv.t[:, 0:n, 0:1], in_=carry.t[:, c0:c0 + n].unsqueeze(2)), r=[carry], w=[prv])
                    S.op("pool", lambda e: e.tensor_copy(out=carry.t[:, c0:c0 + n].unsqueeze(2), in_=pTn.t[:, 0:n, 127:128]), r=[pTn], w=[carry])
                S.op("pool", lambda e: e.tensor_tensor(out=prv.t[:, 0:n, :], in0=prv.t[:, 0:n, :], in1=pTn.t[:, 0:n, :], op=ALU.subtract), r=[prv, pTn], w=[prv])
                mu3 = cp.t[:, CP_MU + c0:CP_MU + c0 + n].unsqueeze(2).broadcast_to([128, n, 128])
                S.op("pool", lambda e: e.tensor_tensor(out=prv.t[:, 0:n, :], in0=prv.t[:, 0:n, :], in1=mu3, op=ALU.mult), r=[prv, cp], w=[prv])
                S.op("pool", lambda e: e.tensor_tensor(out=dst.t[:, 0:n, :], in0=prv.t[:, 0:n, :], in1=pTn.t[:, 0:n, :], op=ALU.add), r=[prv, pTn], w=[dst])

            shift(psl, 24, 2 if full else 1)
            S.op("act", lambda e: e.activation(out=lj.t[0:64, 0:128], in_=psl.t[0:64, 0, :], func=AF.Tanh), r=[psl], w=[lj])
            if full:
                S.op("act", lambda e: e.activation(out=lj.t[0:64, 128:256], in_=psl.t[0:64, 1, :], func=AF.Sigmoid), r=[psl], w=[lj])
            ck(2)
            YT = [banks[0], banks[1]]
            PS = [banks[2], banks[3]]
            id4 = idb.t[:].unsqueeze(1).broadcast_to([128, 4, 128])

            def mk4(off):
                return K.t[:, off:off + 128].unsqueeze(1).broadcast_to([128, 4, 128])

            def b4(pb):
                return pb.t[:].rearrange("p (a b) -> p a b", a=4)

            def fl(tb):
                return tb.t[:].rearrange("p a b -> p (a b)")

            def do_shifts(hg_):
                if full:
                    shift(psr2[hg_ % 2], 2 * hg_, 2)
                shift(psk2[hg_ % 2], 8 + 2 * hg_, 2)
                shift(psv2[hg_ % 2], 16 + 2 * hg_, 2)

            do_shifts(0)
            def E_stage(hg_):
                g0 = 2 * hg_
                (sgw, cum, gam, igam, gprev, a_, kkn, kmod, rt, kt, bt, at, gT, bonus) = Rsets[hg_ % 2]
                psr, psk, psv = psr2[hg_ % 2], psk2[hg_ % 2], psv2[hg_ % 2]

                def bc2(off):
                    return cp.t[:, off + g0:off + g0 + 2].unsqueeze(2).broadcast_to([128, 2, 128])

                pw = TB(banks[0].t[:, 0:256], "pw"); pw.b = banks[0].b
                pa = TB(banks[1].t[:, 0:256], "pa"); pa.b = banks[1].b
                for cc in range(2):
                    S.op("pe", lambda e, cc=cc: e.matmul(pw.t[:, cc * 128:(cc + 1) * 128], lhsT=lo.t[0:64, (g0 + cc) * 128:(g0 + cc + 1) * 128],
                                                         rhs=lj.t[0:64, 0:128], start=True, stop=True), r=[lo, lj], w=[pw])
                for cc in range(2):
                    S.op("pe", lambda e, cc=cc: e.matmul(pa.t[:, cc * 128:(cc + 1) * 128], lhsT=lo.t[64:128, (g0 + cc) * 128:(g0 + cc + 1) * 128],
                                                         rhs=psl.t[64:128, 0, :], start=True, stop=True), r=[lo, psl], w=[pa])
                for cc in range(2):
                    S.op("act", lambda e, cc=cc: e.activation(out=sgw.t[:, cc, :], in_=pw.t[:, cc * 128:(cc + 1) * 128], func=AF.Sigmoid,
                                                              bias=cp.t[:, CP_W0 + g0 + cc:CP_W0 + g0 + cc + 1], scale=1.0), r=[pw, cp], w=[sgw])
                    S.op("act", lambda e, cc=cc: e.activation(out=a_.t[:, cc, :], in_=pa.t[:, cc * 128:(cc + 1) * 128], func=AF.Sigmoid,
                                                              bias=cp.t[:, CP_A0 + g0 + cc:CP_A0 + g0 + cc + 1], scale=1.0), r=[pa, cp], w=[a_])
                if full:
                    pg = TB(banks[0].t[:, 256:512], "pg"); pg.b = banks[0].b
                    for cc in range(2):
                        S.op("pe", lambda e, cc=cc: e.matmul(pg.t[:, cc * 128:(cc + 1) * 128], lhsT=lo.t[0:64, 1024 + (g0 + cc) * 128:1024 + (g0 + cc + 1) * 128],
                                                             rhs=lj.t[0:64, 128:256], start=True, stop=True), r=[lo, lj], w=[pg])
                    S.op("act", lambda e: e.copy(out=fl(gT), in_=pg.t[:, 0:256]), r=[pg], w=[gT])
                ko = K_RSTS if sample else K_RSTP
                S.op("dve", lambda e: e.tensor_tensor_scan(out=fl(cum), data0=K.t[:, ko:ko + 256], data1=fl(sgw), initial=0.0, op0=ALU.mult, op1=ALU.add), r=[K, sgw], w=[cum])
                S.op("act", lambda e: e.activation(out=fl(gam), in_=fl(cum), func=AF.Exp, scale=-C0), r=[cum], w=[gam])
                S.op("act", lambda e: e.activation(out=fl(igam), in_=fl(cum), func=AF.Exp, scale=C0), r=[cum], w=[igam])
                S.op("dve", lambda e: e.tensor_tensor(out=fl(gprev), in0=fl(cum), in1=fl(sgw), op=ALU.subtract), r=[cum, sgw], w=[gprev])
                S.op("act", lambda e: e.activation(out=fl(gprev), in_=fl(gprev), func=AF.Exp, scale=-C0), r=[gprev], w=[gprev])
                S.op("dve", lambda e: e.tensor_tensor(out=kkn.t[:], in0=psk.t[:], in1=bc2(CP_KK), op=ALU.mult), r=[psk, cp], w=[kkn])
                S.op("dve", lambda e: e.tensor_tensor(out=kt.t[:], in0=kkn.t[:], in1=kkn.t[:], op=ALU.mult), r=[kkn], w=[kt])
                pk = TB(banks[2].t[:, 0:256], "pk"); pk.b = banks[2].b
                for cc in range(2):
                    S.op("pe", lambda e, cc=cc: e.matmul(pk.t[:, cc * 128:(cc + 1) * 128], lhsT=K.t[:, K_B64:K_B64 + 128], rhs=kt.t[:, cc, :], start=True, stop=True), r=[K, kt], w=[pk])
                S.op("act", lambda e: e.activation(out=fl(bt), in_=pk.t[:, 0:256], func=AF.Sqrt, scale=1.0, bias=1e-30), r=[pk], w=[bt])
                S.op("dve", lambda e: e.tensor_scalar(out=bt.t[:], in0=bt.t[:], scalar1=1e-12, scalar2=None, op0=ALU.max), r=[bt], w=[bt])
                S.op("dve", lambda e: e.reciprocal(out=bt.t[:], in_=bt.t[:]), r=[bt], w=[bt])
                S.op("dve", lambda e: e.tensor_tensor(out=kkn.t[:], in0=kkn.t[:], in1=bt.t[:], op=ALU.mult), r=[kkn, bt], w=[kkn])
                S.op("dve", lambda e: e.tensor_tensor(out=kmod.t[:], in0=a_.t[:], in1=bc2(CP_KA), op=ALU.mult), r=[a_, cp], w=[kmod])
                S.op("dve", lambda e: e.tensor_tensor(out=kmod.t[:], in0=kmod.t[:], in1=omka.t[:, g0:g0 + 2].unsqueeze(2).broadcast_to([128, 2, 128]), op=ALU.add), r=[kmod, omka], w=[kmod])
                S.op("dve", lambda e: e.tensor_tensor(out=kmod.t[:], in0=kmod.t[:], in1=psk.t[:], op=ALU.mult), r=[kmod, psk], w=[kmod])
                S.op("dve", lambda e: e.scalar_tensor_tensor(out=fl(at), in0=fl(kkn), scalar=-1.0, in1=fl(gprev), op0=ALU.mult, op1=ALU.mult), r=[kkn, gprev], w=[at])
                S.op("dve", lambda e: e.tensor_tensor(out=bt.t[:], in0=kkn.t[:], in1=a_.t[:], op=ALU.mult), r=[kkn, a_], w=[bt])
                S.op("dve", lambda e: e.tensor_tensor(out=bt.t[:], in0=bt.t[:], in1=igam.t[:], op=ALU.mult), r=[bt, igam], w=[bt])
                S.op("dve", lambda e: e.tensor_tensor(out=kt.t[:], in0=kmod.t[:], in1=igam.t[:], op=ALU.mult), r=[kmod, igam], w=[kt])
                if full:
                    S.op("dve", lambda e: e.tensor_tensor(out=rt.t[:], in0=psr.t[:], in1=gam.t[:], op=ALU.mult), r=[psr, gam], w=[rt])
                    S.op("dve", lambda e: e.tensor_tensor(out=bonus.t[:], in0=psr.t[:], in1=kmod.t[:], op=ALU.mult), r=[psr, kmod], w=[bonus])
                    S.op("dve", lambda e: e.tensor_tensor(out=bonus.t[:], in0=bonus.t[:], in1=bc2(CP_RK), op=ALU.mult), r=[bonus, cp], w=[bonus])
                    pbn = TB(banks[2].t[:, 256:512], "pbn"); pbn.b = banks[2].b
                    for cc in range(2):
                        S.op("pe", lambda e, cc=cc: e.matmul(pbn.t[:, cc * 128:(cc + 1) * 128], lhsT=K.t[:, K_B64:K_B64 + 128], rhs=bonus.t[:, cc, :], start=True, stop=True), r=[K, bonus], w=[pbn])
                    S.op("dve", lambda e: e.tensor_tensor(out=fl(bonus), in0=pbn.t[:, 0:256], in1=fl(psv), op=ALU.mult), r=[pbn, psv], w=[bonus])

            E_stage(0)
            for hg in range(4):
                g0 = 2 * hg
                (sgw, cum, gam, igam, gprev, a_, kkn, kmod, rt, kt, bt, at, gT, bonus) = Rsets[hg % 2]
                psr, psk, psv = psr2[hg % 2], psk2[hg % 2], psv2[hg % 2]

                def bc2(off):
                    return cp.t[:, off + g0:off + g0 + 2].unsqueeze(2).broadcast_to([128, 2, 128])

                S.op("act", lambda e: e.copy(out=atb.t[:], in_=at.t[:]), r=[at], w=[atb])
                S.op("act", lambda e: e.copy(out=btb.t[:], in_=bt.t[:]), r=[bt], w=[btb])
                S.op("act", lambda e: e.copy(out=ktb.t[:], in_=kt.t[:]), r=[kt], w=[ktb])
                if full:
                    S.op("act", lambda e: e.copy(out=rtb.t[:], in_=rt.t[:]), r=[rt], w=[rtb])
                ck(3)
                pb = pbank()
                for cc in range(2):
                    S.op("pe", lambda e, cc=cc, pb=pb: e.transpose(pb.t[:, cc * 128:(cc + 1) * 128], psv.t[:, cc, :], ident), r=[psv, K], w=[pb])
                    S.op("pe", lambda e, cc=cc, pb=pb: e.transpose(pb.t[:, 256 + cc * 128:256 + (cc + 1) * 128], bt.t[:, cc, :], ident), r=[bt, K], w=[pb])
                S.op("act", lambda e, pb=pb: e.copy(out=VB.t[:], in_=pb.t[:]), r=[pb], w=[VB])
                pb = pbank()
                for cc in range(2):
                    S.op("pe", lambda e, cc=cc, pb=pb: e.transpose(pb.t[:, cc * 128:(cc + 1) * 128], kt.t[:, cc, :], ident), r=[kt, K], w=[pb])
                    S.op("pe", lambda e, cc=cc, pb=pb: e.transpose(pb.t[:, 256 + cc * 128:256 + (cc + 1) * 128], at.t[:, cc, :], ident), r=[at, K], w=[pb])
                S.op("act", lambda e, pb=pb: e.copy(out=KA.t[:], in_=pb.t[:]), r=[pb], w=[KA])

                ck(4)

                def fm(buf, i):
                    return buf.t[(i % 2) * 64:(i % 2) * 64 + 64, i // 2, :]

                def xmat(lb, rb, moff, dst):
                    pe_, po_ = pbank(), pbank()
                    for i in range(4):
                        pb = pe_ if i % 2 == 0 else po_
                        sl_ = slice((i // 2) * 128, (i // 2 + 1) * 128)
                        S.op("pe", lambda e, i=i, sl_=sl_, pb=pb: e.matmul(pb.t[:, sl_], lhsT=fm(lb, i), rhs=fm(rb, i), start=True, stop=True), r=[lb, rb], w=[pb])
                    m2 = K.t[:, moff:moff + 128].unsqueeze(1).broadcast_to([128, 2, 128])
                    for par, pb in ((0, pe_), (1, po_)):
                        S.op("dve", lambda e, par=par, pb=pb: e.tensor_tensor(out=dst.t[:, par::2, :], in0=pb.t[:, 0:256].rearrange("p (a b) -> p a b", a=2), in1=m2, op=ALU.mult),
                             r=[pb, K], w=[dst])

                xmat(atb, btb, mSL, Xa[0])
                xmat(btb, atb, mSU, Xb[0])
                xmat(ktb, atb, mSU, AakT)
                if full:
                    xmat(btb, rtb, mUI, ArbT)
                    xmat(ktb, rtb, mUI, ArkT)
                ck(5)
                pz = pbank()
                for i in range(4):
                    S.op("pe", lambda e, i=i, pz=pz: e.matmul(pz.t[:, i * 128:i * 128 + 64], lhsT=AakT.t[:, i, :], rhs=Vt.t[:, i * 64:(i + 1) * 64], start=True, stop=True),
                         r=[AakT, Vt], w=[pz])
                S.op("act", lambda e, pz=pz: e.copy(out=Zz[0].t[:, :, 0:64], in_=b4(pz)[:, :, 0:64]), r=[pz], w=[Zz[0]])
                S.op("dve", lambda e: e.tensor_copy(out=Zz[0].t[:, :, 64:128], in_=At.t[:, :].rearrange("p (a b) -> p a b", a=4)), r=[At], w=[Zz[0]])
                lst = []
                if hg + 1 < 4:
                    do_shifts(hg + 1)
                    S.defer = lst
                    E_stage(hg + 1)
                    S.defer = None
                per = (len(lst) + nlev - 1) // nlev
                cur = 0
                for lev in range(nlev):
                    pz = pbank()
                    for i in range(4):
                        S.op("pe", lambda e, i=i, lev=lev, pz=pz, cur=cur: e.matmul(pz.t[:, i * 128:(i + 1) * 128], lhsT=Xb[cur].t[:, i, :], rhs=Zz[lev % 2].t[:, i, :], start=True, stop=False),
                             r=[Xb[cur], Zz[lev % 2]], w=[pz])
                        S.op("pe", lambda e, i=i, lev=lev, pz=pz: e.matmul(pz.t[:, i * 128:(i + 1) * 128], lhsT=idb.t[:], rhs=Zz[lev % 2].t[:, i, :], start=False, stop=True),
                             r=[idb, Zz[lev % 2]], w=[pz])
                    zdst = ZFb if lev == nlev - 1 else Zz[(lev + 1) % 2]
                    S.op("act", lambda e, zdst=zdst, pz=pz: e.copy(out=zdst.t[:], in_=b4(pz)), r=[pz], w=[zdst])
                    if lev < nlev - 1:
                        px, pxt = pbank(), pbank()
                        for i in range(4):
                            S.op("pe", lambda e, i=i, cur=cur, px=px: e.matmul(px.t[:, i * 128:(i + 1) * 128], lhsT=Xb[cur].t[:, i, :], rhs=Xa[cur].t[:, i, :], start=True, stop=True),
                                 r=[Xa[cur], Xb[cur]], w=[px])
                            S.op("pe", lambda e, i=i, cur=cur, pxt=pxt: e.matmul(pxt.t[:, i * 128:(i + 1) * 128], lhsT=Xa[cur].t[:, i, :], rhs=Xb[cur].t[:, i, :], start=True, stop=True),
                                 r=[Xa[cur], Xb[cur]], w=[pxt])
                        S.op("dve", lambda e, cur=cur, px=px: e.tensor_copy(out=Xa[1 - cur].t[:], in_=b4(px)), r=[px], w=[Xa[1 - cur]])
                        S.op("act", lambda e, cur=cur, pxt=pxt: e.copy(out=Xb[1 - cur].t[:], in_=b4(pxt)), r=[pxt], w=[Xb[1 - cur]])
                        cur = 1 - cur
                    S.run_deferred(lst, per)
                    if ac_list and lev in (1, 4):
                        S.run_unit(ac_list)
                S.run_deferred(lst, len(lst))
                if hg == 3:
                    while ac_list:
                        S.run_unit(ac_list)
                ck(6)
                ZF = ZFb
                for i in range(4):
                    h = 4 * hg + i
                    hp, hc, lc = i % 2, h // 2, i // 2
                    prt = slice(hp * 64, hp * 64 + 64)
                    U0 = ZF.t[:, i, 0:64]
                    G = ZF.t[:, i, 64:128]
                    Bh = Bt.t[:, i * 64:(i + 1) * 64]
                    Kh = Kt.t[:, i * 64:(i + 1) * 64]
                    Vh = Vt.t[:, i * 64:(i + 1) * 64]
                    if sample and hp == 0:
                        S.dma("pool", S0c.t[:].rearrange("p a b -> p (a b)"), s0T[:, hc * 1024:(hc + 1) * 1024], w=[S0c])
                        S.op("act", lambda e: e.copy(out=S0cb.t[:], in_=S0c.t[:]), r=[S0c], w=[S0cb])
                    if full:
                        ph = pbank()
                        S.op("pe", lambda e, i=i, G=G, ph=ph, prt=prt: e.matmul(ph.t[prt, 0:128], lhsT=G, rhs=ArbT.t[:, i, :], start=True, stop=True), r=[ZF, ArbT], w=[ph])
                        S.op("dve", lambda e, i=i, ph=ph, prt=prt: e.tensor_tensor(out=HT.t[prt, :], in0=ph.t[prt, 0:128], in1=fm(rt, i), op=ALU.add), r=[ph, rt], w=[HT])
                        ybk = YT[hp]
                        yo = ybk.t[prt, lc * 128:(lc + 1) * 128]
                        S.op("pe", lambda e, i=i, U0=U0, yo=yo: e.matmul(yo, lhsT=U0, rhs=ArbT.t[:, i, :], start=True, stop=False), r=[ZF, ArbT], w=[ybk])
                        S.op("pe", lambda e, i=i, Vh=Vh, yo=yo: e.matmul(yo, lhsT=Vh, rhs=ArkT.t[:, i, :], start=False, stop=False), r=[Vt, ArkT], w=[ybk])
                        if not sample:
                            S.op("pe", lambda e, yo=yo, prt=prt, hc=hc: e.matmul(yo, lhsT=STb.t[prt, hc, :], rhs=HT.t[prt, :], start=False, stop=True), r=[STb, HT], w=[ybk])
                        else:
                            for j in range(16):
                                S.op("pe", lambda e, j=j, prt=prt, hc=hc, ybk=ybk: e.matmul(
                                    ybk.t[prt, lc * 128 + 8 * j:lc * 128 + 8 * j + 8], lhsT=S0cb.t[prt, j, :],
                                    rhs=HT.t[prt, 8 * j:8 * j + 8], start=False, stop=(j == 15)), r=[S0cb, HT], w=[ybk])
                    if not sample:
                        pq = pbank()
                        S.op("pe", lambda e, G=G, Bh=Bh, pq=pq, prt=prt: e.matmul(pq.t[prt, 0:64], lhsT=G, rhs=Bh, start=True, stop=True), r=[ZF, Bt], w=[pq])
                        S.op("act", lambda e, pq=pq, prt=prt: e.copy(out=QTI.t[prt, 0, :], in_=pq.t[prt, 0:64]), r=[pq], w=[QTI])
                        pbk = PS[hp]
                        po = pbk.t[prt, lc * 64:(lc + 1) * 64]
                        S.op("pe", lambda e, po=po, Bh=Bh, U0=U0: e.matmul(po, lhsT=Bh, rhs=U0, start=True, stop=False), r=[Bt, ZF], w=[pbk])
                        S.op("pe", lambda e, po=po, Kh=Kh, Vh=Vh: e.matmul(po, lhsT=Kh, rhs=Vh, start=False, stop=False), r=[Kt, Vt], w=[pbk])
                        S.op("pe", lambda e, po=po, prt=prt, hc=hc: e.matmul(po, lhsT=QTI.t[prt, 0, :], rhs=STb.t[prt, hc, :], start=False, stop=True), r=[QTI, STb], w=[pbk])
                    else:
                        rm = K.t[:, K_RM16:K_RM16 + 16].unsqueeze(2).broadcast_to([128, 16, 64])
                        S.op("dve", lambda e, Bh=Bh, rm=rm: e.tensor_tensor(out=Bm.t[:], in0=Bh.unsqueeze(1).broadcast_to([128, 16, 64]), in1=rm, op=ALU.mult), r=[Bt, K], w=[Bm])
                        S.op("dve", lambda e, Kh=Kh, rm=rm: e.tensor_tensor(out=Km.t[:], in0=Kh.unsqueeze(1).broadcast_to([128, 16, 64]), in1=rm, op=ALU.mult), r=[Kt, K], w=[Km])
                        for j8 in range(2):
                            pq = pbank()
                            for jj in range(8):
                                j = j8 * 8 + jj
                                S.op("pe", lambda e, G=G, j=j, jj=jj, pq=pq, prt=prt: e.matmul(pq.t[prt, jj * 64:(jj + 1) * 64], lhsT=G, rhs=Bm.t[:, j, :], start=True, stop=True), r=[ZF, Bm], w=[pq])
                            S.op("act", lambda e, pq=pq, prt=prt, j8=j8: e.copy(out=QTI.t[prt, j8 * 8:(j8 + 1) * 8, :], in_=pq.t[prt, :].rearrange("p (a b) -> p a b", a=8)),
                                 r=[pq], w=[QTI])
                        for j8 in range(2):
                            po_b = PS[hp]
                            for jj in range(8):
                                j = j8 * 8 + jj
                                po = po_b.t[prt, jj * 64:(jj + 1) * 64]
                                S.op("pe", lambda e, po=po, j=j, U0=U0: e.matmul(po, lhsT=Bm.t[:, j, :], rhs=U0, start=True, stop=False), r=[Bm, ZF], w=[po_b])
                                S.op("pe", lambda e, po=po, j=j, Vh=Vh: e.matmul(po, lhsT=Km.t[:, j, :], rhs=Vh, start=False, stop=False), r=[Km, Vt], w=[po_b])
                                S.op("pe", lambda e, po=po, j=j, prt=prt: e.matmul(po, lhsT=QTI.t[prt, j, :], rhs=S0cb.t[prt, j, :], start=False, stop=True), r=[QTI, S0cb], w=[po_b])
                            gl = gam.t[prt, lc, 7 + 64 * j8:64 * j8 + 64:8].unsqueeze(2).broadcast_to([64, 8, 64])
                            S.op("dve", lambda e, po_b=po_b, prt=prt, j8=j8: e.tensor_tensor(
                                out=S1c.t[prt, j8 * 8:(j8 + 1) * 8, :], in0=po_b.t[prt, :].rearrange("p (a b) -> p a b", a=8), in1=S0c.t[prt, j8 * 8:(j8 + 1) * 8, :], op=ALU.add),
                                 r=[po_b, S0c], w=[S1c])
                            S.op("dve", lambda e, prt=prt, j8=j8, gl=gl: e.tensor_tensor(
                                out=S1c.t[prt, j8 * 8:(j8 + 1) * 8, :], in0=S1c.t[prt, j8 * 8:(j8 + 1) * 8, :], in1=gl, op=ALU.mult), r=[S1c, gam], w=[S1c])
                        if hp == 1:
                            S.dma("pool", o_swkv[:, hc * 1024:(hc + 1) * 1024], S1c.t[:].rearrange("p a b -> p (a b)"), r=[S1c], is_out=True)
                ck(7)
                if not sample:
                    for hp_ in range(2):
                        pr_ = slice(hp_ * 64, hp_ * 64 + 64)
                        gl = gam.t[pr_, :, 127:128].broadcast_to([64, 2, 64])
                        S.op("dve", lambda e, hp_=hp_, pr_=pr_: e.tensor_tensor(out=ST.t[pr_, g0:g0 + 2, :], in0=PS[hp_].t[pr_, 0:128].rearrange("p (a b) -> p a b", a=2),
                                                                                in1=ST.t[pr_, g0:g0 + 2, :], op=ALU.add), r=[PS[hp_], ST], w=[ST])
                        S.op("dve", lambda e, gl=gl, pr_=pr_: e.tensor_tensor(out=ST.t[pr_, g0:g0 + 2, :], in0=ST.t[pr_, g0:g0 + 2, :], in1=gl, op=ALU.mult), r=[ST, gam], w=[ST])
                        S.op("act", lambda e, pr_=pr_: e.copy(out=STb.t[pr_, g0:g0 + 2, :], in_=ST.t[pr_, g0:g0 + 2, :]), r=[ST], w=[STb])
                if not full:
                    continue
                yT, cen, sq, rs_ = sgw, cum, igam, gprev
                for hp_ in range(2):
                    S.op("act", lambda e, hp_=hp_: e.copy(out=fl(yT)[hp_ * 64:hp_ * 64 + 64, :], in_=YT[hp_].t[hp_ * 64:hp_ * 64 + 64, 0:256]), r=[YT[hp_]], w=[yT])
                pm = pbank()
                for cc in range(2):
                    S.op("pe", lambda e, cc=cc, pm=pm: e.matmul(pm.t[:, cc * 128:(cc + 1) * 128], lhsT=K.t[:, K_B64:K_B64 + 128], rhs=yT.t[:, cc, :], start=True, stop=True), r=[K, yT], w=[pm])
                S.op("dve", lambda e, pm=pm: e.scalar_tensor_tensor(out=fl(cen), in0=pm.t[:, 0:256], scalar=-1.0 / 64, in1=fl(yT), op0=ALU.mult, op1=ALU.add), r=[pm, yT], w=[cen])
                S.op("dve", lambda e: e.tensor_tensor(out=sq.t[:], in0=cen.t[:], in1=cen.t[:], op=ALU.mult), r=[cen], w=[sq])
                pv2 = pbank()
                for cc in range(2):
                    S.op("pe", lambda e, cc=cc, pv2=pv2: e.matmul(pv2.t[:, cc * 128:(cc + 1) * 128], lhsT=K.t[:, K_B64:K_B64 + 128], rhs=sq.t[:, cc, :], start=True, stop=True), r=[K, sq], w=[pv2])
                S.op("act", lambda e, pv2=pv2: e.activation(out=fl(rs_), in_=pv2.t[:, 0:256], func=AF.Sqrt, scale=1.0 / 64, bias=64e-5), r=[pv2], w=[rs_])
                S.op("dve", lambda e: e.reciprocal(out=rs_.t[:], in_=rs_.t[:]), r=[rs_], w=[rs_])
                S.op("dve", lambda e: e.tensor_tensor(out=cen.t[:], in0=cen.t[:], in1=rs_.t[:], op=ALU.mult), r=[cen, rs_], w=[cen])
                S.op("dve", lambda e: e.tensor_tensor(out=cen.t[:], in0=cen.t[:], in1=bc2(CP_LG), op=ALU.mult), r=[cen, cp], w=[cen])
                S.op("dve", lambda e: e.tensor_tensor(out=cen.t[:], in0=cen.t[:], in1=bc2(CP_LB), op=ALU.add), r=[cen, cp], w=[cen])
                S.op("dve", lambda e: e.tensor_tensor(out=cen.t[:], in0=cen.t[:], in1=bonus.t[:], op=ALU.add), r=[cen, bonus], w=[cen])
                S.op("dve", lambda e: e.tensor_tensor(out=mixT.t[:, 4 + g0:4 + g0 + 2, :], in0=cen.t[:], in1=gT.t[:], op=ALU.mult), r=[cen, gT], w=[mixT])
            if last_prompt:
                S.dma("pool", o_pwkv[:, :], ST.t[:].rearrange("p a b -> p (a b)"), r=[ST], is_out=True)
            if not full:
                return
            S.barrier()

            for g in range(4):
                slv = next_slab()
                sl, v = slv
                for kc in range(16):
                    S.op("pe", lambda e, kc=kc, g=g, v=v: e.matmul(banks[g].t[:], lhsT=mixT.t[:, kc, :], rhs=v[:, kc, :], start=(kc == 0), stop=(kc == 15)), r=[mixT, sl], w=[banks[g]])
            post_norm_residual(R_GMIXP)
            if STOP_AFTER_MIX:
                return
            if sample:
                S.dma("pool", scar.t[:].rearrange("p a b c -> p (a b c)"), cvT[:, :], w=[scar])
            rmsnorm_to_hT(xt, CP_GFFN)
            carry_only = (ti == 7 and not sample)

            def ffn_up(q):
                sg_, vg = next_slab()
                sv_, vv = next_slab()
                pg_, pv_ = pbank(), pbank()
                for (pb, sl_, vw) in ((pg_, sg_, vg), (pv_, sv_, vv)):
                    for i in range(4):
                        for kc in range(16):
                            S.op("pe", lambda e, i=i, kc=kc, pb=pb, vw=vw: e.matmul(pb.t[:, i * 128:(i + 1) * 128], lhsT=vw[:, kc, i * 128:(i + 1) * 128], rhs=hT.t[:, kc, :],
                                                                                 start=(kc == 0), stop=(kc == 15)), r=[sl_, hT], w=[pb])
                ex, ac = ext8[q % 2], acc8[q % 2]
                W = nseq * (L + 2)
                e4 = ex.t[:, :, 0:W].rearrange("p c (j l) -> p c j l", j=nseq)
                for side, pb in ((0, pg_), (1, pv_)):
                    c0 = 4 * q + side * NFC
                    es = e4[:, 4 * side:4 * side + 4]
                    p4 = pb.t[:].rearrange("p (c j l) -> p c j l", c=4, j=nseq)
                    if sample:
                        S.op("dve", lambda e, es=es, c0=c0: e.tensor_copy(out=es[:, :, :, 0:2], in_=scar.t[:, c0:c0 + 4, :, :]), r=[scar], w=[ex])
                    else:
                        S.op("dve", lambda e, es=es, c0=c0: e.tensor_copy(out=es[:, :, 0, 0:2], in_=ccar.t[:, c0:c0 + 4, :]), r=[ccar], w=[ex])
                    S.op("act", lambda e, es=es, p4=p4: e.copy(out=es[:, :, :, 2:L + 2], in_=p4), r=[pb], w=[ex])
                    if sample:
                        S.op("dve", lambda e, es=es, c0=c0: e.tensor_copy(out=scar.t[:, c0:c0 + 4, :, :], in_=es[:, :, :, L:L + 2]), r=[ex], w=[scar])
                    else:
                        S.op("dve", lambda e, es=es, c0=c0: e.tensor_copy(out=ccar.t[:, c0:c0 + 4, :], in_=es[:, :, 0, L:L + 2]), r=[ex], w=[ccar])
                    for i in range(4):
                        cidx = c0 + i
                        e3 = e4[:, 4 * side + i]
                        a3 = ac.t[:, 4 * side + i, :].rearrange("p (j l) -> p j l", j=nseq)
                        if carry_only:
                            continue
                        S.op("dve", lambda e, e3=e3, a3=a3, cidx=cidx: e.tensor_scalar(out=a3, in0=e3[:, :, 0:L], scalar1=cpc(CP_CW0 + cidx), scalar2=cpc(CP_CB + cidx),
                                                                                   op0=ALU.mult, op1=ALU.add), r=[ex, cp], w=[ac])
                        S.op("dve", lambda e, e3=e3, a3=a3, cidx=cidx: e.scalar_tensor_tensor(out=a3, in0=e3[:, :, 1:L + 1], scalar=cpc(CP_CW1 + cidx), in1=a3,
                                                                                          op0=ALU.mult, op1=ALU.add), r=[ex, cp, ac], w=[ac])
                        S.op("dve", lambda e, e3=e3, a3=a3, cidx=cidx: e.scalar_tensor_tensor(out=a3, in0=e3[:, :, 2:L + 2], scalar=cpc(CP_CW2 + cidx), in1=a3,
                                                                                          op0=ALU.mult, op1=ALU.add), r=[ex, cp, ac], w=[ac])
                if carry_only:
                    return
                S.op("act", lambda e, ac=ac: e.activation(out=ac.t[:, 0:4, :], in_=ac.t[:, 0:4, :], func=AF.Gelu_apprx_tanh), r=[ac], w=[ac])
                S.op("dve", lambda e, ac=ac: e.tensor_tensor(out=actT.t[:, 4 * q:4 * q + 4, :], in0=ac.t[:, 0:4, :], in1=ac.t[:, 4:8, :], op=ALU.mult), r=[ac], w=[actQ[q]])

            def ffn_down(q):
                sd_, vd = next_slab()
                for i in range(4):
                    c = 4 * q + i
                    for g in range(4):
                        S.op("pe", lambda e, i=i, g=g, c=c, vd=vd: e.matmul(banks[g].t[:], lhsT=actT.t[:, c, :], rhs=vd[:, i, g * 512:(g + 1) * 512], start=(c == 0), stop=(c == NFC - 1)),
                             r=[actQ[q], sd_], w=[banks[g]])

            for q in range(12):
                if q < 11:
                    ffn_up(q)
                if q >= 1 and not carry_only:
                    ffn_down(q - 1)
            if not carry_only:
                post_norm_residual(R_GFFNP)
            if sample:
                S.dma("pool", o_ys[:, :], xt.t[:], r=[xt], is_out=True)
                S.dma("pool", o_sconv[:, :], scar.t[:].rearrange("p a b c -> p (a b c)"), r=[scar], is_out=True)
            elif ti == 7:
                S.op("dve", lambda e: e.tensor_scalar(out=ccar.t[:].rearrange("p a b -> p (a b)"), in0=ccar.t[:].rearrange("p a b -> p (a b)"),
                                                      scalar1=cfl.t[:, 0:1], scalar2=None, op0=ALU.mult), r=[ccar, cfl], w=[ccar])
            elif ti >= 8:
                S.dma("pool", o_yp[(ti - 8) * 128:(ti - 7) * 128, :], xt.t[:], r=[xt], is_out=True)
                if last_prompt:
                    S.dma("pool", o_pconv[:, :], ccar.t[:].rearrange("p a b -> p (a b)"), r=[ccar], is_out=True)
            S.barrier()

        def post_norm_residual(gofs):
            S.dma("pool", gpost.t[:], rep[:, gofs:gofs + 2048], w=[gpost])
            for g in range(4):
                S.op("act", lambda e, g=g: e.activation(out=xn.t[:, g * 512:(g + 1) * 512], in_=banks[g].t[:], func=AF.Square, accum_out=ss4.t[:, g:g + 1]),
                     r=[banks[g]], w=[xn, ss4])
            S.op("dve", lambda e: e.reduce_sum(out=st6.t[:, 0:1], in_=ss4.t[:], axis=AX.X), r=[ss4], w=[st6])
            S.op("act", lambda e: e.activation(out=st6.t[:, 1:2], in_=st6.t[:, 0:1], func=AF.Sqrt, scale=1.0 / D, bias=1e-6), r=[st6], w=[st6])
            S.op("dve", lambda e: e.reciprocal(out=rstd.t[:], in_=st6.t[:, 1:2]), r=[st6], w=[rstd])
            for g in range(4):
                S.op("dve", lambda e, g=g: e.scalar_tensor_tensor(out=xn.t[:, g * 512:(g + 1) * 512], in0=banks[g].t[:], scalar=rstd.t[:, 0:1],
                                                                  in1=gpost.t[:, g * 512:(g + 1) * 512], op0=ALU.mult, op1=ALU.mult), r=[banks[g], rstd, gpost], w=[xn])
            S.op("dve", lambda e: e.tensor_tensor(out=xt.t[:], in0=xt.t[:], in1=xn.t[:], op=ALU.add), r=[xt, xn], w=[xt])

        for ti, kd in enumerate(kinds):
          try:
            do_tile(16 - len([k for k in kinds if k != "sample"]) + ti if kd != "sample" else 16, kd)
          except _Stop:
            break
        S.finish()
    return nc


_NC = None
_PREP_ONLY = False


def _consts():
    c = np.zeros((128, NCONST), np.float32)
    i = np.arange(128)
    P, Fd = i[:, None], i[None, :]
    c[:, K_ID:K_ID + 128] = (P == Fd)
    c[:, K_SL:K_SL + 128] = (Fd < P)
    c[:, K_SU:K_SU + 128] = (P < Fd)
    c[:, K_UI:K_UI + 128] = (P <= Fd)
    same = (P // 8 == Fd // 8)
    c[:, K_BSL:K_BSL + 128] = (Fd < P) & same
    c[:, K_BSU:K_BSU + 128] = (P < Fd) & same
    c[:, K_BUI:K_BUI + 128] = (P <= Fd) & same
    c[:, K_B64:K_B64 + 128] = (P // 64 == Fd // 64)
    c[:, K_ONES:K_ONES + 128] = 1.0
    c[:, K_RM16:K_RM16 + 16] = (P // 8 == np.arange(16)[None, :])
    c[:, K_I64:K_I64 + 64] = ((P % 64) == np.arange(64)[None, :])
    col = np.arange(256)
    c[:, K_RSTP:K_RSTP + 256] = (col % 128 != 0)[None, :]
    c[:, K_RSTS:K_RSTS + 256] = (col % 8 != 0)[None, :]
    return c


def _colmajor(v, nchunk):
    out = np.zeros(nchunk * 128, np.float32)
    out[:v.size] = v.reshape(-1)
    return out.reshape(nchunk, 128).T


def kernel(x_prompt, x_sample, mem_prompt, cache_mem_k, cache_mem_v, state_shift, state_wkv, state_conv,
           norm_mix_pre, norm_mix_post, norm_ffn_pre, norm_ffn_post, norm_mem, w_in, w_out, w_mem_k, w_mem_v,
           gm_ln_g, gm_ln_b, gm_ws, gm_bs, rk_mu, rk_w0, rk_w2, rk_a0, rk_a2, rk_g2, rk_kk, rk_ka, rk_rk,
           rk_lnx_g, rk_lnx_b, ffn_w_up, ffn_conv_w, ffn_conv_b, ffn_w_down):
    global _NC
    f = lambda a: np.ascontiguousarray(np.asarray(a, dtype=np.float32))
    x_prompt, x_sample, mem_prompt = f(x_prompt), f(x_sample), f(mem_prompt)
    cache_mem_k, cache_mem_v = f(cache_mem_k)[0], f(cache_mem_v)[0]
    state_shift, state_wkv, state_conv = f(state_shift)[0], f(state_wkv)[0], f(state_conv)[0]
    cpar = np.zeros((128, NCP), np.float32)
    cpar[:, CP_MU:CP_MU + 26] = _colmajor(f(rk_mu)[0], 26)
    for off, v in ((CP_W0, rk_w0), (CP_A0, rk_a0), (CP_KK, rk_kk), (CP_KA, rk_ka), (CP_RK, rk_rk), (CP_LG, rk_lnx_g), (CP_LB, rk_lnx_b)):
        cpar[:, off:off + 8] = _colmajor(f(v)[0], 8)
    for off, v in ((CP_GMIX, norm_mix_pre), (CP_GFFN, norm_ffn_pre), (CP_GMEM, norm_mem)):
        cpar[:, off:off + 16] = _colmajor(f(v)[0], 16)
    cw = f(ffn_conv_w)[0]
    for i, off in enumerate((CP_CW0, CP_CW1, CP_CW2)):
        cpar[:, off:off + 88] = _colmajor(cw[i], 88)
    cpar[:, CP_CB:CP_CB + 88] = _colmajor(f(ffn_conv_b)[0], 88)
    rep = np.zeros((128, NREP), np.float32)
    rep[:, 0:2048] = f(norm_mix_post)[0][None, :]
    rep[:, 2048:4096] = f(norm_ffn_post)[0][None, :]
    rep[:, 4096:4608] = f(gm_ln_g)[0][None, :]
    rep[:, 4608:5120] = f(gm_ln_b)[0][None, :]
    lora = np.zeros((128, 2048), np.float32)
    lora[0:64, 0:1024] = f(rk_w2)[0]
    lora[64:128, 0:1024] = f(rk_a2)[0]
    lora[0:64, 1024:2048] = f(rk_g2)[0]
    ws = f(gm_ws)[0]
    wmTp = np.ascontiguousarray(ws.transpose(2, 0, 1)).reshape(128, 512)
    wmTs = np.zeros((128, 4, 128), np.float32)
    for j in range(16):
        wmTs[8 * j:8 * j + 8, :, 8 * j:8 * j + 8] = ws[:, 0:8, 0:8].transpose(2, 0, 1)
    wmTs = wmTs.reshape(128, 512)
    bs = f(gm_bs)[0]
    bsp = np.ascontiguousarray(bs.T)
    bss = np.ascontiguousarray(np.tile(bs[:, 0:8].T, (16, 1)))
    shared = dict(w_in=f(w_in)[0], w_out=f(w_out)[0], w_mk=f(w_mem_k)[0], w_mv=f(w_mem_v)[0], w_up=f(ffn_w_up)[0], w_dn=f(ffn_w_down)[0],
                  cpar=cpar, cst=_consts(), rep=rep, lora=lora, wmTp=wmTp, wmTs=wmTs, bsp=bsp, bss=bss)
    in_maps = []
    for c in range(8):
        b, half = c // 2, c % 2
        xw = np.zeros((2048, D), np.float32)
        if half == 0:
            xw[1024:] = x_prompt[b, 0:1024]
        else:
            xw[:] = x_prompt[b]
        sq = slice(16 * c, 16 * c + 16)
        shT = np.zeros((26 * 128, 16), np.float32)
        shT[:3264] = state_shift[sq].T
        shT = np.ascontiguousarray(shT.reshape(26, 128, 16).transpose(1, 0, 2)).reshape(128, 26 * 16)
        sw = state_wkv[sq].reshape(16, 8, 2, 64, 64)
        s0T = np.ascontiguousarray(sw.transpose(2, 4, 1, 0, 3)).reshape(128, 8 * 16 * 64)
        cvT = np.ascontiguousarray(state_conv[sq].reshape(16, 2, 88, 128).transpose(3, 2, 0, 1)).reshape(128, 88 * 32)
        ckT = np.ascontiguousarray(cache_mem_k[sq].transpose(0, 3, 2, 1)).reshape(16, 128, 1024)
        cv = np.ascontiguousarray(cache_mem_v[sq]).reshape(16, 256, 512)
        m = dict(shared)
        m.update(cflag=np.full((128, 1), float(half), np.float32), xwin=xw, xs=np.ascontiguousarray(x_sample[sq]).reshape(128, D), mem=mem_prompt[b], ckT=ckT, cv=cv, shT=shT, s0T=s0T, cvT=cvT)
        in_maps.append(m)
    if _PREP_ONLY:
        return in_maps
    if _NC is None:
        _NC = build()
    res = run_bass_kernel_spmd(_NC, in_maps, core_ids=list(range(8)))
    R_ = res.results
    y_p = np.zeros((4, 2048, D), np.float32)
    y_s = np.zeros((128, 8, D), np.float32)
    p_mk = np.zeros((1, 4, 256, 4, 128), np.float32); p_mv = np.zeros_like(p_mk)
    p_cv = np.zeros((1, 4, 128, 4, 128), np.float32)
    p_sh = np.zeros((1, 4, 3264), np.float32)
    p_wkv = np.zeros((1, 4, 16, 64, 64), np.float32)
    p_conv = np.zeros((1, 4, 2, 2 * DFF), np.float32)
    s_cv = np.zeros((1, 128, 8, 4, 128), np.float32)
    s_sh = np.zeros((1, 128, 3264), np.float32)
    s_wkv = np.zeros((1, 128, 16, 64, 64), np.float32)
    s_conv = np.zeros((1, 128, 2, 2 * DFF), np.float32)

    def wkv_back(a, nj):
        a = a.reshape(2, 64, 8, nj, 64)
        return a.transpose(3, 2, 0, 4, 1).reshape(nj, 16, 64, 64)

    for c in range(8):
        b, half = c // 2, c % 2
        r = R_[c]
        y_p[b, half * 1024:(half + 1) * 1024] = r["o_yp"]
        sq = slice(16 * c, 16 * c + 16)
        y_s[sq] = r["o_ys"].reshape(16, 8, D)
        if half == 0:
            p_mk[0, b] = r["o_mk"].reshape(256, 4, 128)
            p_mv[0, b] = r["o_mv"].reshape(256, 4, 128)
        else:
            p_cv[0, b] = r["o_pcv"].reshape(128, 4, 128)
            p_sh[0, b] = r["o_psh"][0]
            p_wkv[0, b] = wkv_back(r["o_pwkv"], 1)[0]
            p_conv[0, b] = r["o_pconv"].reshape(128, 88, 2).transpose(2, 1, 0).reshape(2, 2 * DFF)
        s_cv[0, sq] = r["o_scv"].reshape(16, 8, 4, 128)
        s_sh[0, sq] = r["o_ssh"]
        s_wkv[0, sq] = wkv_back(r["o_swkv"], 16)
        s_conv[0, sq] = r["o_sconv"].reshape(128, 88, 16, 2).transpose(2, 3, 1, 0).reshape(16, 2, 2 * DFF)
    return (y_p, y_s, p_mk, p_mv, p_cv, p_sh, p_wkv, p_conv, s_cv, s_sh, s_wkv, s_conv)
```
